# Optimizing a Trainium2 kernel written in Bass

```python
import math, functools
import jax, jax.numpy as jnp
from jax import lax
import numpy as np

D_MODEL = 1024
BATCH = 1
SEQ = 16384
DEPTH = 1
DEC_BATCH = 32
DEC_SEQ = 1
PAST_LEN = 16384
PAGE_SIZE = 128

HEAD_DIM = 64
N_NSA_HEADS = 8
N_KV = 2
N_RET_HEADS = 8
RET_DK = 64
RET_DV = 64
D_NSA = N_NSA_HEADS * HEAD_DIM
D_RET = N_RET_HEADS * RET_DV
D_MIX = D_NSA + D_RET
CMP_BLOCK = 32
CMP_STRIDE = 16
CMP_HID = 128
SLC_BLOCK = 64
N_SEL = 16
WINDOW = 512
Q_BLOCK = 128
RET_CHUNK = 128
N_BUCKETS = 32
REL_MAX_DIST = 1024
D_FF = 2816
CONV_W = 3
RMS_EPS = 1e-6
GN_EPS = 1e-5
ROPE_BASE = 10000.0
NEG_INF = -1e30
FORCE_SCORE = 1e4
SPLITS = (D_NSA, 2 * N_KV * HEAD_DIM, 2 * N_KV * HEAD_DIM, 2 * N_KV * HEAD_DIM, 3 * N_NSA_HEADS, N_RET_HEADS * RET_DK, N_RET_HEADS * RET_DK, D_RET, D_RET)
D_IN = D_NSA + 6 * N_KV * HEAD_DIM + 3 * N_NSA_HEADS + 2 * N_RET_HEADS * RET_DK + 2 * D_RET

kernel_name = 'hymba_nsa_retention_convffn_step'


def rmsnorm(x, g):
    xf = x.astype(jnp.float32)
    y = xf * lax.rsqrt(jnp.mean(xf * xf, axis=-1, keepdims=True) + RMS_EPS)
    return (y * g).astype(x.dtype)


def adaln(c, w, b):
    mod = jax.nn.silu(c) @ w + b
    return [m[:, None, :] for m in jnp.split(mod, 6, axis=-1)]


def rel_bucket(dist):
    n = jnp.maximum(dist, 0)
    max_exact = N_BUCKETS // 2
    nf = jnp.maximum(n, 1).astype(jnp.float32)
    large = max_exact + (jnp.log(nf / max_exact) / math.log(REL_MAX_DIST / max_exact) * (N_BUCKETS - max_exact)).astype(jnp.int32)
    return jnp.where(n < max_exact, n, jnp.minimum(large, N_BUCKETS - 1))


def head_bias(dist, table):
    tq, n = dist.shape
    b = jnp.take(table, rel_bucket(dist), axis=0)
    return b.reshape(tq, n, N_KV, N_NSA_HEADS // N_KV).transpose(2, 3, 0, 1)


def masked_softmax(s, mask):
    s = jnp.where(mask, s.astype(jnp.float32), NEG_INF)
    return jax.nn.softmax(s, axis=-1) * mask


def rope(x, pos):
    half = x.shape[-1] // 2
    inv = ROPE_BASE ** (-jnp.arange(half, dtype=jnp.float32) / half)
    ang = pos.astype(jnp.float32)[:, None] * inv[None, :]
    cos, sin = jnp.cos(ang)[None, :, None, :], jnp.sin(ang)[None, :, None, :]
    x1, x2 = x[..., :half].astype(jnp.float32), x[..., half:].astype(jnp.float32)
    return jnp.concatenate([x1 * cos - x2 * sin, x1 * sin + x2 * cos], axis=-1).astype(x.dtype)


def split_projection(hn, w_in):
    B, T = hn.shape[:2]
    cuts = [int(v) for v in np.cumsum(SPLITS)[:-1]]
    q_a, kv_c, kv_s, kv_w, gt, q_r, k_r, v_r, g_r = jnp.split(hn @ w_in, cuts, axis=-1)
    kvs = lambda a: a.reshape(B, T, 2, N_KV, HEAD_DIM)
    gates = jax.nn.sigmoid(gt.astype(jnp.float32)).reshape(B, T, 3, N_NSA_HEADS).astype(hn.dtype)
    return (q_a.reshape(B, T, N_NSA_HEADS, HEAD_DIM), kvs(kv_c), kvs(kv_s), kvs(kv_w), gates,
            q_r.reshape(B, T, N_RET_HEADS, RET_DK), k_r.reshape(B, T, N_RET_HEADS, RET_DK),
            v_r.reshape(B, T, N_RET_HEADS, RET_DV), g_r)


def compress(k, pe, w1, w2):
    B, L, G, dh = k.shape
    kc = k.reshape(B, L // CMP_STRIDE, CMP_STRIDE, G, dh)
    lo = jnp.einsum('bnlgd,ldh->bngh', kc, w1[:CMP_STRIDE])
    hi = jnp.einsum('bnlgd,ldh->bngh', kc, w1[CMP_STRIDE:])
    pe_term = jnp.einsum('ld,ldh->h', pe, w1)
    hid = jax.nn.gelu(lo[:, :-1] + hi[:, 1:] + pe_term)
    return jnp.einsum('bngh,he->bnge', hid, w2)


def cmp_to_slc(imp, ns):
    nc = imp.shape[-1]
    r = SLC_BLOCK // CMP_STRIDE
    pad = [(0, 0)] * (imp.ndim - 1) + [(1, r * ns - nc)]
    pp = jnp.pad(imp, pad)
    return pp[..., :r * ns].reshape(*imp.shape[:-1], ns, r).sum(-1) + pp[..., r::r]


def nsa_attend(q, t_pos, kcmp, vcmp, ks_blk, vs_blk, kw, vw, w_pos, gates, table):
    B, Tq = q.shape[:2]
    G, R = N_KV, N_NSA_HEADS // N_KV
    qg = q.reshape(B, Tq, G, R, HEAD_DIM) * (HEAD_DIM ** -0.5)
    nc = kcmp.shape[1]
    c_end = CMP_STRIDE * jnp.arange(nc) + (CMP_BLOCK - 1)
    d_c = t_pos[:, None] - c_end[None, :]
    s_c = jnp.einsum('btgrd,bngd->bgrtn', qg, kcmp) + head_bias(d_c, table)
    p_c = masked_softmax(s_c, d_c >= 0)
    o_c = jnp.einsum('bgrtn,bngd->btgrd', p_c.astype(vcmp.dtype), vcmp)
    ns = ks_blk.shape[1]
    n_sel = min(N_SEL, ns)
    imp = cmp_to_slc(p_c.sum(axis=2), ns)
    blk = jnp.arange(ns)
    cur = t_pos // SLC_BLOCK
    valid = blk[None, :] * SLC_BLOCK <= t_pos[:, None]
    forced = (blk[None, :] == 0) | (blk[None, :] == cur[:, None]) | (blk[None, :] == cur[:, None] - 1)
    score = jnp.where(valid, jnp.where(forced, FORCE_SCORE, imp), NEG_INF)
    _, idx = lax.top_k(score, n_sel)
    b_i = jnp.arange(B)[:, None, None, None]
    g_i = jnp.arange(G)[None, :, None, None]
    k_sel = ks_blk[b_i, idx, :, g_i]
    v_sel = vs_blk[b_i, idx, :, g_i]
    s_pos = idx[..., None] * SLC_BLOCK + jnp.arange(SLC_BLOCK)
    d_s = t_pos[None, None, :, None, None] - s_pos
    bias_s = jnp.moveaxis(table.reshape(N_BUCKETS, G, R)[rel_bucket(d_s), g_i[..., None]], -1, 2)
    s_s = jnp.einsum('btgrd,bgtkld->bgrtkl', qg, k_sel) + bias_s
    p_s = masked_softmax(s_s.reshape(B, G, R, Tq, -1), (d_s >= 0).reshape(B, G, 1, Tq, -1)).reshape(s_s.shape)
    o_s = jnp.einsum('bgrtkl,bgtkld->btgrd', p_s.astype(v_sel.dtype), v_sel)
    d_w = t_pos[:, None] - w_pos[None, :]
    s_w = jnp.einsum('btgrd,bsgd->bgrts', qg, kw) + head_bias(d_w, table)
    p_w = masked_softmax(s_w, (d_w >= 0) & (d_w <= WINDOW) & (w_pos >= 0)[None, :])
    o_w = jnp.einsum('bgrts,bsgd->btgrd', p_w.astype(vw.dtype), vw)
    g = gates.reshape(B, Tq, 3, G, R, 1)
    o = g[:, :, 0] * o_c + g[:, :, 1] * o_s + g[:, :, 2] * o_w
    return o.reshape(B, Tq, D_NSA)


def nsa_prompt(q_a, kv_c, kv_s, kv_w, gates, lp, table):
    B, S = q_a.shape[:2]
    kcmp = compress(kv_c[:, :, 0], lp['cmp_pe_k'], lp['w_cmp1_k'], lp['w_cmp2_k'])
    vcmp = compress(kv_c[:, :, 1], lp['cmp_pe_v'], lp['w_cmp1_v'], lp['w_cmp2_v'])
    ks_blk = kv_s[:, :, 0].reshape(B, S // SLC_BLOCK, SLC_BLOCK, N_KV, HEAD_DIM)
    vs_blk = kv_s[:, :, 1].reshape(B, S // SLC_BLOCK, SLC_BLOCK, N_KV, HEAD_DIM)
    kw_pad = jnp.pad(kv_w, ((0, 0), (WINDOW, 0), (0, 0), (0, 0), (0, 0)))

    def one_block(i):
        start = i * Q_BLOCK
        qb = lax.dynamic_slice_in_dim(q_a, start, Q_BLOCK, axis=1)
        gb = lax.dynamic_slice_in_dim(gates, start, Q_BLOCK, axis=1)
        kwb = lax.dynamic_slice_in_dim(kw_pad, start, Q_BLOCK + WINDOW, axis=1)
        t_pos = start + jnp.arange(Q_BLOCK)
        w_pos = start - WINDOW + jnp.arange(Q_BLOCK + WINDOW)
        return nsa_attend(qb, t_pos, kcmp, vcmp, ks_blk, vs_blk, kwb[:, :, 0], kwb[:, :, 1], w_pos, gb, table)

    o = lax.map(one_block, jnp.arange(S // Q_BLOCK))
    return jnp.moveaxis(o, 0, 1).reshape(B, S, D_NSA), kv_w[:, -WINDOW:]


def nsa_sample(q_a, kv_c, kv_s, kv_w, gates, lp, table, cache_c, cache_s, win_buf, page_table):
    B, T = q_a.shape[:2]
    P = page_table.shape[1] * PAGE_SIZE
    L = P + T
    Lp = -(-L // SLC_BLOCK) * SLC_BLOCK
    t_pos = P + jnp.arange(T)

    def full_rows(cache, new):
        past = cache[page_table].reshape(B, P, 2, N_KV, HEAD_DIM)
        rows = jnp.concatenate([past, new.astype(past.dtype)], axis=1)
        return jnp.pad(rows, ((0, 0), (0, Lp - L), (0, 0), (0, 0), (0, 0)))

    c_rows = full_rows(cache_c, kv_c)
    s_rows = full_rows(cache_s, kv_s)
    kcmp = compress(c_rows[:, :, 0], lp['cmp_pe_k'], lp['w_cmp1_k'], lp['w_cmp2_k'])
    vcmp = compress(c_rows[:, :, 1], lp['cmp_pe_v'], lp['w_cmp1_v'], lp['w_cmp2_v'])
    ks_blk = s_rows[:, :, 0].reshape(B, Lp // SLC_BLOCK, SLC_BLOCK, N_KV, HEAD_DIM)
    vs_blk = s_rows[:, :, 1].reshape(B, Lp // SLC_BLOCK, SLC_BLOCK, N_KV, HEAD_DIM)
    win = jnp.concatenate([win_buf.astype(kv_w.dtype), kv_w], axis=1)
    w_pos = P - win_buf.shape[1] + jnp.arange(win.shape[1])
    o = nsa_attend(q_a, t_pos, kcmp, vcmp, ks_blk, vs_blk, win[:, :, 0], win[:, :, 1], w_pos, gates, table)
    return o, win[:, -WINDOW:]


def retention_scan(q, k, v, s0, chunk):
    B, T, H, dk = q.shape
    dv = v.shape[-1]
    n_ch = T // chunk
    log_g = jnp.log1p(-jnp.exp2(-5.0 - jnp.arange(H, dtype=jnp.float32)))
    n = jnp.arange(chunk, dtype=jnp.float32)
    diff = n[:, None] - n[None, :]
    decay_in = jnp.where(diff >= 0, jnp.exp(jnp.maximum(diff, 0.0)[None] * log_g[:, None, None]), 0.0)
    decay_q = jnp.exp((n[None, :] + 1.0) * log_g[:, None]).T[None, :, :, None]
    decay_k = jnp.exp((chunk - 1.0 - n)[None, :] * log_g[:, None]).T[None, :, :, None]
    decay_c = jnp.exp(chunk * log_g)[None, :, None, None]

    def to_chunks(a):
        return jnp.moveaxis(a.astype(jnp.float32).reshape(B, n_ch, chunk, H, a.shape[-1]), 1, 0)

    def step(s, inp):
        qc, kc, vc = inp
        att = jnp.einsum('bnhd,bmhd->bhnm', qc, kc) * decay_in
        o = jnp.einsum('bhnm,bmhe->bnhe', att, vc) + jnp.einsum('bnhd,bhde->bnhe', qc, s) * decay_q
        s = s * decay_c + jnp.einsum('bmhd,bmhe->bhde', kc * decay_k, vc)
        return s, o

    s, o = lax.scan(step, s0.astype(jnp.float32), (to_chunks(q), to_chunks(k), to_chunks(v)))
    return jnp.moveaxis(o, 0, 1).reshape(B, T, H, dv), s


def retention_group(q, k, v, g, pos, s0, chunk, gn_g):
    B, T = q.shape[:2]
    q = rope(q, pos)
    k = rope(k, pos) * (RET_DK ** -0.5)
    o, s = retention_scan(q, k, v, s0, chunk)
    mu = jnp.mean(o, axis=-1, keepdims=True)
    var = jnp.mean(jnp.square(o - mu), axis=-1, keepdims=True)
    o = ((o - mu) * lax.rsqrt(var + GN_EPS)).reshape(B, T, D_RET) * gn_g
    return (jax.nn.silu(g.astype(jnp.float32)) * o).astype(g.dtype), s


def conv_ffn(hn, buf, w_up, conv_w, conv_b, w_down):
    T = hn.shape[1]
    a_g, a_v = jnp.split(hn @ w_up, 2, axis=-1)
    ext = jnp.concatenate([buf.astype(a_g.dtype), a_g], axis=1)
    conv = conv_b
    for j in range(CONV_W):
        conv = conv + conv_w[j] * ext[:, j:j + T]
    h = jax.nn.silu(conv) * a_v
    return h @ w_down, ext[:, T:]


def decoder_layer(x, c, lp, table, pos, nsa_mixer, ret_state, ret_chunk, conv_buf):
    sh1, sc1, gt1, sh2, sc2, gt2 = adaln(c, lp['w_ada'], lp['b_ada'])
    hn = rmsnorm(x, lp['norm_mix_g']) * (1 + sc1) + sh1
    q_a, kv_c, kv_s, kv_w, gates, q_r, k_r, v_r, g_r = split_projection(hn, lp['w_in'])
    o_a, win_new = nsa_mixer(q_a, kv_c, kv_s, kv_w, gates)
    o_r, ret_new = retention_group(q_r, k_r, v_r, g_r, pos, ret_state, ret_chunk, lp['ret_gn_g'])
    x = x + gt1 * (jnp.concatenate([o_a, o_r.astype(o_a.dtype)], axis=-1) @ lp['w_out'])
    hn2 = rmsnorm(x, lp['norm_ffn_g']) * (1 + sc2) + sh2
    f, conv_new = conv_ffn(hn2, conv_buf, lp['w_up'], lp['conv_w'], lp['conv_b'], lp['w_down'])
    x = x + gt2 * f
    return x, kv_c, kv_s, win_new, ret_new, conv_new


def setup_inputs(seed: int = 0) -> dict:
    key = jax.random.key(seed)
    ks = jax.random.split(key, 32)
    n_pages = PAST_LEN // PAGE_SIZE
    n_pool = (5 * DEC_BATCH * n_pages + 3) // 4
    win_buf = min(WINDOW, PAST_LEN)
    nrm = lambda k, shape, s: jax.random.normal(k, shape, jnp.float32) * s
    page_table = jax.random.permutation(ks[7], n_pool)[:DEC_BATCH * n_pages].reshape(DEC_BATCH, n_pages).astype(jnp.int32)
    return {
        'x_prompt': nrm(ks[0], (BATCH, SEQ, D_MODEL), 1.0),
        'x_sample': nrm(ks[1], (DEC_BATCH, DEC_SEQ, D_MODEL), 1.0),
        'cache_cmp_kv': nrm(ks[2], (DEPTH, n_pool, PAGE_SIZE, 2, N_KV, HEAD_DIM), 1.0),
        'cache_slc_kv': nrm(ks[3], (DEPTH, n_pool, PAGE_SIZE, 2, N_KV, HEAD_DIM), 1.0),
        'state_win_kv': nrm(ks[4], (DEPTH, DEC_BATCH, win_buf, 2, N_KV, HEAD_DIM), 1.0),
        'state_ret': nrm(ks[5], (DEPTH, DEC_BATCH, N_RET_HEADS, RET_DK, RET_DV), 0.5),
        'state_conv': nrm(ks[6], (DEPTH, DEC_BATCH, CONV_W - 1, D_FF), 1.0),
        'page_table': page_table,
        'c_prompt': nrm(ks[8], (BATCH, D_MODEL), 1.0),
        'c_sample': nrm(ks[9], (DEC_BATCH, D_MODEL), 1.0),
        'w_ada': nrm(ks[10], (DEPTH, D_MODEL, 6 * D_MODEL), 0.5 * D_MODEL ** -0.5),
        'b_ada': nrm(ks[11], (DEPTH, 6 * D_MODEL), 0.02),
        'norm_mix_g': 1.0 + nrm(ks[12], (DEPTH, D_MODEL), 0.05),
        'w_in': nrm(ks[13], (DEPTH, D_MODEL, D_IN), D_MODEL ** -0.5),
        'cmp_pe_k': nrm(ks[14], (DEPTH, CMP_BLOCK, HEAD_DIM), 0.5),
        'cmp_pe_v': nrm(ks[15], (DEPTH, CMP_BLOCK, HEAD_DIM), 0.5),
        'w_cmp1_k': nrm(ks[16], (DEPTH, CMP_BLOCK, HEAD_DIM, CMP_HID), (CMP_BLOCK * HEAD_DIM) ** -0.5),
        'w_cmp1_v': nrm(ks[17], (DEPTH, CMP_BLOCK, HEAD_DIM, CMP_HID), (CMP_BLOCK * HEAD_DIM) ** -0.5),
        'w_cmp2_k': nrm(ks[18], (DEPTH, CMP_HID, HEAD_DIM), CMP_HID ** -0.5),
        'w_cmp2_v': nrm(ks[19], (DEPTH, CMP_HID, HEAD_DIM), CMP_HID ** -0.5),
        'rel_bias_table': nrm(ks[20], (N_BUCKETS, N_NSA_HEADS), 0.5),
        'ret_gn_g': 1.0 + nrm(ks[21], (DEPTH, D_RET), 0.05),
        'w_out': nrm(ks[22], (DEPTH, D_MIX, D_MODEL), D_MIX ** -0.5),
        'norm_ffn_g': 1.0 + nrm(ks[23], (DEPTH, D_MODEL), 0.05),
        'w_up': nrm(ks[24], (DEPTH, D_MODEL, 2 * D_FF), D_MODEL ** -0.5),
        'conv_w': nrm(ks[25], (DEPTH, CONV_W, D_FF), CONV_W ** -0.5),
        'conv_b': nrm(ks[26], (DEPTH, D_FF), 0.02),
        'w_down': nrm(ks[27], (DEPTH, D_FF, D_MODEL), D_FF ** -0.5),
        'final_g': 1.0 + nrm(ks[28], (D_MODEL,), 0.05),
    }


def reference(x_prompt, x_sample, cache_cmp_kv, cache_slc_kv, state_win_kv, state_ret, state_conv, page_table,
              c_prompt, c_sample, w_ada, b_ada, norm_mix_g, w_in, cmp_pe_k, cmp_pe_v, w_cmp1_k, w_cmp1_v,
              w_cmp2_k, w_cmp2_v, rel_bias_table, ret_gn_g, w_out, norm_ffn_g, w_up, conv_w, conv_b, w_down, final_g):
    B, S = x_prompt.shape[:2]
    T = x_sample.shape[1]
    P = page_table.shape[1] * PAGE_SIZE
    xp, xs = x_prompt, x_sample
    cmp_p, cmp_s, slc_p, slc_s, win_p, win_s, ret_p, ret_s, conv_p, conv_s = [], [], [], [], [], [], [], [], [], []
    for l in range(DEPTH):
        lp = {'w_ada': w_ada[l], 'b_ada': b_ada[l], 'norm_mix_g': norm_mix_g[l], 'w_in': w_in[l],
              'cmp_pe_k': cmp_pe_k[l], 'cmp_pe_v': cmp_pe_v[l], 'w_cmp1_k': w_cmp1_k[l], 'w_cmp1_v': w_cmp1_v[l],
              'w_cmp2_k': w_cmp2_k[l], 'w_cmp2_v': w_cmp2_v[l], 'ret_gn_g': ret_gn_g[l], 'w_out': w_out[l],
              'norm_ffn_g': norm_ffn_g[l], 'w_up': w_up[l], 'conv_w': conv_w[l], 'conv_b': conv_b[l], 'w_down': w_down[l]}
        xp, a0, a1, a2, a3, a4 = decoder_layer(
            xp, c_prompt, lp, rel_bias_table, jnp.arange(S),
            functools.partial(nsa_prompt, lp=lp, table=rel_bias_table),
            jnp.zeros((B, N_RET_HEADS, RET_DK, RET_DV), jnp.float32), min(RET_CHUNK, S),
            jnp.zeros((B, CONV_W - 1, D_FF), xp.dtype))
        xs, b0, b1, b2, b3, b4 = decoder_layer(
            xs, c_sample, lp, rel_bias_table, P + jnp.arange(T),
            functools.partial(nsa_sample, lp=lp, table=rel_bias_table, cache_c=cache_cmp_kv[l],
                              cache_s=cache_slc_kv[l], win_buf=state_win_kv[l], page_table=page_table),
            state_ret[l], T, state_conv[l])
        cmp_p.append(a0); slc_p.append(a1); win_p.append(a2); ret_p.append(a3); conv_p.append(a4)
        cmp_s.append(b0); slc_s.append(b1); win_s.append(b2); ret_s.append(b3); conv_s.append(b4)
    y_prompt = rmsnorm(xp, final_g)
    y_sample = rmsnorm(xs, final_g)
    return (y_prompt, y_sample, jnp.stack(cmp_p), jnp.stack(cmp_s), jnp.stack(slc_p), jnp.stack(slc_s),
            jnp.stack(win_p), jnp.stack(win_s), jnp.stack(ret_p), jnp.stack(ret_s), jnp.stack(conv_p), jnp.stack(conv_s))
```

```python
import math
from contextlib import ExitStack

import numpy as np
import concourse.bass as bass
import concourse.mybir as mybir
from concourse.bass_utils import run_bass_kernel_spmd

F32 = mybir.dt.float32
BF16 = mybir.dt.bfloat16
I32 = mybir.dt.int32
AF = mybir.ActivationFunctionType
ALU = mybir.AluOpType
AX = mybir.AxisListType

D = 1024
KC = 8
NCORE = 8
BIG = 24576.0
RMS_EPS = 1e-6
GN_EPS = 1e-5
D_FF = 2816
NFC = D_FF // 128
NB_P = 256

SEM_ROT = 30000
import os as _os
STRICT = _os.environ.get("K_STRICT", "0") == "1"
N_DMA_SEMS = 24


class Sched:
    ENGS = ("pe", "act", "dve", "pool", "sp")

    def __init__(self, nc, stack):
        self.nc = nc
        self.stack = stack
        self.prog = {e: [] for e in self.ENGS}
        self.sems = {e: [stack.enter_context(nc.semaphore(f"s_{e}_0"))] for e in self.ENGS}
        self.count = {e: 0 for e in self.ENGS}
        self.seen = {e: {} for e in self.ENGS}
        self.track = {}
        self.dpool = {}
        for cls, n in (("hw", N_DMA_SEMS), ("sw", N_DMA_SEMS // 2)):
            self.dpool[cls] = dict(sems=[stack.enter_context(nc.semaphore(f"s_dma_{cls}_{i}")) for i in range(n)],
                                   count=[0] * n, last=[None] * n, next=0)
        self.ninstr = 0

    def _need_wait(self, eng, tok, raw):
        sem, val, src = tok
        if src == eng and (eng == "pe" or not (raw or STRICT)):
            return False
        k = id(sem)
        if self.seen[eng].get(k, -1) >= val:
            return False
        self.seen[eng][k] = val
        return True

    def _emit_waits(self, eng, toks):
        for tok, raw in toks:
            if tok is None:
                continue
            if self._need_wait(eng, tok, raw):
                sem, val, _ = tok
                self.prog[eng].append(lambda e, sem=sem, val=val: e.wait_ge(sem, val))

    def _deps(self, reads, writes):
        toks = []
        for k in reads:
            t = self.track.get(k)
            if t is not None:
                toks.append((t[0], True))
                if k.startswith("ps"):
                    toks.extend((r, False) for r in t[1])
        for k in writes:
            t = self.track.get(k)
            if t is not None:
                toks.append((t[0], False))
                toks.extend((r, False) for r in t[1])
        return toks

    def _record(self, tok, reads, writes):
        for k in reads:
            t = self.track.get(k)
            if t is None:
                self.track[k] = [None, [tok]]
            else:
                t[1].append(tok)
        for k in writes:
            self.track[k] = [tok, []]

    def op(self, eng, name, reads=(), writes=(), **kw):
        fn = lambda e, name=name, kw=dict(kw): getattr(e, name)(**kw)
        self._emit_waits(eng, self._deps(reads, writes))
        if self.count[eng] >= SEM_ROT:
            self.sems[eng].append(
                self.stack.enter_context(self.nc.semaphore(f"s_{eng}_{len(self.sems[eng])}")))
            self.count[eng] = 0
        sem = self.sems[eng][-1]
        self.count[eng] += 1
        tok = (sem, self.count[eng], eng)
        self.prog[eng].append(lambda e, fn=fn, sem=sem: fn(e).then_inc(sem, 1))
        self._record(tok, reads, writes)
        self.ninstr += 1
        return tok

    def dma(self, eng, reads=(), writes=(), method="dma_start", **kw):
        fn = lambda e, kw=dict(kw), method=method: getattr(e, method)(**kw)
        dp = self.dpool["sw" if eng == "pool" else "hw"]
        i = dp["next"]
        dp["next"] = (i + 1) % len(dp["sems"])
        toks = self._deps(reads, writes)
        if dp["last"][i] is not None:
            toks.append((dp["last"][i], True))
        self._emit_waits(eng, toks)
        if dp["count"][i] >= SEM_ROT:
            dp["sems"][i] = self.stack.enter_context(self.nc.semaphore(f"s_dma_r_{self.ninstr}"))
            dp["count"][i] = 0
        dp["count"][i] += 16
        sem = dp["sems"][i]
        tok = (sem, dp["count"][i], None)
        dp["last"][i] = tok
        self.prog[eng].append(lambda e, fn=fn, sem=sem: fn(e).then_inc(sem, 16))
        self._record(tok, reads, writes)
        self.ninstr += 1
        return tok

    def flush(self):
        for dp in self.dpool.values():
            for t in dp["last"]:
                if t is not None:
                    self._emit_waits("sp", [(t, True)])
        prog = self.prog
        with self.nc.Block() as block:
            @block.sync
            def _(e):
                for f in prog["sp"]:
                    f(e)

            @block.tensor
            def _(e):
                for f in prog["pe"]:
                    f(e)

            @block.scalar
            def _(e):
                for f in prog["act"]:
                    f(e)

            @block.vector
            def _(e):
                for f in prog["dve"]:
                    f(e)

            @block.gpsimd
            def _(e):
                for f in prog["pool"]:
                    f(e)
        self.prog = {e: [] for e in self.ENGS}
        self.track = {}


def rel_bucket_np(n):
    n = np.maximum(n, 0)
    nf = np.maximum(n, 1).astype(np.float32)
    large = 16 + (np.log(nf / np.float32(16)) / np.float32(math.log(64.0)) * np.float32(16)).astype(np.int32)
    return np.where(n < 16, n, np.minimum(large, 31))


class Cfg:
    def __init__(self, NT, NPG=128, NPOOL=5120):
        assert NT % 32 == 0
        self.NPG = NPG
        self.NPOOL = NPOOL
        self.NCB_S = (8 * NPG + 127) // 128
        self.NB_S = ((2 * NPG + 1 + 127) // 128) * 128
        self.NT = NT
        self.NSLOT = NT
        self.KSEG = NT // 32
        self.NOWN = 4 * self.KSEG
        self.NCB = (NT * 8 + 127) // 128
        self.NWIN = 9 * self.KSEG

    def shift(self, c):
        return 4 * (7 - c)

    def halo_slot(self, k):
        return 32 * k + 27

    def own_slot(self, k, u):
        return 32 * k + 28 + u

    def win_index(self, p):
        k, r = divmod(p, 32)
        if r < 23:
            return None
        return 9 * k + (r - 23)


XS_LEN = 1152 + 8
XW_LEN = 768 + 8
XC_LEN = 5120


def make_consts(cfg, c):
    NS = cfg.NSLOT
    sh = cfg.shift(c)
    out = {}
    pos = (np.arange(NS * 128) - sh * 128).astype(np.int64)
    posc = np.maximum(pos, 0).astype(np.float32)
    inv = (10000.0 ** (-np.arange(32, dtype=np.float32) / 32)).astype(np.float32)
    ang = posc[:, None] * inv[None, :]
    out["c_cos"] = np.cos(ang).astype(np.float32)
    out["c_sin"] = np.sin(ang).astype(np.float32)
    h = np.arange(8, dtype=np.float64)
    gam = 1.0 - 2.0 ** (-5.0 - h)
    m = np.arange(128, dtype=np.float64)
    gk = (gam[None, :] ** (-(m[:, None] + 1.0))) / 8.0
    gq = gam[None, :] ** (m[:, None] + 1.0)
    out["c_gk"] = np.repeat(gk[:, :, None], 64, axis=2).reshape(128, 512).astype(np.float32)
    out["c_gq"] = np.repeat(gq[:, :, None], 64, axis=2).reshape(128, 512).astype(np.float32)
    dc = gam ** 128.0
    out["c_dc"] = np.broadcast_to(np.repeat(dc, 64)[None, :], (128, 512)).astype(np.float32).copy()
    valid = (np.arange(NS) >= sh).astype(np.float32)
    out["c_valid"] = np.broadcast_to(valid[None, :], (128, NS)).copy()
    tri = (np.arange(128)[:, None] <= np.arange(128)[None, :]).astype(np.float32)
    out["c_tri8"] = np.tile(tri, (1, 8)).astype(np.float32)
    out["c_ident"] = np.eye(128, dtype=np.float32)
    out["c_anti"] = np.eye(128, dtype=np.float32)[::-1].copy()
    ind = (np.arange(128)[:, None] == (np.arange(8192)[None, :] // 64)).astype(np.float32)
    out["c_ind"] = ind
    NB = NB_P
    n = np.arange(cfg.NCB * 128)
    b = np.arange(NB)
    A = ((n[:, None] >= 4 * b[None, :] - 1) & (n[:, None] <= 4 * b[None, :] + 3)).astype(np.float32)
    out["c_amat"] = A.reshape(cfg.NCB, 128, NB).transpose(1, 0, 2).copy()
    nabs = n - 8 * sh
    out["c_cmpb"] = np.where(nabs >= 0, 0.0, -BIG).astype(np.float32).reshape(cfg.NCB, 128).T.copy()
    wb = np.zeros((128, cfg.NWIN), np.float32)
    for p in range(NS):
        wi = cfg.win_index(p)
        if wi is not None:
            wb[:, wi] = 0.0 if p >= sh else -BIG
    out["c_winb"] = wb
    fsel = np.zeros((cfg.KSEG * 5, 128, NB), np.float32)
    for k in range(cfg.KSEG):
        for j in range(5):
            p = cfg.halo_slot(k) + j
            t_abs = (p - sh) * 128 + np.arange(128)
            b_abs = np.arange(NB) - 2 * sh
            cur = t_abs // 64
            validb = (b_abs[None, :] >= 0) & (b_abs[None, :] * 64 <= t_abs[:, None])
            forced = (b_abs[None, :] == 0) | (b_abs[None, :] == cur[:, None]) | (b_abs[None, :] == cur[:, None] - 1)
            f = np.where(validb, np.where(forced, 1e4, 0.0), -1e30)
            fsel[k * 5 + j] = f
    out["c_fsel"] = fsel
    hv = np.ones((128, max(cfg.KSEG, 1)), np.float32)
    for k in range(cfg.KSEG):
        if cfg.halo_slot(k) < sh:
            hv[:, k] = 0.0
    out["c_hval"] = hv
    pm = np.zeros((128, 8, 64), np.float32)
    for h_ in range(8):
        pm[64 * (h_ % 2):64 * (h_ % 2) + 64, h_, :] = 1.0
    out["c_pmk"] = pm.reshape(128, 512)
    def oh(dists, mode):
        X = len(dists)
        o = np.zeros((33, X), np.float32)
        bk = rel_bucket_np(dists)
        for x in range(X):
            d = dists[x]
            if mode == "s":
                if d < 0:
                    o[32, x] = 1.0
                else:
                    o[bk[x], x] += 1.0
                    o[31, x] -= 1.0
            elif mode == "w":
                if d < 0 or d > 512:
                    o[32, x] = 1.0
                else:
                    o[bk[x], x] = 1.0
            else:
                if d < 0:
                    o[32, x] = 1.0
                else:
                    o[bk[x], x] = 1.0
        return o
    NPG, NB_S, NCB_S = cfg.NPG, cfg.NB_S, cfg.NCB_S
    out["c_iota"] = np.arange(128, dtype=np.float32)[:, None].copy()
    angp = np.float32(128 * NPG) * inv
    out["c_cs_s"] = np.stack([np.cos(angp), np.sin(angp)]).astype(np.float32)
    tP = 128 * NPG
    bS = np.arange(NB_S)
    curS = tP // 64
    vS = bS * 64 <= tP
    fS = (bS == 0) | (bS == curS) | (bS == curS - 1)
    rowS = np.where(vS, np.where(fS, 1e4, 0.0), -1e30).astype(np.float32)
    out["c_fsel_s"] = np.broadcast_to(rowS[None, :], (128, NB_S)).copy()
    nS = np.arange(NCB_S * 128)
    AS = ((nS[:, None] >= 4 * bS[None, :] - 1) & (nS[:, None] <= 4 * bS[None, :] + 3)).astype(np.float32)
    out["c_amat_s"] = AS.reshape(NCB_S, 128, NB_S).transpose(1, 0, 2).copy()
    out["c_dc1"] = np.broadcast_to(np.repeat(gam, 64)[None, :], (128, 512)).astype(np.float32).copy()
    out["c_ohs"] = oh(np.arange(XS_LEN) - 127, "s")
    out["c_ohw"] = oh(np.arange(XW_LEN) - 127, "w")
    out["c_ohc"] = oh(np.arange(XC_LEN) - 2063, "c")
    return out


W_KV_COLS = 1792
W_Q_COLS = 1560


def build_program(cfg, phases=("0", "1a", "1b", "2"), debug=False):
    import os
    STOP = os.environ.get("K_STOP", "")
    NSLOT_LIM = int(os.environ.get("K_NSLOT", "100000"))
    nc = bass.Bass("TRN2", target_bir_lowering=False)
    NS, KSEG, NCB, NWIN, NOWN = cfg.NSLOT, cfg.KSEG, cfg.NCB, cfg.NWIN, cfg.NOWN
    NB = NB_P
    NPIECE = NB // 128

    def din(name, shape, dt=F32):
        return nc.dram_tensor(name, list(shape), dt, kind="ExternalInput")

    def dout(name, shape, dt=F32):
        return nc.dram_tensor(name, list(shape), dt, kind="ExternalOutput")

    xp = din("xp", [NS * 128, D])
    w_kv = din("w_kv", [D, W_KV_COLS])
    w_q = din("w_q", [D, W_Q_COLS])
    w_ada = din("w_ada", [D, 6 * D])
    b_ada = din("b_ada", [1, 6 * D])
    cin = din("cin", [33, D])
    g1 = din("g1", [1, D]); g2 = din("g2", [1, D]); gf = din("gf", [1, D])
    table = din("table", [32, 8])
    w1k = din("w1k", [128, 32 * 128]); w1v = din("w1v", [128, 32 * 128])
    w2k = din("w2k", [128, 64]); w2v = din("w2v", [128, 64])
    pek = din("pek", [128, 32]); pev = din("pev", [128, 32])
    gng = din("gng", [1, 512])
    w_out = din("w_out", [D, D]); w_up = din("w_up", [D, 2 * D_FF]); w_down = din("w_down", [D_FF, D])
    conv_w = din("conv_w", [3, D_FF]); conv_b = din("conv_b", [1, D_FF])
    NPG, NPOOL, NSMP = cfg.NPG, cfg.NPOOL, 4
    NCB_S, NB_S = cfg.NCB_S, cfg.NB_S
    NSLT = max(NS, NPG + 1)
    NCBT = max(NCB, NCB_S)
    TS = KSEG * 5
    xs = din("xs", [NSMP, D]); ptab = din("pt", [NSMP, NPG], I32)
    cache_c = din("cache_c", [NPOOL * 128, 256]); cache_s = din("cache_s", [NPOOL * 128, 256])
    swin = din("swin", [NSMP, 512, 256]); sret = din("sret", [NSMP, 8, 64, 64]); sconv = din("sconv", [NSMP * 2, D_FF])
    cn = {}
    for name, shape in (("c_iota", [128, 1]), ("c_cs_s", [2, 32]), ("c_fsel_s", [128, NB_S]), ("c_amat_s", [128, NCB_S, NB_S]), ("c_dc1", [128, 512])):
        cn[name] = din(name, shape)
    for name, shape in (("c_cos", [NS * 128, 32]), ("c_sin", [NS * 128, 32]), ("c_gk", [128, 512]), ("c_gq", [128, 512]),
                        ("c_dc", [128, 512]), ("c_valid", [128, NS]), ("c_tri8", [128, 1024]), ("c_ident", [128, 128]),
                        ("c_anti", [128, 128]), ("c_ind", [128, 8192]), ("c_amat", [128, NCB, NB]), ("c_cmpb", [128, NCB]),
                        ("c_winb", [128, NWIN]), ("c_fsel", [KSEG * 5, 128, NB]), ("c_hval", [128, KSEG]), ("c_pmk", [128, 512]),
                        ("c_ohs", [33, XS_LEN]), ("c_ohw", [33, XW_LEN]), ("c_ohc", [33, XC_LEN])):
        cn[name] = din(name, shape)

    o_y = dout("o_y", [NOWN, 128, D])
    o_kvc = dout("o_kvc", [NOWN, 128, 256]); o_kvs = dout("o_kvs", [NOWN, 128, 256]); o_kvw = dout("o_kvw", [NOWN, 128, 256])
    o_ret = dout("o_ret", [128, 512])
    o_conv = dout("o_conv", [2, D_FF])
    o_ys = dout("o_ys", [NSMP, D]); o_skv = dout("o_skv", [NSMP, 768]); o_swin = dout("o_swin", [NSMP, 512, 256])
    o_sret = dout("o_sret", [NSMP, 64, 512]); o_sconv = dout("o_sconv", [NSMP, 2, D_FF])
    dbg = {}
    if debug:
        dbg["kcmpT"] = dout("d_kcmpT", [128, NCB * 128])
        dbg["vcmp"] = dout("d_vcmp", [128, NCB * 2 * 65])
        dbg["omix"] = dout("d_omix", [KSEG * 5, 128, 1024])

    modv_d = nc.dram_tensor("modv_d", [33, 6 * D], F32)
    krv_d = nc.dram_tensor("krv_d", [KSEG * 5 + 1, 128, 1536], BF16)
    skv_d = nc.dram_tensor("skv_d", [4, 768], F32)
    bvs_d = nc.dram_tensor("bvs_d", [8, XS_LEN], F32)
    bvw_d = nc.dram_tensor("bvw_d", [8, XW_LEN], F32)
    bvc_d = nc.dram_tensor("bvc_d", [8, XC_LEN], F32)
    om_d = nc.dram_tensor("om_d", [KSEG * 5 + 1, 128, 1024], BF16)
    kw_d = nc.dram_tensor("kw_d", [NWIN + 5, 128, 128], BF16)
    vw_d = nc.dram_tensor("vw_d", [NWIN + 5, 128, 128], BF16)
    snap_d = nc.dram_tensor("snap_d", [KSEG, 128, 512], F32)
    qt_d = nc.dram_tensor("qt_d", [KSEG * 5 + 1, 128, 512], BF16)
    qrt_d = nc.dram_tensor("qrt_d", [KSEG * 5 + 1, 128, 512], BF16)
    gates_d = nc.dram_tensor("gates_d", [KSEG * 5 + 1, 128, 24], F32)
    sg_d = nc.dram_tensor("sg_d", [KSEG * 5 + 1, 128, 512], F32)

    with ExitStack() as top:
        S = Sched(nc, top)

        _uid = [0]

        def SB(stack, name, shape, dt):
            _uid[0] += 1
            return stack.enter_context(nc.sbuf_tensor(f"{name}_{_uid[0]}", list(shape), dt))

        def PS(stack, name, shape, dt):
            _uid[0] += 1
            return stack.enter_context(nc.psum_tensor(f"{name}_{_uid[0]}", list(shape), dt))

        p1 = top.enter_context(ExitStack())
        ident_f = SB(p1, "ident_f", [128, 128], F32)
        ident_b = SB(p1, "ident_b", [128, 128], BF16)
        anti_b = SB(p1, "anti_b", [128, 128], BF16)
        A1bc = SB(p1, "A1bc", [128, D], F32)
        SH1bc = SB(p1, "SH1bc", [128, D], F32)
        KsT = SB(p1, "KsT", [128, NSLT * 128], BF16)
        Vs = SB(p1, "Vs", [128, NSLT, 2, 65], BF16)
        kcmpT = SB(p1, "kcmpT", [128, NCBT * 128], BF16)
        vcmp = SB(p1, "vcmp", [128, NCBT, 2, 65], BF16)
        S.dma("sp", out=ident_f[:], in_=cn["c_ident"].ap()[:, :], writes=["ident_f"])
        S.dma("pool", out=ident_b[:], in_=cn["c_ident"].ap()[:, :], writes=["ident_b"])
        S.dma("pool", out=anti_b[:], in_=cn["c_anti"].ap()[:, :], writes=["anti_b"])
        S.op("pool", "memset", ap=Vs[:], constant=1.0, writes=["Vs"])
        S.op("pool", "memset", ap=vcmp[:], constant=1.0, writes=["vcmp"])

        with ExitStack() as ph:
            cs = SB(ph, "cs", [33, D], F32)
            csT = SB(ph, "csT", [128, KC, 33], F32)
            mod = SB(ph, "mod", [33, 6 * D], F32)
            bab = SB(ph, "bab", [33, 6 * D], F32)
            gb = SB(ph, "gb", [33, 2, D], F32)
            mv = SB(ph, "mv", [33, 6, D], F32)
            wa = [SB(ph, f"wa{i}", [128, KC, 512], F32) for i in range(2)]
            ps0 = PS(ph, "ps0", [128, 512], F32)
            ps1 = [PS(ph, f"ps1_{i}", [128, 512], F32) for i in range(2)]
            S.dma("sp", out=cs[:], in_=cin.ap()[:, :], writes=["cs"])
            S.dma("sp", out=bab[:], in_=b_ada.ap()[0:1, :].partition_broadcast(33), writes=["bab"])
            S.dma("sp", out=gb[:, 0, :], in_=g1.ap()[0:1, :].partition_broadcast(33), writes=["gb0"])
            S.dma("sp", out=gb[:, 1, :], in_=g2.ap()[0:1, :].partition_broadcast(33), writes=["gb1"])
            S.op("act", "activation", out=cs[:], in_=cs[:], func=AF.Silu, reads=["cs"], writes=["cs"])
            for kc in range(KC):
                S.op("pe", "transpose", out=ps0[:, kc * 33:(kc + 1) * 33], in_=cs[0:33, kc * 128:(kc + 1) * 128],
                     identity=ident_f[0:33, 0:33], reads=["cs", "ident_f"], writes=["ps0"])
            S.op("dve", "tensor_copy", out=csT[:].rearrange("p k m -> p (k m)"), in_=ps0[:, 0:KC * 33], reads=["ps0"], writes=["csT"])
            for cb in range(12):
                w = wa[cb % 2]
                S.dma("sp" if cb % 2 == 0 else "act", out=w[:],
                      in_=w_ada.ap()[:, cb * 512:(cb + 1) * 512].rearrange("(k p) c -> p k c", p=128), writes=[f"wa{cb % 2}"])
                pz = ps1[cb % 2]
                for kc in range(KC):
                    S.op("pe", "matmul", out=pz[0:33, :], lhsT=csT[:, kc, :], rhs=w[:, kc, :], start=(kc == 0), stop=(kc == KC - 1),
                         reads=["csT", f"wa{cb % 2}"], writes=[f"ps1_{cb % 2}"])
                S.op("dve", "tensor_tensor", out=mod[:, cb * 512:(cb + 1) * 512], in0=pz[0:33, :], in1=bab[:, cb * 512:(cb + 1) * 512],
                     op=ALU.add, reads=[f"ps1_{cb % 2}", "bab"], writes=["mod"])
            S.op("dve", "scalar_tensor_tensor", out=mv[:, 0, :], in0=mod[:, D:2 * D], scalar=1.0, in1=gb[:, 0, :], op0=ALU.add, op1=ALU.mult,
                 reads=["mod", "gb0"], writes=["mv"])
            S.op("dve", "scalar_tensor_tensor", out=mv[:, 3, :], in0=mod[:, 4 * D:5 * D], scalar=1.0, in1=gb[:, 1, :], op0=ALU.add, op1=ALU.mult,
                 reads=["mod", "gb1"], writes=["mv"])
            S.op("pool", "tensor_copy", out=mv[:, 1, :], in_=mod[:, 0:D], reads=["mod"], writes=["mv1"])
            S.op("pool", "tensor_copy", out=mv[:, 2, :], in_=mod[:, 2 * D:3 * D], reads=["mod"], writes=["mv2"])
            S.op("pool", "tensor_copy", out=mv[:, 4, :], in_=mod[:, 3 * D:4 * D], reads=["mod"], writes=["mv4"])
            S.op("pool", "tensor_copy", out=mv[:, 5, :], in_=mod[:, 5 * D:6 * D], reads=["mod"], writes=["mv5"])
            S.dma("sp", out=modv_d.ap()[:, :], in_=mv[:].rearrange("p a d -> p (a d)"), reads=["mv", "mv1", "mv2", "mv4", "mv5"], writes=["modv_d"])
            S.dma("sp", out=A1bc[:], in_=modv_d.ap()[32:33, 0:D].partition_broadcast(128), reads=["modv_d"], writes=["A1bc"])
            S.dma("sp", out=SH1bc[:], in_=modv_d.ap()[32:33, D:2 * D].partition_broadcast(128), reads=["modv_d"], writes=["SH1bc"])
            S.flush()

        if "1a" not in phases:
            return nc
        def phase_1a(smp):
            NSL = NS if smp is None else NPG
            with ExitStack() as ph:
                W1 = [SB(ph, "W1k", [128, 32, 128], BF16), SB(ph, "W1v", [128, 32, 128], BF16)]
                W2 = SB(ph, "W2", [128, 2, 2, 128], BF16)
                peT = SB(ph, "peT", [128, 2, 32], BF16)
                PRE = SB(ph, "PRE", [128, 4, max(NS, NPG) * 8 + 2], F32)
                GK = SB(ph, "GK", [128, 512], F32)
                DC = SB(ph, "DC", [128, 512], F32)
                VALID = SB(ph, "VALID", [128, NS], F32)
                Sst = SB(ph, "Sst", [128, 512], F32)
                psT = PS(ph, "psT", [128, 1024], BF16)
                psT2 = PS(ph, "psT2", [128, 1024], BF16)
                psZ = [PS(ph, f"psZ{i}", [128, 512], F32) for i in range(4)]
                psU = PS(ph, "psU", [128, 512], F32)
                psC = PS(ph, "psC", [128, 512], F32)

                ws = ExitStack()
                Wkv = SB(ws, "Wkv", [128, KC, W_KV_COLS], BF16)
                csb = [SB(ws, f"csb{i}", [128, 2, 32], F32) for i in range(2)]
                xb = [SB(ws, f"xb{i}", [128, D], F32) for i in range(2)]
                st4 = [SB(ws, f"st4_{i}", [128, 4], F32) for i in range(2)]
                t1 = SB(ws, "t1", [128, D], F32)
                hn = [SB(ws, f"hn{i}", [128, D], BF16) for i in range(2)]
                hnT = [SB(ws, f"hnT{i}", [128, KC, 128], BF16) for i in range(2)]
                NKB = 4
                kvb = [SB(ws, f"kvb{i}", [128, 768], BF16) for i in range(NKB)]
                kvf = [SB(ws, "kvf0", [128, 768], F32)] * 2
                BT = 16 if NSL % 16 == 0 else 4
                CT4 = [SB(ws, f"CT4_{i}", [128, BT, 2, 128], BF16) for i in range(2)]
                rt = [SB(ws, f"rt{i}", [128, 8, 32], F32) for i in range(4)]
                kro = SB(ws, "kro", [128, 8, 2, 32], F32)
                k2 = [SB(ws, f"k2_{i}", [128, 8, 2, 64], BF16) for i in range(2)]
                vb = [SB(ws, f"vb{i}", [128, 512], BF16) for i in range(2)]
                tmpS = SB(ws, "tmpS", [128, 512], F32)
                kwt = [SB(ws, f"kwt{i}", [128, 128], BF16) for i in range(2)]
                pti = SB(ws, "pti", [128, NPG], I32)
                IDX = SB(ws, "IDX", [128, NPG], I32)
                iota = SB(ws, "iota", [128, 1], F32)

                if smp is None:
                    for kc in range(KC):
                        S.dma("pool", out=Wkv[:, kc, :], in_=w_kv.ap()[kc * 128:(kc + 1) * 128, :], writes=["Wkv"])
                else:
                    S.dma("sp", out=pti[:], in_=ptab.ap()[smp:smp + 1, :].partition_broadcast(128), writes=["pti"])
                    S.dma("sp", out=iota[:], in_=cn["c_iota"].ap()[:, :], writes=["iota"])
                    S.op("dve", "tensor_scalar", out=IDX[:], in0=pti[:], scalar1=128.0, scalar2=iota[:, 0:1], op0=ALU.mult, op1=ALU.add,
                         reads=["pti", "iota"], writes=["IDX"])
                for i, w1 in enumerate((w1k, w1v)):
                    for l0 in range(0, 32, 8):
                        S.dma("pool", out=W1[i][:, l0:l0 + 8, :], in_=w1.ap()[:, l0 * 128:(l0 + 8) * 128].rearrange("p (l h) -> p l h", h=128),
                              writes=[f"W1_{i}"])
                S.op("pool", "memset", ap=W2[:], constant=0.0, writes=["W2"])
                for kv, w2 in enumerate((w2k, w2v)):
                    for g in range(2):
                        S.dma("pool", out=W2[:, kv, g, 64 * g:64 * g + 64], in_=w2.ap()[:, :], reads=["W2"], writes=[f"W2_{kv}{g}"])
                S.dma("pool", out=peT[:, 0, :], in_=pek.ap()[:, :], writes=["peT0"])
                S.dma("pool", out=peT[:, 1, :], in_=pev.ap()[:, :], writes=["peT1"])
                S.dma("sp", out=GK[:], in_=cn["c_gk"].ap()[:, :], writes=["GK"])
                S.dma("sp", out=DC[:], in_=cn["c_dc"].ap()[:, :], writes=["DC"])
                S.dma("sp", out=VALID[:], in_=cn["c_valid"].ap()[:, :], writes=["VALID"])
                S.op("dve", "memset", ap=Sst[:], constant=0.0, writes=["Sst"])
                S.op("dve", "memset", ap=PRE[:], constant=0.0, writes=["PRE"])

                own_of_slot = {}
                for k in range(KSEG):
                    own_of_slot[cfg.halo_slot(k)] = (k, 0)
                    for u in range(4):
                        own_of_slot[cfg.own_slot(k, u)] = (k, u + 1)

                for p in range(NSL):
                    b2 = p % 2
                    kb = p % NKB
                    if smp is None:
                        x_t = xb[b2]
                        S.dma("sp", out=x_t[:], in_=xp.ap()[p * 128:(p + 1) * 128, :], writes=[f"xb{b2}"])
                        stt_ = st4[b2]
                        S.op("act", "activation", out=t1[:], in_=x_t[:], func=AF.Square, accum_out=stt_[:, 0:1],
                             reads=[f"xb{b2}"], writes=["t1", f"st{b2}"])
                        S.op("dve", "tensor_scalar", out=stt_[:, 1:2], in0=stt_[:, 0:1], scalar1=1.0 / D, scalar2=RMS_EPS, op0=ALU.mult, op1=ALU.add,
                             reads=[f"st{b2}"], writes=[f"st{b2}"])
                        S.op("act", "activation", out=stt_[:, 2:3], in_=stt_[:, 1:2], func=AF.Sqrt, reads=[f"st{b2}"], writes=[f"st{b2}"])
                        S.op("dve", "reciprocal", out=stt_[:, 3:4], in_=stt_[:, 2:3], reads=[f"st{b2}"], writes=[f"st{b2}"])
                        S.op("dve", "scalar_tensor_tensor", out=t1[:], in0=x_t[:], scalar=stt_[:, 3:4], in1=A1bc[:], op0=ALU.mult, op1=ALU.mult,
                             reads=[f"xb{b2}", f"st{b2}", "A1bc"], writes=["t1"])
                        S.op("pool", "tensor_tensor", out=hn[b2][:], in0=t1[:], in1=SH1bc[:], op=ALU.add, reads=["t1", "SH1bc"], writes=[f"hn{b2}"])
                        for kc in range(KC):
                            S.op("pe", "transpose", out=psT[:, kc * 128:(kc + 1) * 128], in_=hn[b2][:, kc * 128:(kc + 1) * 128], identity=ident_b[:],
                                 reads=[f"hn{b2}", "ident_b"], writes=["psT"])
                        S.op("act", "copy", out=hnT[b2][:].rearrange("p k t -> p (k t)"), in_=psT[:, :], reads=["psT"], writes=[f"hnT{b2}"])
                        pieces = ((0, 0, 512), (1, 512, 256), (2, 768, 512), (3, 1280, 512))
                        for zi, c0, w in pieces:
                            for kc in range(KC):
                                S.op("pe", "matmul", out=psZ[zi][:, 0:w], lhsT=hnT[b2][:, kc, :], rhs=Wkv[:, kc, c0:c0 + w], start=(kc == 0), stop=(kc == KC - 1),
                                     reads=[f"hnT{b2}", "Wkv"], writes=[f"psZ{zi}"])
                        S.op("act", "copy", out=kvb[kb][:, 0:512], in_=psZ[0][:, 0:512], reads=["psZ0"], writes=[f"kvb{kb}a"])
                        S.op("dve", "tensor_copy", out=kvb[kb][:, 512:768], in_=psZ[1][:, 0:256], reads=["psZ1"], writes=[f"kvb{kb}b"])
                        if p in own_of_slot and own_of_slot[p][1] >= 1:
                            k, j = own_of_slot[p]
                            oi = 4 * k + (j - 1)
                            S.op("dve", "tensor_copy", out=kvf[b2][:, 0:512], in_=psZ[0][:, 0:512], reads=["psZ0"], writes=["kvf0a"])
                            S.op("act", "copy", out=kvf[b2][:, 512:768], in_=psZ[1][:, 0:256], reads=["psZ1"], writes=["kvf0b"])
                            if os.environ.get("K_NOOUT", "") != "1":
                                S.dma("sp", out=o_kvc.ap()[oi], in_=kvf[b2][:, 0:256], reads=["kvf0a"])
                                S.dma("sp", out=o_kvs.ap()[oi], in_=kvf[b2][:, 256:512], reads=["kvf0a"])
                                S.dma("sp", out=o_kvw.ap()[oi], in_=kvf[b2][:, 512:768], reads=["kvf0b"])

                    else:
                        S.dma("pool", method="indirect_dma_start", out=kvb[kb][:, 0:256], out_offset=None, in_=cache_c.ap()[:, :],
                              in_offset=bass.IndirectOffsetOnAxis(ap=IDX[:, p:p + 1], axis=0), reads=["IDX"], writes=[f"kvb{kb}a"])
                        S.dma("pool", method="indirect_dma_start", out=kvb[kb][:, 256:512], out_offset=None, in_=cache_s.ap()[:, :],
                              in_offset=bass.IndirectOffsetOnAxis(ap=IDX[:, p:p + 1], axis=0), reads=["IDX"], writes=[f"kvb{kb}s"])
                    S.op("pool", "tensor_copy", out=Vs[:, p, :, 0:64], in_=kvb[kb][:, 384:512].rearrange("p (g d) -> p g d", g=2),
                         reads=[f"kvb{kb}a", f"kvb{kb}s", "Vs"], writes=[f"Vs{p}"])
                    wi = cfg.win_index(p) if smp is None else None
                    S.op("pe", "transpose", out=psT2[:, 0:128], in_=kvb[kb][:, 0:128], identity=ident_b[:], reads=[f"kvb{kb}a", "ident_b"], writes=["psT2"])
                    S.op("pe", "transpose", out=psT2[:, 128:256], in_=kvb[kb][:, 128:256], identity=ident_b[:], reads=[f"kvb{kb}a"], writes=["psT2"])
                    S.op("pe", "transpose", out=psT2[:, 256:384], in_=kvb[kb][:, 256:384], identity=ident_b[:], reads=[f"kvb{kb}a", f"kvb{kb}s"], writes=["psT2"])
                    if wi is not None:
                        S.op("pe", "transpose", out=psT2[:, 384:512], in_=kvb[kb][:, 512:640], identity=ident_b[:], reads=[f"kvb{kb}b"], writes=["psT2"])
                    c4 = CT4[(p // BT) % 2]
                    S.op("act", "copy", out=c4[:, p % BT, :, :], in_=psT2[:, 0:256].rearrange("p (a x) -> p a x", a=2), reads=["psT2"],
                         writes=[f"CT4_{(p // BT) % 2}_{p % BT}"])
                    S.op("act", "copy", out=KsT[:, p * 128:(p + 1) * 128], in_=psT2[:, 256:384], reads=["psT2"], writes=[f"KsT{p}"])
                    if wi is not None:
                        S.op("act", "copy", out=kwt[b2][:], in_=psT2[:, 384:512], reads=["psT2"], writes=[f"kwt{b2}"])
                        S.dma("sp", out=kw_d.ap()[wi], in_=kwt[b2][:], reads=[f"kwt{b2}"])
                        S.dma("sp", out=vw_d.ap()[wi], in_=kvb[kb][:, 640:768], reads=[f"kvb{kb}b"])
                    if smp is None:
                        kq = psZ[2][:, :].rearrange("p (h two i) -> p h two i", h=8, two=2)
                        S.dma("act", out=csb[b2][:, 0, :], in_=cn["c_cos"].ap()[p * 128:(p + 1) * 128, :], writes=[f"cosr{b2}"])
                        S.dma("act", out=csb[b2][:, 1, :], in_=cn["c_sin"].ap()[p * 128:(p + 1) * 128, :], writes=[f"sinr{b2}"])
                        cs_ = csb[b2][:, 0, :].unsqueeze(1).to_broadcast([128, 8, 32])
                        sn_ = csb[b2][:, 1, :].unsqueeze(1).to_broadcast([128, 8, 32])
                        S.op("dve", "tensor_tensor", out=rt[0][:], in0=kq[:, :, 0, :], in1=cs_, op=ALU.mult, reads=["psZ2", f"cosr{b2}"], writes=["rt0"])
                        S.op("dve", "tensor_tensor", out=rt[1][:], in0=kq[:, :, 1, :], in1=sn_, op=ALU.mult, reads=["psZ2", f"sinr{b2}"], writes=["rt1"])
                        S.op("dve", "tensor_tensor", out=rt[2][:], in0=kq[:, :, 0, :], in1=sn_, op=ALU.mult, reads=["psZ2", f"sinr{b2}"], writes=["rt2"])
                        S.op("dve", "tensor_tensor", out=rt[3][:], in0=kq[:, :, 1, :], in1=cs_, op=ALU.mult, reads=["psZ2", f"cosr{b2}"], writes=["rt3"])
                        S.op("pool", "tensor_tensor", out=kro[:, :, 0, :], in0=rt[0][:], in1=rt[1][:], op=ALU.subtract, reads=["rt0", "rt1"], writes=["kro0"])
                        S.op("pool", "tensor_tensor", out=kro[:, :, 1, :], in0=rt[2][:], in1=rt[3][:], op=ALU.add, reads=["rt2", "rt3"], writes=["kro1"])
                        krov = kro[:].rearrange("p h two i -> p h (two i)")
                        gkv = GK[:].rearrange("p (h d) -> p h d", h=8)
                        S.op("pool", "tensor_tensor", out=k2[b2][:, :, 0, :], in0=krov, in1=gkv, op=ALU.mult, reads=["kro0", "kro1", "GK"], writes=[f"k2_{b2}a"])
                        S.op("dve", "tensor_tensor", out=k2[b2][:, :, 1, :], in0=krov, in1=gkv, op=ALU.mult, reads=["kro0", "kro1", "GK"], writes=[f"k2_{b2}b"])
                        S.op("act", "copy", out=vb[b2][:], in_=psZ[3][:, :], reads=["psZ3"], writes=[f"vb{b2}"])
                        if p in own_of_slot:
                            k, j = own_of_slot[p]
                            ti = 5 * k + j
                            S.dma("sp", out=krv_d.ap()[ti, :, 0:1024], in_=k2[b2][:].rearrange("p h a d -> p (h a d)"), reads=[f"k2_{b2}a", f"k2_{b2}b"], writes=[f"krv{ti}a"])
                            S.dma("sp", out=krv_d.ap()[ti, :, 1024:1536], in_=vb[b2][:], reads=[f"vb{b2}"], writes=[f"krv{ti}b"])
                            if j == 0:
                                S.dma("sp", out=snap_d.ap()[k], in_=Sst[:], reads=["Sst"])
                        for h in range(8):
                            S.op("pe", "matmul", out=psU[:, h * 64:(h + 1) * 64], lhsT=k2[b2][:, h, :, :].rearrange("p a d -> p (a d)"), rhs=vb[b2][:, h * 64:(h + 1) * 64],
                                 start=True, stop=True, reads=[f"k2_{b2}a", f"k2_{b2}b", f"vb{b2}"], writes=["psU"])
                        S.op("dve", "scalar_tensor_tensor", out=tmpS[:], in0=psU[:, :], scalar=VALID[:, p:p + 1], in1=Sst[:], op0=ALU.mult, op1=ALU.add,
                             reads=["psU", "VALID", "Sst"], writes=["tmpS"])
                        S.op("pool", "tensor_tensor", out=Sst[:], in0=tmpS[:], in1=DC[:], op=ALU.mult, reads=["tmpS", "DC"], writes=["Sst"])

                    if p % BT == BT - 1:
                        bi = p // BT
                        NBc = 8 * BT
                        c4k = [f"CT4_{bi % 2}_{s}" for s in range(BT)]
                        for g in range(2):
                            bank, bkey = (psC, "psC") if g == 0 else (psU, "psU")
                            for kv in range(2):
                                for hl in range(2):
                                    col0 = (kv * 2 + hl) * NBc
                                    for l in range(16):
                                        S.op("pe", "matmul", out=bank[:, col0:col0 + NBc].rearrange("p (s x) -> p s x", s=BT), lhsT=W1[kv][64 * g:64 * g + 64, hl * 16 + l, :],
                                             rhs=c4[64 * g:64 * g + 64, :, kv, l::16], start=(l == 0), stop=(l == 15),
                                             reads=c4k + [f"W1_{kv}"], writes=[bkey])
                        for g in range(2):
                            bank, bkey = (psC, "psC") if g == 0 else (psU, "psU")
                            pc = bank[:, 0:4 * NBc].rearrange("p (kv hl n) -> p kv hl n", kv=2, hl=2)
                            S.op("dve", "tensor_copy", out=PRE[:, g::2, NBc * bi + 1:NBc * bi + NBc + 1], in_=pc[:, :, 0, :], reads=[bkey, "PRE"], writes=["PRE"])
                            S.op("dve", "tensor_tensor", out=PRE[:, g::2, NBc * bi:NBc * bi + NBc], in0=PRE[:, g::2, NBc * bi:NBc * bi + NBc], in1=pc[:, :, 1, :], op=ALU.add,
                                 reads=[bkey, "PRE"], writes=["PRE"])
                if smp is None:
                    S.dma("sp", out=o_ret.ap()[:, :], in_=Sst[:], reads=["Sst"])
                else:
                    S.op("pool", "memset", ap=KsT[:, NPG * 128:(NPG + 1) * 128], constant=0.0, writes=[f"KsT{NPG}"])
                    S.op("pool", "memset", ap=Vs[:, NPG, :, 0:64], constant=0.0, reads=["Vs"], writes=[f"Vs{NPG}"])
                    S.dma("pool", out=KsT[:, NPG * 128:NPG * 128 + 1], in_=skv_d.ap()[smp:smp + 1, 256:384].rearrange("o c -> c o"),
                          reads=[f"KsT{NPG}"], writes=[f"KsT{NPG}x"], allow_slow_non_contiguous=True)
                    S.dma("pool", out=Vs[0:1, NPG, :, 0:64], in_=skv_d.ap()[smp:smp + 1, 384:512].rearrange("o (g d) -> o g d", g=2),
                          reads=[f"Vs{NPG}"], writes=[f"Vs{NPG}x"])
                    for c in range(5):
                        b2 = c % 2
                        wt = xb[b2]
                        if c < 4:
                            S.dma("sp", out=wt[:, 0:256], in_=swin.ap()[smp, 128 * c:128 * (c + 1), :], writes=[f"xb{b2}"])
                            if c == 0:
                                S.dma("sp", out=o_swin.ap()[smp, 0:127, :], in_=wt[1:128, 0:256], reads=[f"xb{b2}"])
                            else:
                                S.dma("sp", out=o_swin.ap()[smp, 128 * c - 1:128 * c + 127, :], in_=wt[:, 0:256], reads=[f"xb{b2}"])
                        else:
                            S.op("dve", "memset", ap=wt[:, 0:256], constant=0.0, writes=[f"xb{b2}"])
                            S.dma("sp", out=wt[0:1, 0:256], in_=skv_d.ap()[smp:smp + 1, 512:768], reads=[f"xb{b2}"], writes=[f"xb{b2}r"])
                            S.dma("sp", out=o_swin.ap()[smp, 511:512, :], in_=wt[0:1, 0:256], reads=[f"xb{b2}r"])
                        S.op("dve", "tensor_copy", out=kvb[b2][:, 512:768], in_=wt[:, 0:256], reads=[f"xb{b2}", f"xb{b2}r"], writes=[f"kvb{b2}b"])
                        S.op("pe", "transpose", out=psT2[:, 384:512], in_=kvb[b2][:, 512:640], identity=ident_b[:], reads=[f"kvb{b2}b", "ident_b"], writes=["psT2"])
                        S.op("act", "copy", out=kwt[b2][:], in_=psT2[:, 384:512], reads=["psT2"], writes=[f"kwt{b2}"])
                        S.dma("sp", out=kw_d.ap()[NWIN + c], in_=kwt[b2][:], reads=[f"kwt{b2}"])
                        S.dma("sp", out=vw_d.ap()[NWIN + c], in_=kvb[b2][:, 640:768], reads=[f"kvb{b2}b"])

                S.flush()
                ws.close()
                NBK = NSL * 8
                NCBx = NCB if smp is None else NCB_S
                pet = SB(ph, "pet", [128, 2], F32)
                u_ = SB(ph, "u_", [128, NBK], F32)
                w_ = SB(ph, "w_", [128, NBK], F32)
                hid = SB(ph, "hid", [128, 4, NCBx * 128], BF16)
                for kv in range(2):
                    for l in range(32):
                        S.op("pe", "matmul", out=psC[:, 300 + kv:301 + kv], lhsT=W1[kv][0:64, l, :], rhs=peT[0:64, kv, l:l + 1], start=(l == 0), stop=(l == 31),
                             reads=[f"W1_{kv}", f"peT{kv}"], writes=["psC"])
                S.op("dve", "tensor_copy", out=pet[:], in_=psC[:, 300:302], reads=["psC"], writes=["pet"])
                S.op("pool", "memset", ap=hid[:], constant=0.0, writes=["hid"])
                for q in range(4):
                    S.op("dve", "tensor_scalar", out=u_[:], in0=PRE[:, q, 1:NBK + 1], scalar1=pet[:, q // 2:q // 2 + 1], scalar2=None, op0=ALU.add,
                         reads=["PRE", "pet"], writes=["u_"])
                    S.op("pool", "tensor_tensor", out=w_[:], in0=u_[:], in1=u_[:], op=ALU.mult, reads=["u_"], writes=["w_"])
                    S.op("dve", "tensor_scalar", out=w_[:], in0=w_[:], scalar1=0.044715, scalar2=1.0, op0=ALU.mult, op1=ALU.add, reads=["w_"], writes=["w_"])
                    S.op("pool", "tensor_tensor", out=w_[:], in0=w_[:], in1=u_[:], op=ALU.mult, reads=["u_", "w_"], writes=["w_"])
                    S.op("act", "activation", out=w_[:], in_=w_[:], func=AF.Sigmoid, scale=1.5957691216057308, reads=["w_"], writes=["w_"])
                    S.op("dve", "tensor_tensor", out=hid[:, q, 0:NBK], in0=w_[:], in1=u_[:], op=ALU.mult, reads=["u_", "w_", "hid"], writes=["hid"])
                for cb in range(NCBx):
                    for g in range(2):
                        S.op("pe", "matmul", out=psZ[0][:, 0:128], lhsT=W2[:, 0, g, :], rhs=hid[:, 0 * 2 + g, cb * 128:(cb + 1) * 128], start=(g == 0), stop=(g == 1),
                             reads=["hid", "W2_00", "W2_01"], writes=["psZ0"])
                    S.op("act", "copy", out=kcmpT[:, cb * 128:(cb + 1) * 128], in_=psZ[0][:, 0:128], reads=["psZ0"], writes=["kcmpT"])
                    for g in range(2):
                        S.op("pe", "matmul", out=psZ[1][:, 64 * g:64 * g + 64], lhsT=hid[:, 2 + g, cb * 128:(cb + 1) * 128], rhs=W2[:, 1, g, 64 * g:64 * g + 64],
                             start=True, stop=True, reads=["hid", "W2_10", "W2_11"], writes=["psZ1"])
                    S.op("dve", "tensor_copy", out=vcmp[:, cb, :, 0:64], in_=psZ[1][:, 0:128].rearrange("p (g d) -> p g d", g=2), reads=["psZ1", "vcmp"],
                         writes=[f"vcmp{cb}"])
                if debug and smp is None:
                    dk = SB(ph, "dk", [128, NCB * 128], F32)
                    dv = SB(ph, "dv", [128, NCB * 130], F32)
                    S.op("dve", "tensor_copy", out=dk[:], in_=kcmpT[:, 0:NCB * 128], reads=["kcmpT"], writes=["dk"])
                    S.op("dve", "tensor_copy", out=dv[:], in_=vcmp[:, 0:NCB].rearrange("p c g d -> p (c g d)"), reads=[f"vcmp{cb}" for cb in range(NCB)], writes=["dv"])
                    S.dma("sp", out=dbg["kcmpT"].ap()[:, :], in_=dk[:], reads=["dk"])
                    S.dma("sp", out=dbg["vcmp"].ap()[:, :], in_=dv[:], reads=["dv"])
                S.flush()


        phase_1a(None)
        if "1b" not in phases:
            return nc

        with ExitStack() as ph:
            Wq = SB(ph, "Wq", [128, KC, W_Q_COLS], BF16)
            GQf = SB(ph, "GQf", [128, 512], F32)
            QTz = [SB(ph, f"QTq{g}", [128, 4, 128], BF16) for g in range(2)]
            xq = SB(ph, "xq", [128, D], F32)
            t1 = SB(ph, "t1b", [128, D], F32)
            st4 = SB(ph, "st4b", [128, 4], F32)
            hnq = SB(ph, "hnq", [128, D], BF16)
            hnqT = SB(ph, "hnqT", [128, KC, 128], BF16)
            gates = SB(ph, "gatesq", [128, 24], F32)
            csq = SB(ph, "csq", [128, 2, 32], F32)
            rt = [SB(ph, f"rtb{i}", [128, 8, 32], F32) for i in range(4)]
            qro = SB(ph, "qro", [128, 8, 2, 32], F32)
            qt_ = SB(ph, "qt_", [128, 512], BF16)
            qT = SB(ph, "qTq", [128, 4, 128], BF16)
            sg = SB(ph, "sgq", [128, 512], F32)
            Wkv = SB(ph, "Wkvq", [128, KC, W_KV_COLS], BF16)
            A1s = SB(ph, "A1s", [NSMP, D], F32)
            SH1s = SB(ph, "SH1s", [NSMP, D], F32)
            GQ0 = SB(ph, "GQ0", [NSMP, 512], F32)
            GK0 = SB(ph, "GK0", [NSMP, 512], F32)
            kvf_s = SB(ph, "kvf_s", [NSMP, 768], F32)
            k2s = SB(ph, "k2s", [NSMP, 8, 2, 64], BF16)
            vs_ = SB(ph, "vs_", [NSMP, 512], BF16)
            psT = PS(ph, "psTq", [128, 1024], BF16)
            psQ = PS(ph, "psQq", [128, 512], F32)
            psZa = PS(ph, "psZaq", [128, 512], F32)
            psZb = PS(ph, "psZbq", [128, 512], F32)
            psX = PS(ph, "psXq", [128, 512], F32)
            for kc in range(KC):
                S.dma("pool", out=Wq[:, kc, :], in_=w_q.ap()[kc * 128:(kc + 1) * 128, :], writes=["Wq"])
            S.dma("sp", out=GQf[:], in_=cn["c_gq"].ap()[:, :], writes=["GQf"])
            GQh = GQf
            items = [(32 * k + 27 + j, 5 * k + j, 128, None) for k in range(KSEG) for j in range(5)]
            if "s" in phases:
                items.append((None, TS, NSMP, "s"))
                for kc in range(KC):
                    S.dma("pool", out=Wkv[:, kc, :], in_=w_kv.ap()[kc * 128:(kc + 1) * 128, :], writes=["Wkvq"])
                S.dma("sp", out=A1s[:], in_=modv_d.ap()[0:NSMP, 0:D], writes=["A1s"])
                S.dma("sp", out=SH1s[:], in_=modv_d.ap()[0:NSMP, D:2 * D], writes=["SH1s"])
                S.dma("sp", out=GQ0[:], in_=cn["c_gq"].ap()[0:1, :].partition_broadcast(NSMP), writes=["GQ0"])
                S.dma("sp", out=GK0[:], in_=cn["c_gk"].ap()[0:1, :].partition_broadcast(NSMP), writes=["GK0"])
            for (p, ti, T, kind) in items:
                if True:
                    tq0, j = 0, 1
                    N = 4 * T
                    if kind is None:
                        x_src = xp.ap()[p * 128:p * 128 + T, :]
                        cos_src = cn["c_cos"].ap()[p * 128:p * 128 + T, :]
                        sin_src = cn["c_sin"].ap()[p * 128:p * 128 + T, :]
                        A_t, SH_t, GQ_t = A1bc, SH1bc, GQf
                    else:
                        x_src = xs.ap()[:, :]
                        cos_src = cn["c_cs_s"].ap()[0:1, :].partition_broadcast(T)
                        sin_src = cn["c_cs_s"].ap()[1:2, :].partition_broadcast(T)
                        A_t, SH_t, GQ_t = A1s, SH1s, GQ0
                    S.dma("sp", out=xq[0:T, :], in_=x_src, writes=["xq"])
                    S.dma("act", out=csq[0:T, 0, :], in_=cos_src, writes=["csq0"])
                    S.dma("act", out=csq[0:T, 1, :], in_=sin_src, writes=["csq1"])
                    S.op("act", "activation", out=t1[0:T, :], in_=xq[0:T, :], func=AF.Square, accum_out=st4[0:T, 0:1], reads=["xq"], writes=["t1", "st4"])
                    S.op("dve", "tensor_scalar", out=st4[0:T, 1:2], in0=st4[0:T, 0:1], scalar1=1.0 / D, scalar2=RMS_EPS, op0=ALU.mult, op1=ALU.add,
                         reads=["st4"], writes=["st4"])
                    S.op("act", "activation", out=st4[0:T, 2:3], in_=st4[0:T, 1:2], func=AF.Sqrt, reads=["st4"], writes=["st4"])
                    S.op("dve", "reciprocal", out=st4[0:T, 3:4], in_=st4[0:T, 2:3], reads=["st4"], writes=["st4"])
                    S.op("dve", "scalar_tensor_tensor", out=t1[0:T, :], in0=xq[0:T, :], scalar=st4[0:T, 3:4], in1=A_t[0:T, :], op0=ALU.mult, op1=ALU.mult,
                         reads=["xq", "st4", "A1bc", "A1s"], writes=["t1"])
                    S.op("pool", "tensor_tensor", out=hnq[0:T, :], in0=t1[0:T, :], in1=SH_t[0:T, :], op=ALU.add, reads=["t1", "SH1bc", "SH1s"], writes=["hnq"])
                    for kc in range(KC):
                        S.op("pe", "transpose", out=psT[:, kc * 128:kc * 128 + T], in_=hnq[0:T, kc * 128:(kc + 1) * 128], identity=ident_b[0:T, 0:T],
                             reads=["hnq", "ident_b"], writes=["psT"])
                    S.op("act", "copy", out=hnqT[:, :, 0:T], in_=psT[:, :].rearrange("p (k t) -> p k t", k=KC)[:, :, 0:T], reads=["psT"], writes=["hnqT"])
                    for c in range(4):
                        for kc in range(KC):
                            S.op("pe", "matmul", out=psQ[:, c * T:(c + 1) * T], lhsT=Wq[:, kc, c * 128:(c + 1) * 128], rhs=hnqT[:, kc, 0:T],
                                 start=(kc == 0), stop=(kc == KC - 1), reads=["Wq", "hnqT"], writes=["psQ"])
                    for g in range(2):
                        S.op("act", "activation", out=QTz[g][64 * g:64 * g + 64, :, 0:T], in_=psQ[64 * g:64 * g + 64, 0:N].rearrange("p (c t) -> p c t", c=4),
                             func=AF.Copy, scale=0.125, reads=["psQ", f"QTz{g}"], writes=[f"QTz{g}"])
                    for (bank, bkey, c0, w) in ((psX, "psX", 512, 24), (psZa, "psZa", 536, 512), (psZb, "psZb", 1048, 512)):
                        for kc in range(KC):
                            S.op("pe", "matmul", out=bank[0:T, 0:w], lhsT=hnqT[:, kc, 0:T], rhs=Wq[:, kc, c0:c0 + w], start=(kc == 0), stop=(kc == KC - 1),
                                 reads=["Wq", "hnqT"], writes=[bkey])
                    S.op("act", "activation", out=gates[0:T, :], in_=psX[0:T, 0:24], func=AF.Sigmoid, reads=["psX"], writes=["gates"])
                    S.op("act", "activation", out=sg[0:T, :], in_=psZb[0:T, :], func=AF.Silu, reads=["psZb"], writes=["sg"])
                    qv = psZa[0:T, :].rearrange("p (h two i) -> p h two i", h=8, two=2)
                    cs_ = csq[0:T, 0, :].unsqueeze(1).to_broadcast([T, 8, 32])
                    sn_ = csq[0:T, 1, :].unsqueeze(1).to_broadcast([T, 8, 32])
                    S.op("dve", "tensor_tensor", out=rt[0][0:T], in0=qv[:, :, 0, :], in1=cs_, op=ALU.mult, reads=["psZa", "csq0"], writes=["rt0"])
                    S.op("dve", "tensor_tensor", out=rt[1][0:T], in0=qv[:, :, 1, :], in1=sn_, op=ALU.mult, reads=["psZa", "csq1"], writes=["rt1"])
                    S.op("dve", "tensor_tensor", out=rt[2][0:T], in0=qv[:, :, 0, :], in1=sn_, op=ALU.mult, reads=["psZa", "csq1"], writes=["rt2"])
                    S.op("dve", "tensor_tensor", out=rt[3][0:T], in0=qv[:, :, 1, :], in1=cs_, op=ALU.mult, reads=["psZa", "csq0"], writes=["rt3"])
                    S.op("pool", "tensor_tensor", out=qro[0:T, :, 0, :], in0=rt[0][0:T], in1=rt[1][0:T], op=ALU.subtract, reads=["rt0", "rt1"], writes=["qro0"])
                    S.op("pool", "tensor_tensor", out=qro[0:T, :, 1, :], in0=rt[2][0:T], in1=rt[3][0:T], op=ALU.add, reads=["rt2", "rt3"], writes=["qro1"])
                    S.op("pool", "tensor_tensor", out=qt_[0:T, :], in0=qro[0:T].rearrange("p h a i -> p (h a i)"), in1=GQ_t[0:T, :], op=ALU.mult,
                         reads=["qro0", "qro1", "GQf", "GQ0"], writes=["qt_"])
                    for c in range(4):
                        S.op("pe", "transpose", out=psT[:, c * 128:c * 128 + T], in_=qt_[0:T, c * 128:(c + 1) * 128], identity=ident_b[0:T, 0:T],
                             reads=["qt_", "ident_b"], writes=["psT"])
                    S.op("act", "copy", out=qT[:, :, 0:T], in_=psT[:, 0:512].rearrange("p (c t) -> p c t", c=4)[:, :, 0:T], reads=["psT"], writes=["qT"])
                    S.dma("sp", out=qt_d.ap()[ti, 0:64, :].rearrange("p (c t) -> p c t", c=4), in_=QTz[0][0:64, :, :], reads=["QTz0"], writes=[f"qt_d{ti}"])
                    S.dma("sp", out=qt_d.ap()[ti, 64:128, :].rearrange("p (c t) -> p c t", c=4), in_=QTz[1][64:128, :, :], reads=["QTz1"], writes=[f"qt_d{ti}"])
                    S.dma("sp", out=qrt_d.ap()[ti].rearrange("p (c t) -> p c t", c=4), in_=qT[:, :, :], reads=["qT"], writes=[f"qrt_d{ti}"])
                    S.dma("sp", out=gates_d.ap()[ti], in_=gates[:, :], reads=["gates"], writes=[f"gates_d{ti}"])
                    S.dma("sp", out=sg_d.ap()[ti], in_=sg[:, :], reads=["sg"], writes=[f"sg_d{ti}"])
                    if kind == "s":
                        for (bank, bkey, c0, w) in ((psZa, "psZa", 0, 512), (psZb, "psZb", 512, 256)):
                            for kc in range(KC):
                                S.op("pe", "matmul", out=bank[0:T, 0:w], lhsT=hnqT[:, kc, 0:T], rhs=Wkv[:, kc, c0:c0 + w], start=(kc == 0), stop=(kc == KC - 1),
                                     reads=["Wkvq", "hnqT"], writes=[bkey])
                        S.op("dve", "tensor_copy", out=kvf_s[0:T, 0:512], in_=psZa[0:T, 0:512], reads=["psZa"], writes=["kvf_sa"])
                        S.op("dve", "tensor_copy", out=kvf_s[0:T, 512:768], in_=psZb[0:T, 0:256], reads=["psZb"], writes=["kvf_sb"])
                        S.dma("sp", out=o_skv.ap()[:, :], in_=kvf_s[:, :], reads=["kvf_sa", "kvf_sb"])
                        S.dma("sp", out=skv_d.ap()[:, :], in_=kvf_s[:, :], reads=["kvf_sa", "kvf_sb"], writes=["skv_d"])
                        for (bank, bkey, c0, w) in ((psZa, "psZa", 768, 512), (psZb, "psZb", 1280, 512)):
                            for kc in range(KC):
                                S.op("pe", "matmul", out=bank[0:T, 0:w], lhsT=hnqT[:, kc, 0:T], rhs=Wkv[:, kc, c0:c0 + w], start=(kc == 0), stop=(kc == KC - 1),
                                     reads=["Wkvq", "hnqT"], writes=[bkey])
                        S.op("act", "copy", out=vs_[0:T, :], in_=psZb[0:T, :], reads=["psZb"], writes=["vs_"])
                        kq = psZa[0:T, :].rearrange("p (h two i) -> p h two i", h=8, two=2)
                        S.op("dve", "tensor_tensor", out=rt[0][0:T], in0=kq[:, :, 0, :], in1=cs_, op=ALU.mult, reads=["psZa", "csq0"], writes=["rt0"])
                        S.op("dve", "tensor_tensor", out=rt[1][0:T], in0=kq[:, :, 1, :], in1=sn_, op=ALU.mult, reads=["psZa", "csq1"], writes=["rt1"])
                        S.op("dve", "tensor_tensor", out=rt[2][0:T], in0=kq[:, :, 0, :], in1=sn_, op=ALU.mult, reads=["psZa", "csq1"], writes=["rt2"])
                        S.op("dve", "tensor_tensor", out=rt[3][0:T], in0=kq[:, :, 1, :], in1=cs_, op=ALU.mult, reads=["psZa", "csq0"], writes=["rt3"])
                        S.op("pool", "tensor_tensor", out=qro[0:T, :, 0, :], in0=rt[0][0:T], in1=rt[1][0:T], op=ALU.subtract, reads=["rt0", "rt1"], writes=["qro0"])
                        S.op("pool", "tensor_tensor", out=qro[0:T, :, 1, :], in0=rt[2][0:T], in1=rt[3][0:T], op=ALU.add, reads=["rt2", "rt3"], writes=["qro1"])
                        gkv = GK0[0:T, :].rearrange("p (h d) -> p h d", h=8)
                        krov = qro[0:T].rearrange("p h two i -> p h (two i)")
                        S.op("pool", "tensor_tensor", out=k2s[0:T, :, 0, :], in0=krov, in1=gkv, op=ALU.mult, reads=["qro0", "qro1", "GK0"], writes=["k2sa"])
                        S.op("dve", "tensor_tensor", out=k2s[0:T, :, 1, :], in0=krov, in1=gkv, op=ALU.mult, reads=["qro0", "qro1", "GK0"], writes=["k2sb"])
                        S.dma("sp", out=krv_d.ap()[TS, 0:T, 0:1024], in_=k2s[0:T].rearrange("p h a d -> p (h a d)"), reads=["k2sa", "k2sb"], writes=["krvs"])
                        S.dma("sp", out=krv_d.ap()[TS, 0:T, 1024:1536], in_=vs_[0:T, :], reads=["vs_"], writes=["krvs2"])
            S.flush()
        def phase_1b(smp):
            NBv = NB if smp is None else NB_S
            NPv = NBv // 128
            NCBv = NCB if smp is None else NCB_S
            nmc_s = (8 * NPG - 2) // 128 + 1
            with ExitStack() as ph:
                IND = SB(ph, "IND", [128, 8192], BF16)
                GS = SB(ph, "GS", [128, 8, 8, 128], BF16)
                GW = SB(ph, "GW", [128, 5, 8, 128], BF16)
                if smp is None:
                    rc_list = sorted({(32 * k + 27 + j) - 16 * m for k in range(KSEG) for j in range(5)
                                      for m in range(((8 * (32 * k + 27 + j) + 7) + 127) // 128) if (32 * k + 27 + j) - 16 * m < 23})
                else:
                    rc_list = sorted({NPG - 16 * m for m in range(nmc_s) if NPG - 16 * m < 23})
                rc_idx = {r: i for i, r in enumerate(rc_list)}
                GC = SB(ph, "GC", [128, max(len(rc_list), 1), 8, 128], BF16)
                CHT = SB(ph, "CHT", [128, 8, 128], BF16)
                CH8 = SB(ph, "CH8", [128, 8], F32)
                AM = SB(ph, "AM", [128, NCBv, NBv], BF16)
                TRI8 = SB(ph, "TRI8", [128, 4, 128], BF16)
                GNG = SB(ph, "GNG", [128, 512], F32)
                PMK = SB(ph, "PMK", [128, 512], F32)
                CMPB = SB(ph, "CMPB", [128, NCB], F32)
                WINB = SB(ph, "WINB", [128, NWIN + 5], F32)
                VALID = SB(ph, "VALIDb", [128, max(NS, NPG + 1)], F32)
                DC = SB(ph, "DCb", [128, 512], F32)
                Sst = SB(ph, "Sstb", [128, 512], F32)
                Sblk = SB(ph, "Sblk", [128, 512], BF16)
                QTz = [SB(ph, f"QTz{g}", [128, 4, 128], BF16) for g in range(2)]
                MST = SB(ph, "MST", [128, NPv, 4, 128], BF16)
                PTc = [SB(ph, f"PTc{i}", [128, 4, 128], BF16) for i in range(NCBv)]
                PT = [SB(ph, f"PT{i}", [128, 4, 128], BF16) for i in range(3)]
                PTm = [SB(ph, f"PTm{i}", [128, 4, 128], BF16) for i in range(2)]
                KW = SB(ph, "KW", [128, 5, 128], BF16)
                VW = SB(ph, "VW", [128, 5, 2, 65], BF16)
                gates = SB(ph, "gates", [128, 24], F32)
                qT = SB(ph, "qT", [128, 4, 128], BF16)
                sg = SB(ph, "sg", [128, 512], F32)
                k2 = SB(ph, "k2b", [128, 8, 2, 64], BF16)
                vbk = SB(ph, "vbk", [128, 512], BF16)
                kT = SB(ph, "kT", [128, 8, 128], BF16)
                attm = SB(ph, "attm", [128, 8, 128], BF16)
                sq_ = SB(ph, "sq_", [128, 512], F32)
                gst = SB(ph, "gst", [128, 8, 6], F32)
                on_ = SB(ph, "on_", [128, 512], F32)
                oret = SB(ph, "oret", [128, 512], BF16)
                tmpS = SB(ph, "tmpSb", [128, 512], F32)
                OT = SB(ph, "OT", [65, 512], F32)
                Ob = [SB(ph, f"Ob{i}", [128, 4, 65], F32) for i in range(3)]
                coef = SB(ph, "coef", [128, 3, 4], F32)
                rz = SB(ph, "rz", [128, 3, 4], F32)
                impa = SB(ph, "impa", [128, NBv], F32)
                FS = SB(ph, "FS", [128, NBv], F32)
                wk = SB(ph, "wk", [128, NBv], F32)
                m8 = SB(ph, "m8", [128, 16], F32)
                Mneg = SB(ph, "Mneg", [128, NBv], BF16)
                onsa = SB(ph, "onsa", [128, 8, 64], F32)
                onsb = SB(ph, "onsb", [128, 512], BF16)
                omT = SB(ph, "omT", [128, 8, 128], BF16)
                tabx = SB(ph, "tabx", [33, 8], F32)
                ohb = sq_
                bvr = on_
                psT = PS(ph, "psTb", [128, 1024], BF16)
                psQ = PS(ph, "psQ", [128, 512], F32)
                psZa = PS(ph, "psZa", [128, 512], F32)
                psZb = PS(ph, "psZb", [128, 512], F32)
                psS = [PS(ph, f"psS{i}", [128, 512], F32) for i in range(2)]
                psO = PS(ph, "psO", [128, 512], F32)
                psX = PS(ph, "psX", [128, 512], F32)

                for i in range(4):
                    S.dma("pool", out=IND[:, i * 2048:(i + 1) * 2048], in_=cn["c_ind"].ap()[:, i * 2048:(i + 1) * 2048], writes=["IND"])
                for cb in range(NCBv):
                    S.dma("pool", out=AM[:, cb, :], in_=cn["c_amat" if smp is None else "c_amat_s"].ap()[:, cb, :], writes=["AM"])
                S.dma("pool", out=TRI8[:].rearrange("p h n -> p (h n)"), in_=cn["c_tri8"].ap()[:, 0:512], writes=["TRI8"])
                S.dma("sp", out=GNG[:], in_=gng.ap()[0:1, :].partition_broadcast(128), writes=["GNG"])
                S.dma("sp", out=PMK[:], in_=cn["c_pmk"].ap()[:, :], writes=["PMK"])
                if smp is None:
                    S.dma("sp", out=CMPB[:], in_=cn["c_cmpb"].ap()[:, :], writes=["CMPB"])
                    S.dma("sp", out=WINB[:, 0:NWIN], in_=cn["c_winb"].ap()[:, :], writes=["WINB"])
                    S.dma("sp", out=VALID[:, 0:NS], in_=cn["c_valid"].ap()[:, :], writes=["VALID"])
                    S.dma("sp", out=DC[:], in_=cn["c_dc"].ap()[:, :], writes=["DC"])
                else:
                    S.op("dve", "memset", ap=CMPB[:], constant=0.0, writes=["CMPB"])
                    S.op("dve", "memset", ap=WINB[:], constant=0.0, writes=["WINB"])
                    S.op("dve", "memset", ap=VALID[:], constant=1.0, writes=["VALID"])
                    S.dma("sp", out=DC[:], in_=cn["c_dc1"].ap()[:, :], writes=["DC"])
                S.dma("sp", out=CH8[:], in_=table.ap()[31:32, :].partition_broadcast(128), writes=["CH8"])
                S.op("dve", "tensor_copy", out=CHT[:], in_=CH8[:].unsqueeze(2).to_broadcast([128, 8, 128]), reads=["CH8"], writes=["CHT"])
                for g in range(2):
                    S.op("pool", "memset", ap=QTz[g][:], constant=0.0, writes=[f"QTz{g}"])
                S.op("pool", "memset", ap=VW[:], constant=1.0, writes=["VW"])
                S.op("dve", "memset", ap=tabx[:], constant=-BIG, writes=["tabx"])
                S.dma("sp", out=tabx[0:32, :], in_=table.ap()[:, :], reads=["tabx"], writes=["tabx"])
                for (ohd, bvd, xlen) in ((cn["c_ohs"], bvs_d, XS_LEN), (cn["c_ohw"], bvw_d, XW_LEN), (cn["c_ohc"], bvc_d, XC_LEN)):
                    for x0 in range(0, xlen, 512):
                        w = min(512, xlen - x0)
                        S.dma("sp", out=ohb[0:33, 0:w], in_=ohd.ap()[:, x0:x0 + w], writes=["ohb"])
                        S.op("pe", "matmul", out=psQ[0:8, 0:w], lhsT=tabx[:, :], rhs=ohb[0:33, 0:w], start=True, stop=True, reads=["tabx", "ohb"], writes=["psQ"])
                        S.op("dve", "tensor_copy", out=bvr[0:8, 0:w], in_=psQ[0:8, 0:w], reads=["psQ"], writes=["bvr"])
                        S.dma("sp", out=bvd.ap()[:, x0:x0 + w], in_=bvr[0:8, 0:w], reads=["bvr"], writes=["bv_d"])
                for h in range(8):
                    S.dma("pool", out=GS[:, :, h, :], in_=bass.AP(bvs_d, h * XS_LEN, [[1, 128], [128, 8], [1, 128]]), reads=["bv_d"], writes=["GS"])
                    S.dma("pool", out=GW[:, :, h, :], in_=bass.AP(bvw_d, h * XW_LEN, [[1, 128], [128, 5], [1, 128]]), reads=["bv_d"], writes=["GW"])
                    for r_, i_ in rc_idx.items():
                        S.dma("pool", out=GC[:, i_, h, :], in_=bass.AP(bvc_d, h * XC_LEN + 128 * r_, [[16, 128], [1, 128]]), reads=["bv_d"], writes=["GC"])

                if smp is None:
                    tiles = [(k, j, 32 * k + 27 + j, 5 * k + j) + ((2, 126, 126) if j == 0 else (128, 0, 0)) for k in range(KSEG) for j in range(5)]
                else:
                    tiles = [(0, 1, NPG, TS, 1, 0, smp)]
                for (k, j, p, ti, T, tq0, tsr) in tiles:
                    if smp is None and j == 0:
                        S.dma("sp", out=Sst[:], in_=snap_d.ap()[k], writes=["Sst"])
                    if smp is not None:
                        for hf in range(2):
                            S.dma("sp", out=Sst[64 * hf:64 * hf + 64, :].rearrange("p (h e) -> p h e", h=8), in_=sret.ap()[smp].rearrange("h d e -> d h e"), writes=[f"Sst_{hf}"])
                    if True:
                        N = 4 * T
                        if smp is None:
                            S.dma("sp", out=FS[0:T, :], in_=cn["c_fsel"].ap()[ti, tq0:tq0 + T, :], writes=["FS"])
                            S.dma("sp", out=k2[:].rearrange("p h a d -> p (h a d)"), in_=krv_d.ap()[ti, :, 0:1024], writes=["k2"])
                            S.dma("sp", out=vbk[:], in_=krv_d.ap()[ti, :, 1024:1536], writes=["vbk"])
                            wi0 = cfg.win_index(p - 4)
                        else:
                            S.dma("sp", out=FS[0:T, :], in_=cn["c_fsel_s"].ap()[0:1, :], writes=["FS"])
                            S.op("pool", "memset", ap=k2[:], constant=0.0, writes=["k2"])
                            S.op("pool", "memset", ap=vbk[:], constant=0.0, writes=["vbk"])
                            S.dma("sp", out=k2[0:1].rearrange("p h a d -> p (h a d)"), in_=krv_d.ap()[TS, smp:smp + 1, 0:1024], reads=["k2"], writes=["k2r"])
                            S.dma("sp", out=vbk[0:1, :], in_=krv_d.ap()[TS, smp:smp + 1, 1024:1536], reads=["vbk"], writes=["vbkr"])
                            wi0 = NWIN
                        S.dma("sp", out=KW[:], in_=kw_d.ap()[wi0:wi0 + 5].rearrange("w p c -> p w c"), writes=["KW"])
                        for g_ in range(2):
                            S.dma("sp", out=VW[:, :, g_, 0:64], in_=vw_d.ap()[wi0:wi0 + 5, :, 64 * g_:64 * g_ + 64].rearrange("w p d -> p w d"), reads=["VW"], writes=[f"VWd{g_}"])
                        for g_ in range(2):
                            S.dma("sp", out=QTz[g_][64 * g_:64 * g_ + 64, :, 0:T], in_=qt_d.ap()[ti, 64 * g_:64 * g_ + 64, :].rearrange("p (c t) -> p c t", c=4)[:, :, tsr:tsr + T],
                                  reads=[f"QTz{g_}"], writes=[f"QTz{g_}"], allow_slow_non_contiguous=True)
                        S.dma("sp", out=qT[:, :, 0:T], in_=qrt_d.ap()[ti].rearrange("p (c t) -> p c t", c=4)[:, :, tsr:tsr + T], writes=["qT"], allow_slow_non_contiguous=True)
                        S.dma("sp", out=gates[0:T, :], in_=gates_d.ap()[ti, tsr:tsr + T, :], writes=["gates"])
                        S.dma("sp", out=sg[0:T, :], in_=sg_d.ap()[ti, tsr:tsr + T, :], writes=["sg"])
                        for h in range(8):
                            S.op("pe", "transpose", out=psT[:, h * 128:(h + 1) * 128], in_=k2[:, h, :, :].rearrange("p a d -> p (a d)"), identity=ident_b[:],
                                 reads=["k2", "k2r", "ident_b"], writes=["psT"])
                        S.op("act", "copy", out=kT[:].rearrange("p h m -> p (h m)"), in_=psT[:, :], reads=["psT"], writes=["kT"])
                        for h in range(8):
                            b = h % 2
                            bank, bkey = (psZa, "psZa") if b == 0 else (psZb, "psZb")
                            S.op("pe", "matmul", out=bank[:, (h // 2) * T:(h // 2 + 1) * T], lhsT=kT[64 * b:64 * b + 64, h, :], rhs=qT[64 * b:64 * b + 64, h // 2, 0:T],
                                 start=True, stop=True, reads=["kT", "qT"], writes=[bkey])
                        for b in range(2):
                            bank, bkey = (psZa, "psZa") if b == 0 else (psZb, "psZb")
                            S.op("dve", "tensor_tensor", out=attm[:, b::2, 0:T], in0=bank[:, 0:N].rearrange("p (c t) -> p c t", c=4), in1=TRI8[:, 0:4, tq0:tq0 + T],
                                 op=ALU.mult, reads=[bkey, "TRI8"], writes=[f"attm{b}"])
                        S.op("pool", "tensor_tensor", out=Sblk[:], in0=Sst[:], in1=PMK[:], op=ALU.mult, reads=["Sst", "Sst_0", "Sst_1", "PMK"], writes=["Sblk"])
                        for h in range(8):
                            S.op("pe", "matmul", out=psX[0:T, h * 64:(h + 1) * 64], lhsT=attm[:, h, 0:T], rhs=vbk[:, h * 64:(h + 1) * 64], start=True, stop=False,
                                 reads=["attm0", "attm1", "vbk", "vbkr"], writes=["psX"])
                            S.op("pe", "matmul", out=psX[0:T, h * 64:(h + 1) * 64], lhsT=qT[:, h // 2, 0:T], rhs=Sblk[:, h * 64:(h + 1) * 64], start=False, stop=True,
                                 reads=["qT", "Sblk"], writes=["psX"])
                        S.op("dve", "tensor_reduce", out=gst[0:T, :, 0], in_=psX[0:T, :].rearrange("p (h e) -> p h e", h=8), axis=AX.X, op=ALU.add, reads=["psX"], writes=["gst0"])
                        S.op("act", "activation", out=sq_[0:T, :], in_=psX[0:T, :], func=AF.Square, reads=["psX"], writes=["sq_"])
                        S.op("dve", "tensor_reduce", out=gst[0:T, :, 1], in_=sq_[0:T, :].rearrange("p (h e) -> p h e", h=8), axis=AX.X, op=ALU.add, reads=["sq_"], writes=["gst1"])
                        S.op("dve", "tensor_scalar", out=gst[0:T, :, 2], in0=gst[0:T, :, 0], scalar1=1.0 / 64, scalar2=None, op0=ALU.mult, reads=["gst0"], writes=["gst2"])
                        S.op("dve", "tensor_tensor", out=gst[0:T, :, 3], in0=gst[0:T, :, 2], in1=gst[0:T, :, 2], op=ALU.mult, reads=["gst2"], writes=["gst3"])
                        S.op("dve", "scalar_tensor_tensor", out=gst[0:T, :, 4], in0=gst[0:T, :, 1], scalar=1.0 / 64, in1=gst[0:T, :, 3], op0=ALU.mult, op1=ALU.subtract,
                             reads=["gst1", "gst3"], writes=["gst4"])
                        S.op("dve", "tensor_scalar", out=gst[0:T, :, 4], in0=gst[0:T, :, 4], scalar1=GN_EPS, scalar2=None, op0=ALU.add, reads=["gst4"], writes=["gst4"])
                        S.op("act", "activation", out=gst[0:T, :, 5], in_=gst[0:T, :, 4], func=AF.Sqrt, reads=["gst4"], writes=["gst5"])
                        S.op("dve", "reciprocal", out=gst[0:T, :, 5], in_=gst[0:T, :, 5], reads=["gst5"], writes=["gst5"])
                        for h in range(8):
                            S.op("dve", "tensor_scalar", out=on_[0:T, h * 64:(h + 1) * 64], in0=psX[0:T, h * 64:(h + 1) * 64], scalar1=gst[0:T, h, 2:3], scalar2=gst[0:T, h, 5:6],
                                 op0=ALU.subtract, op1=ALU.mult, reads=["psX", "gst2", "gst5"], writes=["on_"])
                        S.op("pool", "tensor_tensor", out=on_[0:T, :], in0=on_[0:T, :], in1=GNG[0:T, :], op=ALU.mult, reads=["on_", "GNG"], writes=["on_"])
                        S.op("pool", "tensor_tensor", out=oret[0:T, :], in0=on_[0:T, :], in1=sg[0:T, :], op=ALU.mult, reads=["on_", "sg"], writes=["oret"])
                        if j < 4:
                            for h in range(8):
                                S.op("pe", "matmul", out=psQ[:, h * 64:(h + 1) * 64], lhsT=k2[:, h, :, :].rearrange("p a d -> p (a d)"), rhs=vbk[:, h * 64:(h + 1) * 64],
                                     start=True, stop=True, reads=["k2", "k2r", "vbk", "vbkr"], writes=["psQ"])
                            S.op("dve", "scalar_tensor_tensor", out=tmpS[:], in0=psQ[:, :], scalar=VALID[:, p:p + 1], in1=Sst[:], op0=ALU.mult, op1=ALU.add,
                                 reads=["psQ", "VALID", "Sst", "Sst_0", "Sst_1"], writes=["tmpS"])
                            S.op("pool", "tensor_tensor", out=Sst[:], in0=tmpS[:], in1=DC[:], op=ALU.mult, reads=["tmpS", "DC", "Sblk"], writes=["Sst"])
                            if smp is not None:
                                S.dma("sp", out=o_sret.ap()[smp], in_=Sst[0:64, :], reads=["Sst"])
                        nmc = ((8 * p + 7) + 127) // 128 if smp is None else nmc_s
                        for g in range(2):
                            def evac_O(bi_, gate_b):
                                S.op("act", "copy", out=OT[0:65, 0:N], in_=psO[0:65, 0:N], reads=["psO"], writes=["OT"])
                                for r in range(4):
                                    S.op("pe", "transpose", out=psX[0:T, r * 65:(r + 1) * 65], in_=OT[0:65, r * T:(r + 1) * T], identity=ident_f[0:65, 0:65],
                                         reads=["OT", "ident_f"], writes=["psX"])
                                S.op("dve", "tensor_copy", out=Ob[bi_][0:T].rearrange("p r d -> p (r d)"), in_=psX[0:T, 0:260], reads=["psX"], writes=[f"Ob{bi_}"])
                                S.op("dve", "tensor_scalar", out=rz[0:T, bi_, :], in0=Ob[bi_][0:T, :, 64], scalar1=1e-30, scalar2=None, op0=ALU.add, reads=[f"Ob{bi_}"], writes=[f"rz{bi_}"])
                                S.op("dve", "reciprocal", out=rz[0:T, bi_, :], in_=rz[0:T, bi_, :], reads=[f"rz{bi_}"], writes=[f"rz{bi_}"])
                                S.op("dve", "tensor_tensor", out=coef[0:T, bi_, :], in0=rz[0:T, bi_, :], in1=gates[0:T, gate_b * 8 + 4 * g:gate_b * 8 + 4 * g + 4], op=ALU.mult,
                                     reads=[f"rz{bi_}", "gates"], writes=[f"coef{bi_}"])
                            for m in range(nmc):
                                rr = p - 16 * m
                                pz, pk = psS[m % 2], f"psS{m % 2}"
                                S.op("pe", "matmul", out=pz[:, 0:N], lhsT=kcmpT[:, m * 128:(m + 1) * 128], rhs=QTz[g][:, :, 0:T], start=True, stop=False,
                                     reads=["kcmpT", f"QTz{g}"], writes=[pk])
                                brhs = GC[:, rc_idx[rr], 4 * g:4 * g + 4, tq0:tq0 + T] if rr < 23 else CHT[:, 4 * g:4 * g + 4, tq0:tq0 + T]
                                S.op("pe", "matmul", out=pz[:, 0:N], lhsT=anti_b[:], rhs=brhs, start=False, stop=True, reads=["GC", "CHT", "anti_b"], writes=[pk])
                                S.op("act", "activation", out=PTc[m][:, :, 0:T], in_=pz[:, 0:N].rearrange("p (c t) -> p c t", c=4), func=AF.Exp, bias=CMPB[:, m:m + 1], scale=1.0,
                                     reads=[pk, "CMPB"], writes=[f"PTc{m}"])
                                S.op("pe", "matmul", out=psO[0:65, 0:N], lhsT=vcmp[:, m, g, :], rhs=PTc[m][:, :, 0:T], start=(m == 0), stop=(m == nmc - 1),
                                     reads=["vcmp", f"PTc{m}"], writes=["psO"])
                            evac_O(0, 0)
                            for r in range(4):
                                for m in range(nmc):
                                    S.op("pe", "matmul", out=psQ[0:T, 0:NBv], lhsT=PTc[m][:, r, 0:T], rhs=AM[:, m, :], start=(m == 0), stop=(m == nmc - 1),
                                         reads=[f"PTc{m}", "AM"], writes=["psQ"])
                                if r == 0:
                                    S.op("dve", "scalar_tensor_tensor", out=impa[0:T, :], in0=psQ[0:T, 0:NBv], scalar=rz[0:T, 0, r:r + 1], in1=FS[0:T, :], op0=ALU.mult, op1=ALU.add,
                                         reads=["psQ", "rz0", "FS"], writes=["impa"])
                                else:
                                    S.op("dve", "scalar_tensor_tensor", out=impa[0:T, :], in0=psQ[0:T, 0:NBv], scalar=rz[0:T, 0, r:r + 1], in1=impa[0:T, :], op0=ALU.mult, op1=ALU.add,
                                         reads=["psQ", "rz0", "impa"], writes=["impa"])
                            S.op("dve", "max", out=m8[0:T, 0:8], in_=impa[0:T, :], reads=["impa"], writes=["m8"])
                            S.op("dve", "match_replace", out=wk[0:T, :], in_to_replace=m8[0:T, 0:8], in_values=impa[0:T, :], imm_value=-1e30, reads=["impa", "m8"], writes=["wk"])
                            S.op("dve", "max", out=m8[0:T, 8:16], in_=wk[0:T, :], reads=["wk"], writes=["m8"])
                            S.op("dve", "tensor_scalar", out=m8[0:T, 15:16], in0=m8[0:T, 15:16], scalar1=-1e29, scalar2=None, op0=ALU.max, reads=["m8"], writes=["m8"])
                            S.op("dve", "tensor_scalar", out=Mneg[0:T, :], in0=impa[0:T, :], scalar1=m8[0:T, 15:16], scalar2=1.0, op0=ALU.is_ge, op1=ALU.mult,
                                 reads=["impa", "m8"], writes=["Mneg"])
                            for pc_ in range(NPv):
                                S.op("pe", "transpose", out=psT[:, pc_ * 128:pc_ * 128 + T], in_=Mneg[0:T, pc_ * 128:(pc_ + 1) * 128], identity=ident_b[0:T, 0:T],
                                     reads=["Mneg", "ident_b"], writes=["psT"])
                            for pc_ in range(NPv):
                                S.op("act", "copy", out=MST[:, pc_, 0, 0:T], in_=psT[:, pc_ * 128:pc_ * 128 + T], reads=["psT"], writes=["MST"])
                            for jj in range(p + 1):
                                rr = p - jj
                                pz, pk = psS[jj % 2], f"psS{jj % 2}"
                                pm, pmk = (psQ, "psQ") if jj % 2 == 0 else (psX, "psX")
                                S.op("pe", "matmul", out=pm[:, 0:T], lhsT=IND[:, 128 * (jj % 64):128 * (jj % 64) + 128], rhs=MST[:, jj // 64, 0, 0:T], start=True, stop=True,
                                     reads=["IND", "MST"], writes=[pmk])
                                S.op("pe", "matmul", out=pz[:, 0:N], lhsT=KsT[:, jj * 128:(jj + 1) * 128], rhs=QTz[g][:, :, 0:T], start=True, stop=(rr > 7),
                                     reads=["KsT", f"QTz{g}"], writes=[pk])
                                if rr <= 7:
                                    S.op("pe", "matmul", out=pz[:, 0:N], lhsT=anti_b[:], rhs=GS[:, rr, 4 * g:4 * g + 4, tq0:tq0 + T], start=False, stop=True,
                                         reads=["GS", "anti_b"], writes=[pk])
                                pt, ptk = PT[jj % 3], f"PT{jj % 3}"
                                S.op("act", "activation", out=pt[:, :, 0:T], in_=pz[:, 0:N].rearrange("p (c t) -> p c t", c=4), func=AF.Exp, reads=[pk], writes=[ptk])
                                ptm, ptmk = PTm[jj % 2], f"PTm{jj % 2}"
                                S.op("dve", "tensor_tensor", out=ptm[:, :, 0:T], in0=pt[:, :, 0:T], in1=pm[:, 0:T].unsqueeze(1).to_broadcast([128, 4, T]), op=ALU.mult,
                                     reads=[ptk, pmk], writes=[ptmk])
                                S.op("pe", "matmul", out=psO[0:65, 0:N], lhsT=Vs[:, jj, g, :], rhs=ptm[:, :, 0:T], start=(jj == 0), stop=(jj == p),
                                     reads=["Vs", ptmk], writes=["psO"])
                            evac_O(1, 1)
                            for rr in range(5):
                                wslot = 4 - rr
                                pz, pk = psS[rr % 2], f"psS{rr % 2}"
                                S.op("pe", "matmul", out=pz[:, 0:N], lhsT=KW[:, wslot, :], rhs=QTz[g][:, :, 0:T], start=True, stop=False, reads=["KW", f"QTz{g}"], writes=[pk])
                                S.op("pe", "matmul", out=pz[:, 0:N], lhsT=anti_b[:], rhs=GW[:, rr, 4 * g:4 * g + 4, tq0:tq0 + T], start=False, stop=True,
                                     reads=["GW", "anti_b"], writes=[pk])
                                pt, ptk = PT[rr % 3], f"PT{rr % 3}"
                                S.op("act", "activation", out=pt[:, :, 0:T], in_=pz[:, 0:N].rearrange("p (c t) -> p c t", c=4), func=AF.Exp, bias=WINB[:, wi0 + wslot:wi0 + wslot + 1], scale=1.0,
                                     reads=[pk, "WINB"], writes=[ptk])
                                S.op("pe", "matmul", out=psO[0:65, 0:N], lhsT=VW[:, wslot, g, :], rhs=pt[:, :, 0:T], start=(rr == 0), stop=(rr == 4),
                                     reads=["VW", "VWd0", "VWd1", ptk], writes=["psO"])
                            evac_O(2, 2)
                            for r in range(4):
                                hh = 4 * g + r
                                S.op("dve", "tensor_scalar", out=onsa[0:T, hh, :], in0=Ob[0][0:T, r, 0:64], scalar1=coef[0:T, 0, r:r + 1], scalar2=None, op0=ALU.mult,
                                     reads=["Ob0", "coef0"], writes=[f"onsa{hh}"])
                                for bi_ in (1, 2):
                                    S.op("dve", "scalar_tensor_tensor", out=onsa[0:T, hh, :], in0=Ob[bi_][0:T, r, 0:64], scalar=coef[0:T, bi_, r:r + 1], in1=onsa[0:T, hh, :],
                                         op0=ALU.mult, op1=ALU.add, reads=[f"Ob{bi_}", f"coef{bi_}", f"onsa{hh}"], writes=[f"onsa{hh}"])
                        S.op("pool", "tensor_copy", out=onsb[0:T, :], in_=onsa[0:T].rearrange("p h d -> p (h d)"), reads=[f"onsa{h}" for h in range(8)], writes=["onsb"])
                        for c in range(4):
                            S.op("pe", "transpose", out=psT[:, c * 128:c * 128 + T], in_=onsb[0:T, c * 128:(c + 1) * 128], identity=ident_b[0:T, 0:T],
                                 reads=["onsb", "ident_b"], writes=["psT"])
                            S.op("pe", "transpose", out=psT[:, (4 + c) * 128:(4 + c) * 128 + T], in_=oret[0:T, c * 128:(c + 1) * 128], identity=ident_b[0:T, 0:T],
                                 reads=["oret"], writes=["psT"])
                        S.op("act", "copy", out=omT[:, :, 0:T], in_=psT[:, :].rearrange("p (k t) -> p k t", k=8)[:, :, 0:T], reads=["psT"], writes=["omT"])
                        S.dma("sp", out=om_d.ap()[ti].rearrange("p (k t) -> p k t", k=8)[:, :, (tsr if smp is not None else 0):(tsr if smp is not None else 0) + T], in_=omT[:, :, 0:T], reads=["omT"], writes=[f"om_d{ti}"], allow_slow_non_contiguous=True)
                        if debug and smp is None:
                            S.dma("sp", out=dbg["omix"].ap()[ti, 0:T, 0:512], in_=onsa[0:T].rearrange("p h d -> p (h d)"), reads=[f"onsa{h}" for h in range(8)])
                            S.dma("pool", out=dbg["omix"].ap()[ti, 0:T, 512:1024], in_=oret[0:T, :], reads=["oret"])
                S.flush()


        phase_1b(None)
        if "s" in phases:
            for smp_ in range(NSMP):
                phase_1a(smp_)
                phase_1b(smp_)
        if "2" not in phases:
            return nc
        p1.close()
        with ExitStack() as ph:
            idb2 = SB(ph, "idb2", [128, 128], BF16)
            Wo = SB(ph, "Wo", [128, KC, D], BF16)
            Wd = SB(ph, "Wd", [128, NFC, D], BF16)
            wu = [SB(ph, f"wu{i}", [128, KC, 2, 128], BF16) for i in range(2)]
            VEC = SB(ph, "VEC", [128, 5, D], F32)
            CW = SB(ph, "CW", [128, NFC, 4], F32)
            HVAL = SB(ph, "HVAL", [128, KSEG], F32)
            x1s = SB(ph, "x1s", [128, 5, D], F32)
            hn2T = SB(ph, "hn2T", [128, KC, 514], BF16)
            hT = SB(ph, "hT", [128, NFC, 512], BF16)
            omT = SB(ph, "omT2", [128, KC, 514], BF16)
            agx = SB(ph, "agx", [128, 514], F32)
            cv = SB(ph, "cv", [128, 512], F32)
            sl = SB(ph, "sl", [128, 512], F32)
            cvo = SB(ph, "cvo", [128, NFC, 2], F32)
            xt = [SB(ph, f"xt2_{i}", [128, D], F32) for i in range(2)]
            t1 = SB(ph, "t1c", [128, D], F32)
            t2 = SB(ph, "t2c", [128, D], F32)
            hn2 = SB(ph, "hn2", [128, D], BF16)
            st4 = SB(ph, "st4c", [128, 4], F32)
            yb = [SB(ph, f"yb{i}", [128, D], F32) for i in range(2)]
            MVs = SB(ph, "MVs", [NSMP, 4, D], F32)
            bld = SB(ph, "bld", [8, D_FF], F32)
            Bst = SB(ph, "Bst", [128, NFC, 8], F32)
            ags = SB(ph, "ags", [128, NFC, NSMP], F32)
            idf2 = SB(ph, "idf2", [128, 128], F32)
            psA = PS(ph, "psA2", [128, 512], F32)
            psB = PS(ph, "psB2", [128, 512], F32)
            psG = PS(ph, "psG2", [128, 512], F32)
            psV = PS(ph, "psV2", [128, 512], F32)
            psH = PS(ph, "psH2", [128, 512], F32)
            psT = PS(ph, "psT2c", [128, 1024], BF16)
            S.dma("pool", out=idb2[:], in_=cn["c_ident"].ap()[:, :], writes=["idb2"])
            for kc in range(KC):
                S.dma("pool", out=Wo[:, kc, :], in_=w_out.ap()[kc * 128:(kc + 1) * 128, :], writes=["Wo"])
            for f in range(NFC):
                S.dma("pool", out=Wd[:, f, :], in_=w_down.ap()[f * 128:(f + 1) * 128, :], writes=["Wd"])
            for i, col in enumerate((2, 3, 4, 5)):
                S.dma("sp", out=VEC[:, i, :], in_=modv_d.ap()[32:33, col * D:(col + 1) * D].partition_broadcast(128), writes=[f"VEC{i}"])
            S.dma("sp", out=VEC[:, 4, :], in_=gf.ap()[0:1, :].partition_broadcast(128), writes=["VEC4"])
            for jw in range(3):
                S.dma("sp", out=CW[:, :, jw], in_=conv_w.ap()[jw:jw + 1, :].rearrange("o (f p) -> p (o f)", p=128), writes=[f"CW{jw}"], allow_slow_non_contiguous=True)
            S.dma("sp", out=CW[:, :, 3], in_=conv_b.ap()[0:1, :].rearrange("o (f p) -> p (o f)", p=128), writes=["CW3"], allow_slow_non_contiguous=True)
            S.dma("sp", out=HVAL[:], in_=cn["c_hval"].ap()[:, :], writes=["HVAL"])
            cwk = ["CW0", "CW1", "CW2", "CW3"]
            GT1, A2, SH2, GT2, GFb = [VEC[:, i, :] for i in range(5)]

            def rms(src, T, key):
                S.op("act", "activation", out=t2[0:T, :], in_=src, func=AF.Square, accum_out=st4[0:T, 0:1], reads=[key], writes=["t2", "st4"])
                S.op("dve", "tensor_scalar", out=st4[0:T, 1:2], in0=st4[0:T, 0:1], scalar1=1.0 / D, scalar2=RMS_EPS, op0=ALU.mult, op1=ALU.add, reads=["st4"], writes=["st4"])
                S.op("act", "activation", out=st4[0:T, 2:3], in_=st4[0:T, 1:2], func=AF.Sqrt, reads=["st4"], writes=["st4"])
                S.op("dve", "reciprocal", out=st4[0:T, 3:4], in_=st4[0:T, 2:3], reads=["st4"], writes=["st4"])

            for k in range(KSEG):
                S.dma("sp", out=omT[:, :, 0:2], in_=om_d.ap()[5 * k].rearrange("p (k t) -> p k t", k=KC)[:, :, 0:2], writes=["omT"])
                for u in range(4):
                    S.dma("sp", out=omT[:, :, 2 + 128 * u:2 + 128 * (u + 1)], in_=om_d.ap()[5 * k + 1 + u].rearrange("p (k t) -> p k t", k=KC), writes=[f"omT{u}"])
                for j in range(5):
                    p = 32 * k + 27 + j
                    T, tq0 = (2, 126) if j == 0 else (128, 0)
                    c0 = 0 if j == 0 else 2 + 128 * (j - 1)
                    xx = xt[j % 2]
                    S.dma("sp", out=xx[0:T, :], in_=xp.ap()[p * 128 + tq0:p * 128 + tq0 + T, :], writes=[f"xt{j % 2}"])
                    for half, bank, bkey in ((0, psA, "psA"), (1, psB, "psB")):
                        for kc in range(KC):
                            S.op("pe", "matmul", out=bank[0:T, :], lhsT=omT[:, kc, c0:c0 + T], rhs=Wo[:, kc, half * 512:(half + 1) * 512], start=(kc == 0), stop=(kc == KC - 1),
                                 reads=["omT", f"omT{max(j - 1, 0)}", "Wo"], writes=[bkey])
                        S.op("dve", "tensor_tensor", out=t1[0:T, half * 512:(half + 1) * 512], in0=bank[0:T, :], in1=GT1[0:T, half * 512:(half + 1) * 512], op=ALU.mult,
                             reads=[bkey, "VEC0"], writes=[f"t1h{half}"])
                    S.op("pool", "tensor_tensor", out=x1s[0:T, j, :], in0=t1[0:T, :], in1=xx[0:T, :], op=ALU.add, reads=["t1h0", "t1h1", f"xt{j % 2}"], writes=[f"x1s{j}"])
                    rms(x1s[0:T, j, :], T, f"x1s{j}")
                    S.op("dve", "scalar_tensor_tensor", out=t1[0:T, :], in0=x1s[0:T, j, :], scalar=st4[0:T, 3:4], in1=A2[0:T, :], op0=ALU.mult, op1=ALU.mult,
                         reads=[f"x1s{j}", "st4", "VEC1", "t1h0", "t1h1"], writes=["t1h0", "t1h1"])
                    S.op("pool", "tensor_tensor", out=hn2[0:T, :], in0=t1[0:T, :], in1=SH2[0:T, :], op=ALU.add, reads=["t1h0", "t1h1", "VEC2"], writes=["hn2"])
                    for kc in range(KC):
                        S.op("pe", "transpose", out=psT[:, kc * 128:kc * 128 + T], in_=hn2[0:T, kc * 128:(kc + 1) * 128], identity=idb2[0:T, 0:T],
                             reads=["hn2", "idb2"], writes=["psT"])
                    S.op("act", "copy", out=hn2T[:, :, c0:c0 + T], in_=psT[:, :].rearrange("p (k t) -> p k t", k=KC)[:, :, 0:T], reads=["psT"], writes=[f"hn2T{j}"])
                hk = [f"hn2T{j}" for j in range(5)]
                for f in range(NFC):
                    w = wu[f % 2]
                    wk_ = f"wu{f % 2}"
                    S.dma("pool", out=w[:, :, 0, :], in_=w_up.ap()[:, f * 128:(f + 1) * 128].rearrange("(k p) c -> p k c", p=128), writes=[wk_ + "g"])
                    S.dma("pool", out=w[:, :, 1, :], in_=w_up.ap()[:, D_FF + f * 128:D_FF + (f + 1) * 128].rearrange("(k p) c -> p k c", p=128), writes=[wk_ + "v"])
                    for kc in range(KC):
                        S.op("pe", "matmul", out=psG[:, :], lhsT=w[:, kc, 0, :], rhs=hn2T[:, kc, 2:514], start=(kc == 0), stop=(kc == KC - 1), reads=hk + [wk_ + "g"], writes=["psG"])
                    for kc in range(KC):
                        S.op("pe", "matmul", out=psH[:, 0:2], lhsT=w[:, kc, 0, :], rhs=hn2T[:, kc, 0:2], start=(kc == 0), stop=(kc == KC - 1), reads=hk + [wk_ + "g"], writes=["psH"])
                    for kc in range(KC):
                        S.op("pe", "matmul", out=psV[:, :], lhsT=w[:, kc, 1, :], rhs=hn2T[:, kc, 2:514], start=(kc == 0), stop=(kc == KC - 1), reads=hk + [wk_ + "v"], writes=["psV"])
                    S.op("dve", "tensor_scalar", out=agx[:, 0:2], in0=psH[:, 0:2], scalar1=HVAL[:, k:k + 1], scalar2=None, op0=ALU.mult, reads=["psH", "HVAL"], writes=["agxh"])
                    S.op("act", "copy", out=agx[:, 2:514], in_=psG[:, :], reads=["psG"], writes=["agx"])
                    S.op("dve", "tensor_scalar", out=cv[:], in0=agx[:, 0:512], scalar1=CW[:, f, 0:1], scalar2=CW[:, f, 3:4], op0=ALU.mult, op1=ALU.add,
                         reads=["agx", "agxh"] + cwk, writes=["cv"])
                    S.op("dve", "scalar_tensor_tensor", out=cv[:], in0=agx[:, 1:513], scalar=CW[:, f, 1:2], in1=cv[:], op0=ALU.mult, op1=ALU.add, reads=["agx", "agxh", "cv"] + cwk, writes=["cv"])
                    S.op("dve", "scalar_tensor_tensor", out=cv[:], in0=agx[:, 2:514], scalar=CW[:, f, 2:3], in1=cv[:], op0=ALU.mult, op1=ALU.add, reads=["agx", "cv"] + cwk, writes=["cv"])
                    S.op("act", "activation", out=sl[:], in_=cv[:], func=AF.Silu, reads=["cv"], writes=["sl"])
                    S.op("dve", "tensor_tensor", out=hT[:, f, :], in0=sl[:], in1=psV[:, :], op=ALU.mult, reads=["sl", "psV"], writes=[f"hT{f}"])
                    if k == KSEG - 1:
                        S.op("pool", "tensor_copy", out=cvo[:, f, :], in_=agx[:, 512:514], reads=["agx"], writes=["cvo"])
                htk = [f"hT{f}" for f in range(NFC)]
                for u in range(4):
                    j = u + 1
                    for half, bank, bkey in ((0, psA, "psA"), (1, psB, "psB")):
                        for f in range(NFC):
                            S.op("pe", "matmul", out=bank[:, :], lhsT=hT[:, f, u * 128:(u + 1) * 128], rhs=Wd[:, f, half * 512:(half + 1) * 512], start=(f == 0), stop=(f == NFC - 1),
                                 reads=htk + ["Wd"], writes=[bkey])
                        S.op("dve", "tensor_tensor", out=t1[:, half * 512:(half + 1) * 512], in0=bank[:, :], in1=GT2[:, half * 512:(half + 1) * 512], op=ALU.mult,
                             reads=[bkey, "VEC3"], writes=[f"t1h{half}"])
                    S.op("pool", "tensor_tensor", out=t1[:], in0=t1[:], in1=x1s[:, j, :], op=ALU.add, reads=["t1h0", "t1h1", f"x1s{j}"], writes=["t1h0", "t1h1"])
                    rms(t1[:], 128, "t1h0")
                    y_ = yb[u % 2]
                    S.op("dve", "scalar_tensor_tensor", out=y_[:], in0=t1[:], scalar=st4[:, 3:4], in1=GFb, op0=ALU.mult, op1=ALU.mult,
                         reads=["t1h0", "t1h1", "st4", "VEC4"], writes=[f"yb{u % 2}"])
                    S.dma("sp", out=o_y.ap()[4 * k + u], in_=y_[:], reads=[f"yb{u % 2}"])

            if "s" in phases:
                T = NSMP
                S.dma("sp", out=idf2[:], in_=cn["c_ident"].ap()[:, :], writes=["idf2"])
                S.dma("sp", out=omT[:, :, 0:T], in_=om_d.ap()[TS].rearrange("p (k t) -> p k t", k=KC)[:, :, 0:T], writes=["omT"])
                for i, col in enumerate((2, 3, 4, 5)):
                    S.dma("sp", out=MVs[0:T, i, :], in_=modv_d.ap()[0:T, col * D:(col + 1) * D], writes=[f"MVs{i}"])
                xx = xt[0]
                S.dma("sp", out=xx[0:T, :], in_=xs.ap()[:, :], writes=["xt0"])
                S.dma("sp", out=bld[:], in_=sconv.ap()[:, :], writes=["bld"])
                for f in range(NFC):
                    S.op("pe", "transpose", out=psG[:, f * 8:(f + 1) * 8], in_=bld[0:8, f * 128:(f + 1) * 128], identity=idf2[0:8, 0:8],
                         reads=["bld", "idf2"], writes=["psG"])
                S.op("dve", "tensor_copy", out=Bst[:].rearrange("p f e -> p (f e)"), in_=psG[:, 0:NFC * 8], reads=["psG"], writes=["Bst"])
                for half, bank, bkey in ((0, psA, "psA"), (1, psB, "psB")):
                    for kc in range(KC):
                        S.op("pe", "matmul", out=bank[0:T, :], lhsT=omT[:, kc, 0:T], rhs=Wo[:, kc, half * 512:(half + 1) * 512], start=(kc == 0), stop=(kc == KC - 1),
                             reads=["omT", "Wo"], writes=[bkey])
                    S.op("dve", "tensor_tensor", out=t1[0:T, half * 512:(half + 1) * 512], in0=bank[0:T, :], in1=MVs[0:T, 0, half * 512:(half + 1) * 512], op=ALU.mult,
                         reads=[bkey, "MVs0"], writes=[f"t1h{half}"])
                S.op("pool", "tensor_tensor", out=x1s[0:T, 0, :], in0=t1[0:T, :], in1=xx[0:T, :], op=ALU.add, reads=["t1h0", "t1h1", "xt0"], writes=["x1s0"])
                rms(x1s[0:T, 0, :], T, "x1s0")
                S.op("dve", "scalar_tensor_tensor", out=t1[0:T, :], in0=x1s[0:T, 0, :], scalar=st4[0:T, 3:4], in1=MVs[0:T, 1, :], op0=ALU.mult, op1=ALU.mult,
                     reads=["x1s0", "st4", "MVs1", "t1h0", "t1h1"], writes=["t1h0", "t1h1"])
                S.op("pool", "tensor_tensor", out=hn2[0:T, :], in0=t1[0:T, :], in1=MVs[0:T, 2, :], op=ALU.add, reads=["t1h0", "t1h1", "MVs2"], writes=["hn2"])
                for kc in range(KC):
                    S.op("pe", "transpose", out=psT[:, kc * 128:kc * 128 + T], in_=hn2[0:T, kc * 128:(kc + 1) * 128], identity=idb2[0:T, 0:T],
                         reads=["hn2", "idb2"], writes=["psT"])
                S.op("act", "copy", out=hn2T[:, :, 0:T], in_=psT[:, :].rearrange("p (k t) -> p k t", k=KC)[:, :, 0:T], reads=["psT"], writes=["hn2Ts"])
                for f in range(NFC):
                    w = wu[f % 2]
                    wk_ = f"wu{f % 2}"
                    S.dma("pool", out=w[:, :, 0, :], in_=w_up.ap()[:, f * 128:(f + 1) * 128].rearrange("(k p) c -> p k c", p=128), writes=[wk_ + "g"])
                    S.dma("pool", out=w[:, :, 1, :], in_=w_up.ap()[:, D_FF + f * 128:D_FF + (f + 1) * 128].rearrange("(k p) c -> p k c", p=128), writes=[wk_ + "v"])
                    for kc in range(KC):
                        S.op("pe", "matmul", out=psG[:, 0:T], lhsT=w[:, kc, 0, :], rhs=hn2T[:, kc, 0:T], start=(kc == 0), stop=(kc == KC - 1), reads=["hn2Ts", wk_ + "g"], writes=["psG"])
                    for kc in range(KC):
                        S.op("pe", "matmul", out=psV[:, 0:T], lhsT=w[:, kc, 1, :], rhs=hn2T[:, kc, 0:T], start=(kc == 0), stop=(kc == KC - 1), reads=["hn2Ts", wk_ + "v"], writes=["psV"])
                    S.op("act", "copy", out=ags[:, f, :], in_=psG[:, 0:T], reads=["psG"], writes=[f"ags{f}"])
                    S.op("dve", "tensor_scalar", out=cv[:, 0:T], in0=Bst[:, f, 0::2], scalar1=CW[:, f, 0:1], scalar2=CW[:, f, 3:4], op0=ALU.mult, op1=ALU.add,
                         reads=["Bst"] + cwk, writes=["cv"])
                    S.op("dve", "scalar_tensor_tensor", out=cv[:, 0:T], in0=Bst[:, f, 1::2], scalar=CW[:, f, 1:2], in1=cv[:, 0:T], op0=ALU.mult, op1=ALU.add, reads=["Bst", "cv"] + cwk, writes=["cv"])
                    S.op("dve", "scalar_tensor_tensor", out=cv[:, 0:T], in0=ags[:, f, :], scalar=CW[:, f, 2:3], in1=cv[:, 0:T], op0=ALU.mult, op1=ALU.add, reads=[f"ags{f}", "cv"] + cwk, writes=["cv"])
                    S.op("act", "activation", out=sl[:, 0:T], in_=cv[:, 0:T], func=AF.Silu, reads=["cv"], writes=["sl"])
                    S.op("dve", "tensor_tensor", out=hT[:, f, 0:T], in0=sl[:, 0:T], in1=psV[:, 0:T], op=ALU.mult, reads=["sl", "psV"], writes=[f"hTs{f}"])
                htk = [f"hTs{f}" for f in range(NFC)]
                for half, bank, bkey in ((0, psA, "psA"), (1, psB, "psB")):
                    for f in range(NFC):
                        S.op("pe", "matmul", out=bank[0:T, :], lhsT=hT[:, f, 0:T], rhs=Wd[:, f, half * 512:(half + 1) * 512], start=(f == 0), stop=(f == NFC - 1),
                             reads=htk + ["Wd"], writes=[bkey])
                    S.op("dve", "tensor_tensor", out=t1[0:T, half * 512:(half + 1) * 512], in0=bank[0:T, :], in1=MVs[0:T, 3, half * 512:(half + 1) * 512], op=ALU.mult,
                         reads=[bkey, "MVs3"], writes=[f"t1h{half}"])
                S.op("pool", "tensor_tensor", out=t1[0:T, :], in0=t1[0:T, :], in1=x1s[0:T, 0, :], op=ALU.add, reads=["t1h0", "t1h1", "x1s0"], writes=["t1h0", "t1h1"])
                rms(t1[0:T, :], T, "t1h0")
                S.op("dve", "scalar_tensor_tensor", out=yb[0][0:T, :], in0=t1[0:T, :], scalar=st4[0:T, 3:4], in1=GFb[0:T, :], op0=ALU.mult, op1=ALU.mult,
                     reads=["t1h0", "t1h1", "st4", "VEC4"], writes=["yb0"])
                S.dma("sp", out=o_ys.ap()[:, :], in_=yb[0][0:T, :], reads=["yb0"])
                for sq in range(NSMP):
                    S.dma("sp", out=o_sconv.ap()[sq, 0:1, :], in_=bld[2 * sq + 1:2 * sq + 2, :], reads=["bld"])
                    S.dma("sp", out=o_sconv.ap()[sq, 1:2, :].rearrange("o (f p) -> p (o f)", p=128), in_=ags[:, :, sq], reads=[f"ags{f}" for f in range(NFC)],
                          allow_slow_non_contiguous=True)
            for jj in range(2):
                S.dma("sp", out=o_conv.ap()[jj:jj + 1, :].rearrange("o (f p) -> p (o f)", p=128), in_=cvo[:, :, jj], reads=["cvo"], allow_slow_non_contiguous=True)
            S.flush()
    return nc


def prep_shared(inputs):
    w_in = np.asarray(inputs["w_in"][0], np.float32)
    sh = {}
    sh["w_kv"] = np.ascontiguousarray(np.concatenate([w_in[:, 512:1280], w_in[:, 1816:2840]], axis=1))
    qperm = []
    for c in range(4):
        qperm += list(range(64 * c, 64 * c + 64)) + list(range(64 * (4 + c), 64 * (4 + c) + 64))
    sh["w_q"] = np.ascontiguousarray(np.concatenate([w_in[:, qperm], w_in[:, 1280:1304], w_in[:, 1304:1816], w_in[:, 2840:3352]], axis=1))
    sh["w_ada"] = np.ascontiguousarray(inputs["w_ada"][0], np.float32)
    sh["b_ada"] = np.ascontiguousarray(inputs["b_ada"][0:1], np.float32)
    sh["g1"] = np.ascontiguousarray(inputs["norm_mix_g"][0:1], np.float32)
    sh["g2"] = np.ascontiguousarray(inputs["norm_ffn_g"][0:1], np.float32)
    sh["gf"] = np.ascontiguousarray(np.asarray(inputs["final_g"], np.float32)[None, :])
    sh["table"] = np.ascontiguousarray(inputs["rel_bias_table"], np.float32)
    for nm, key in (("w1k", "w_cmp1_k"), ("w1v", "w_cmp1_v")):
        w = np.asarray(inputs[key][0], np.float32).transpose(1, 0, 2).reshape(64, 32 * 128)
        sh[nm] = np.ascontiguousarray(np.concatenate([w, w], axis=0))
    sh["w2k"] = np.ascontiguousarray(inputs["w_cmp2_k"][0], np.float32)
    sh["w2v"] = np.ascontiguousarray(inputs["w_cmp2_v"][0], np.float32)
    for nm, key in (("pek", "cmp_pe_k"), ("pev", "cmp_pe_v")):
        pe = np.asarray(inputs[key][0], np.float32).T
        sh[nm] = np.ascontiguousarray(np.concatenate([pe, pe], axis=0))
    sh["gng"] = np.ascontiguousarray(inputs["ret_gn_g"][0:1], np.float32)
    sh["w_out"] = np.ascontiguousarray(inputs["w_out"][0], np.float32)
    sh["w_up"] = np.ascontiguousarray(inputs["w_up"][0], np.float32)
    sh["w_down"] = np.ascontiguousarray(inputs["w_down"][0], np.float32)
    sh["conv_w"] = np.ascontiguousarray(inputs["conv_w"][0], np.float32)
    sh["conv_b"] = np.ascontiguousarray(inputs["conv_b"][0:1], np.float32)
    cc = np.asarray(inputs["cache_cmp_kv"], np.float32)
    sh["_cache_c"] = np.ascontiguousarray(cc[0]).reshape(cc.shape[1] * 128, 256)
    cs_ = np.asarray(inputs["cache_slc_kv"], np.float32)
    sh["_cache_s"] = np.ascontiguousarray(cs_[0]).reshape(cs_.shape[1] * 128, 256)
    return sh


def prep_core(cfg, inputs, shared, c):
    m = dict(shared)
    x = np.asarray(inputs["x_prompt"][0], np.float32)
    shf = cfg.shift(c) * 128
    xs = np.zeros((cfg.NSLOT * 128, D), np.float32)
    n = cfg.NSLOT * 128 - shf
    xs[shf:] = x[:n]
    m["xp"] = xs
    cin = np.zeros((33, D), np.float32)
    cs = np.asarray(inputs["c_sample"], np.float32)
    nb = cs.shape[0] // NCORE
    cin[0:nb] = cs[c * nb:(c + 1) * nb]
    cin[32] = np.asarray(inputs["c_prompt"], np.float32)[0]
    m["cin"] = cin
    m.update(make_consts(cfg, c))
    sl_ = slice(c * nb, (c + 1) * nb)
    m["xs"] = np.ascontiguousarray(np.asarray(inputs["x_sample"], np.float32)[sl_, 0, :])
    m["pt"] = np.ascontiguousarray(np.asarray(inputs["page_table"], np.int32)[sl_])
    m["cache_c"] = shared["_cache_c"]
    m["cache_s"] = shared["_cache_s"]
    m["swin"] = np.ascontiguousarray(np.asarray(inputs["state_win_kv"], np.float32)[0, sl_].reshape(nb, 512, 256))
    m["sret"] = np.ascontiguousarray(np.asarray(inputs["state_ret"], np.float32)[0, sl_])
    m["sconv"] = np.ascontiguousarray(np.asarray(inputs["state_conv"], np.float32)[0, sl_].reshape(nb * 2, D_FF))
    m.pop("_cache_c"); m.pop("_cache_s")
    return m


_PROG_CACHE = {}


def kernel(**inputs):
    x_prompt = np.asarray(inputs["x_prompt"])
    S_len = x_prompt.shape[1]
    NT = S_len // 128
    NPG = np.asarray(inputs["page_table"]).shape[1]
    NPOOL = np.asarray(inputs["cache_cmp_kv"]).shape[1]
    cfg = Cfg(NT, NPG=NPG, NPOOL=NPOOL)
    key = (NT, NPG, NPOOL)
    if key not in _PROG_CACHE:
        _PROG_CACHE[key] = build_program(cfg, phases=("0", "1a", "1b", "2", "s"))
    nc = _PROG_CACHE[key]
    shared = prep_shared(inputs)
    maps = [prep_core(cfg, inputs, shared, c) for c in range(NCORE)]
    res = run_bass_kernel_spmd(nc, maps, core_ids=list(range(NCORE)))
    rs = res.results

    def assemble(name, width):
        out = np.zeros((NT * 128, width), np.float32)
        for c in range(NCORE):
            for k in range(cfg.KSEG):
                for u in range(4):
                    t = 32 * k + 4 * c + u
                    out[t * 128:(t + 1) * 128] = rs[c][name][4 * k + u]
        return out

    y_prompt = assemble("o_y", D).reshape(1, S_len, D)
    kvc = assemble("o_kvc", 256).reshape(1, 1, S_len, 2, 2, 64)
    kvs = assemble("o_kvs", 256).reshape(1, 1, S_len, 2, 2, 64)
    kvw = assemble("o_kvw", 256)[-512:].reshape(1, 1, 512, 2, 2, 64)
    ret_p = rs[7]["o_ret"][:64].reshape(64, 8, 64).transpose(1, 0, 2).reshape(1, 1, 8, 64, 64).copy()
    conv_p = rs[7]["o_conv"].reshape(1, 1, 2, D_FF).copy()
    NBS = np.asarray(inputs["x_sample"]).shape[0]
    y_sample = np.concatenate([rs[c]["o_ys"] for c in range(NCORE)], axis=0).reshape(NBS, 1, D)
    skv = np.concatenate([rs[c]["o_skv"] for c in range(NCORE)], axis=0)
    cmp_s = skv[:, 0:256].reshape(1, NBS, 1, 2, 2, 64).copy()
    slc_s = skv[:, 256:512].reshape(1, NBS, 1, 2, 2, 64).copy()
    win_s = np.concatenate([rs[c]["o_swin"] for c in range(NCORE)], axis=0).reshape(1, NBS, 512, 2, 2, 64)
    ret_s = np.concatenate([rs[c]["o_sret"] for c in range(NCORE)], axis=0).reshape(NBS, 64, 8, 64).transpose(0, 2, 1, 3).reshape(1, NBS, 8, 64, 64).copy()
    conv_s = np.concatenate([rs[c]["o_sconv"] for c in range(NCORE)], axis=0).reshape(1, NBS, 2, D_FF)
    return (y_prompt, y_sample, kvc, cmp_s, kvs, slc_s, kvw, win_s, ret_p, ret_s, conv_p, conv_s)
```

```python
import math
from contextlib import ExitStack

import numpy as np
import concourse.bass as bass
import concourse.mybir as mybir
from concourse.bass_utils import run_bass_kernel_spmd

F32 = mybir.dt.float32
BF16 = mybir.dt.bfloat16
I32 = mybir.dt.int32
AF = mybir.ActivationFunctionType
ALU = mybir.AluOpType
AX = mybir.AxisListType

D = 1024
KC = 8
NCORE = 8
BIG = 24576.0
RMS_EPS = 1e-6
GN_EPS = 1e-5
D_FF = 2816
NFC = D_FF // 128
NB_P = 256

SEM_ROT = 30000
import os as _os
STRICT = _os.environ.get("K_STRICT", "0") == "1"
N_DMA_SEMS = 24


class Sched:
    ENGS = ("pe", "act", "dve", "pool", "sp")

    def __init__(self, nc, stack):
        self.nc = nc
        self.stack = stack
        self.prog = {e: [] for e in self.ENGS}
        self.sems = {e: [stack.enter_context(nc.semaphore(f"s_{e}_0"))] for e in self.ENGS}
        self.count = {e: 0 for e in self.ENGS}
        self.seen = {e: {} for e in self.ENGS}
        self.track = {}
        self.dpool = {}
        for cls, n in (("hw", N_DMA_SEMS), ("sw", N_DMA_SEMS // 2)):
            self.dpool[cls] = dict(sems=[stack.enter_context(nc.semaphore(f"s_dma_{cls}_{i}")) for i in range(n)],
                                   count=[0] * n, last=[None] * n, next=0)
        self.ninstr = 0

    def _need_wait(self, eng, tok, raw):
        sem, val, src = tok
        if src == eng and (eng == "pe" or not (raw or STRICT)):
            return False
        k = id(sem)
        if self.seen[eng].get(k, -1) >= val:
            return False
        self.seen[eng][k] = val
        return True

    def _emit_waits(self, eng, toks):
        for tok, raw in toks:
            if tok is None:
                continue
            if self._need_wait(eng, tok, raw):
                sem, val, _ = tok
                self.prog[eng].append(lambda e, sem=sem, val=val: e.wait_ge(sem, val))

    def _deps(self, reads, writes):
        toks = []
        for k in reads:
            t = self.track.get(k)
            if t is not None:
                toks.append((t[0], True))
                if k.startswith("ps"):
                    toks.extend((r, False) for r in t[1])
        for k in writes:
            t = self.track.get(k)
            if t is not None:
                toks.append((t[0], False))
                toks.extend((r, False) for r in t[1])
        return toks

    def _record(self, tok, reads, writes):
        for k in reads:
            t = self.track.get(k)
            if t is None:
                self.track[k] = [None, [tok]]
            else:
                t[1].append(tok)
        for k in writes:
            self.track[k] = [tok, []]

    def op(self, eng, name, reads=(), writes=(), **kw):
        fn = lambda e, name=name, kw=dict(kw): getattr(e, name)(**kw)
        self._emit_waits(eng, self._deps(reads, writes))
        if self.count[eng] >= SEM_ROT:
            self.sems[eng].append(
                self.stack.enter_context(self.nc.semaphore(f"s_{eng}_{len(self.sems[eng])}")))
            self.count[eng] = 0
        sem = self.sems[eng][-1]
        self.count[eng] += 1
        tok = (sem, self.count[eng], eng)
        self.prog[eng].append(lambda e, fn=fn, sem=sem: fn(e).then_inc(sem, 1))
        self._record(tok, reads, writes)
        self.ninstr += 1
        return tok

    def dma(self, eng, reads=(), writes=(), method="dma_start", **kw):
        fn = lambda e, kw=dict(kw), method=method: getattr(e, method)(**kw)
        dp = self.dpool["sw" if eng == "pool" else "hw"]
        i = dp["next"]
        dp["next"] = (i + 1) % len(dp["sems"])
        toks = self._deps(reads, writes)
        if dp["last"][i] is not None:
            toks.append((dp["last"][i], True))
        self._emit_waits(eng, toks)
        if dp["count"][i] >= SEM_ROT:
            dp["sems"][i] = self.stack.enter_context(self.nc.semaphore(f"s_dma_r_{self.ninstr}"))
            dp["count"][i] = 0
        dp["count"][i] += 16
        sem = dp["sems"][i]
        tok = (sem, dp["count"][i], None)
        dp["last"][i] = tok
        self.prog[eng].append(lambda e, fn=fn, sem=sem: fn(e).then_inc(sem, 16))
        self._record(tok, reads, writes)
        self.ninstr += 1
        return tok

    def flush(self):
        for dp in self.dpool.values():
            for t in dp["last"]:
                if t is not None:
                    self._emit_waits("sp", [(t, True)])
        prog = self.prog
        with self.nc.Block() as block:
            @block.sync
            def _(e):
                for f in prog["sp"]:
                    f(e)

            @block.tensor
            def _(e):
                for f in prog["pe"]:
                    f(e)

            @block.scalar
            def _(e):
                for f in prog["act"]:
                    f(e)

            @block.vector
            def _(e):
                for f in prog["dve"]:
                    f(e)

            @block.gpsimd
            def _(e):
                for f in prog["pool"]:
                    f(e)
        self.prog = {e: [] for e in self.ENGS}
        self.track = {}


def rel_bucket_np(n):
    n = np.maximum(n, 0)
    nf = np.maximum(n, 1).astype(np.float32)
    large = 16 + (np.log(nf / np.float32(16)) / np.float32(math.log(64.0)) * np.float32(16)).astype(np.int32)
    return np.where(n < 16, n, np.minimum(large, 31))


class Cfg:
    def __init__(self, NT, NPG=128, NPOOL=5120):
        assert NT % 32 == 0
        self.NPG = NPG
        self.NPOOL = NPOOL
        self.NCB_S = (8 * NPG + 127) // 128
        self.NB_S = ((2 * NPG + 1 + 127) // 128) * 128
        self.NT = NT
        self.NSLOT = NT
        self.KSEG = NT // 32
        self.NOWN = 4 * self.KSEG
        self.NCB = (NT * 8 + 127) // 128
        self.NWIN = 9 * self.KSEG

    def shift(self, c):
        return 4 * (7 - c)

    def halo_slot(self, k):
        return 32 * k + 27

    def own_slot(self, k, u):
        return 32 * k + 28 + u

    def win_index(self, p):
        k, r = divmod(p, 32)
        if r < 23:
            return None
        return 9 * k + (r - 23)


XS_LEN = 1152 + 8
XW_LEN = 768 + 8
XC_LEN = 5120


def make_consts(cfg, c):
    NS = cfg.NSLOT
    sh = cfg.shift(c)
    out = {}
    pos = (np.arange(NS * 128) - sh * 128).astype(np.int64)
    posc = np.maximum(pos, 0).astype(np.float32)
    inv = (10000.0 ** (-np.arange(32, dtype=np.float32) / 32)).astype(np.float32)
    ang = posc[:, None] * inv[None, :]
    out["c_cos"] = np.cos(ang).astype(np.float32)
    out["c_sin"] = np.sin(ang).astype(np.float32)
    h = np.arange(8, dtype=np.float64)
    gam = 1.0 - 2.0 ** (-5.0 - h)
    m = np.arange(128, dtype=np.float64)
    gk = (gam[None, :] ** (-(m[:, None] + 1.0))) / 8.0
    gq = gam[None, :] ** (m[:, None] + 1.0)
    out["c_gk"] = np.repeat(gk[:, :, None], 64, axis=2).reshape(128, 512).astype(np.float32)
    out["c_gq"] = np.repeat(gq[:, :, None], 64, axis=2).reshape(128, 512).astype(np.float32)
    dc = gam ** 128.0
    out["c_dc"] = np.broadcast_to(np.repeat(dc, 64)[None, :], (128, 512)).astype(np.float32).copy()
    valid = (np.arange(NS) >= sh).astype(np.float32)
    out["c_valid"] = np.broadcast_to(valid[None, :], (128, NS)).copy()
    tri = (np.arange(128)[:, None] <= np.arange(128)[None, :]).astype(np.float32)
    out["c_tri8"] = np.tile(tri, (1, 8)).astype(np.float32)
    out["c_ident"] = np.eye(128, dtype=np.float32)
    out["c_anti"] = np.eye(128, dtype=np.float32)[::-1].copy()
    ind = (np.arange(128)[:, None] == (np.arange(8192)[None, :] // 64)).astype(np.float32)
    out["c_ind"] = ind
    NB = NB_P
    n = np.arange(cfg.NCB * 128)
    b = np.arange(NB)
    A = ((n[:, None] >= 4 * b[None, :] - 1) & (n[:, None] <= 4 * b[None, :] + 3)).astype(np.float32)
    out["c_amat"] = A.reshape(cfg.NCB, 128, NB).transpose(1, 0, 2).copy()
    nabs = n - 8 * sh
    out["c_cmpb"] = np.where(nabs >= 0, 0.0, -BIG).astype(np.float32).reshape(cfg.NCB, 128).T.copy()
    wb = np.zeros((128, cfg.NWIN), np.float32)
    for p in range(NS):
        wi = cfg.win_index(p)
        if wi is not None:
            wb[:, wi] = 0.0 if p >= sh else -BIG
    out["c_winb"] = wb
    fsel = np.zeros((cfg.KSEG * 5, 128, NB), np.float32)
    for k in range(cfg.KSEG):
        for j in range(5):
            p = cfg.halo_slot(k) + j
            t_abs = (p - sh) * 128 + np.arange(128)
            b_abs = np.arange(NB) - 2 * sh
            cur = t_abs // 64
            validb = (b_abs[None, :] >= 0) & (b_abs[None, :] * 64 <= t_abs[:, None])
            forced = (b_abs[None, :] == 0) | (b_abs[None, :] == cur[:, None]) | (b_abs[None, :] == cur[:, None] - 1)
            f = np.where(validb, np.where(forced, 1e4, 0.0), -1e30)
            fsel[k * 5 + j] = f
    out["c_fsel"] = fsel
    hv = np.ones((128, max(cfg.KSEG, 1)), np.float32)
    for k in range(cfg.KSEG):
        if cfg.halo_slot(k) < sh:
            hv[:, k] = 0.0
    out["c_hval"] = hv
    pm = np.zeros((128, 8, 64), np.float32)
    for h_ in range(8):
        pm[64 * (h_ % 2):64 * (h_ % 2) + 64, h_, :] = 1.0
    out["c_pmk"] = pm.reshape(128, 512)
    def oh(dists, mode):
        X = len(dists)
        o = np.zeros((33, X), np.float32)
        bk = rel_bucket_np(dists)
        for x in range(X):
            d = dists[x]
            if mode == "s":
                if d < 0:
                    o[32, x] = 1.0
                else:
                    o[bk[x], x] += 1.0
                    o[31, x] -= 1.0
            elif mode == "w":
                if d < 0 or d > 512:
                    o[32, x] = 1.0
                else:
                    o[bk[x], x] = 1.0
            else:
                if d < 0:
                    o[32, x] = 1.0
                else:
                    o[bk[x], x] = 1.0
        return o
    NPG, NB_S, NCB_S = cfg.NPG, cfg.NB_S, cfg.NCB_S
    out["c_iota"] = np.arange(128, dtype=np.float32)[:, None].copy()
    angp = np.float32(128 * NPG) * inv
    out["c_cs_s"] = np.stack([np.cos(angp), np.sin(angp)]).astype(np.float32)
    tP = 128 * NPG
    bS = np.arange(NB_S)
    curS = tP // 64
    vS = bS * 64 <= tP
    fS = (bS == 0) | (bS == curS) | (bS == curS - 1)
    rowS = np.where(vS, np.where(fS, 1e4, 0.0), -1e30).astype(np.float32)
    out["c_fsel_s"] = np.broadcast_to(rowS[None, :], (128, NB_S)).copy()
    nS = np.arange(NCB_S * 128)
    AS = ((nS[:, None] >= 4 * bS[None, :] - 1) & (nS[:, None] <= 4 * bS[None, :] + 3)).astype(np.float32)
    out["c_amat_s"] = AS.reshape(NCB_S, 128, NB_S).transpose(1, 0, 2).copy()
    out["c_dc1"] = np.broadcast_to(np.repeat(gam, 64)[None, :], (128, 512)).astype(np.float32).copy()
    out["c_ohs"] = oh(np.arange(XS_LEN) - 127, "s")
    out["c_ohw"] = oh(np.arange(XW_LEN) - 127, "w")
    out["c_ohc"] = oh(np.arange(XC_LEN) - 2063, "c")
    return out


W_KV_COLS = 1792
W_Q_COLS = 1560


def build_program(cfg, phases=("0", "1a", "1b", "2"), debug=False):
    import os
    STOP = os.environ.get("K_STOP", "")
    NSLOT_LIM = int(os.environ.get("K_NSLOT", "100000"))
    nc = bass.Bass("TRN2", target_bir_lowering=False)
    NS, KSEG, NCB, NWIN, NOWN = cfg.NSLOT, cfg.KSEG, cfg.NCB, cfg.NWIN, cfg.NOWN
    NB = NB_P
    NPIECE = NB // 128

    def din(name, shape, dt=F32):
        return nc.dram_tensor(name, list(shape), dt, kind="ExternalInput")

    def dout(name, shape, dt=F32):
        return nc.dram_tensor(name, list(shape), dt, kind="ExternalOutput")

    xp = din("xp", [NS * 128, D])
    w_kv = din("w_kv", [D, W_KV_COLS])
    w_q = din("w_q", [D, W_Q_COLS])
    w_ada = din("w_ada", [D, 6 * D])
    b_ada = din("b_ada", [1, 6 * D])
    cin = din("cin", [33, D])
    g1 = din("g1", [1, D]); g2 = din("g2", [1, D]); gf = din("gf", [1, D])
    table = din("table", [32, 8])
    w1k = din("w1k", [128, 32 * 128]); w1v = din("w1v", [128, 32 * 128])
    w2k = din("w2k", [128, 64]); w2v = din("w2v", [128, 64])
    pek = din("pek", [128, 32]); pev = din("pev", [128, 32])
    gng = din("gng", [1, 512])
    w_out = din("w_out", [D, D]); w_up = din("w_up", [D, 2 * D_FF]); w_down = din("w_down", [D_FF, D])
    conv_w = din("conv_w", [3, D_FF]); conv_b = din("conv_b", [1, D_FF])
    NPG, NPOOL, NSMP = cfg.NPG, cfg.NPOOL, 4
    NCB_S, NB_S = cfg.NCB_S, cfg.NB_S
    NSLT = max(NS, NPG + 1)
    NCBT = max(NCB, NCB_S)
    TS = KSEG * 5
    xs = din("xs", [NSMP, D]); ptab = din("pt", [NSMP, NPG], I32)
    cache_c = din("cache_c", [NPOOL * 128, 256]); cache_s = din("cache_s", [NPOOL * 128, 256])
    swin = din("swin", [NSMP, 512, 256]); sret = din("sret", [NSMP, 8, 64, 64]); sconv = din("sconv", [NSMP * 2, D_FF])
    cn = {}
    for name, shape in (("c_iota", [128, 1]), ("c_cs_s", [2, 32]), ("c_fsel_s", [128, NB_S]), ("c_amat_s", [128, NCB_S, NB_S]), ("c_dc1", [128, 512])):
        cn[name] = din(name, shape)
    for name, shape in (("c_cos", [NS * 128, 32]), ("c_sin", [NS * 128, 32]), ("c_gk", [128, 512]), ("c_gq", [128, 512]),
                        ("c_dc", [128, 512]), ("c_valid", [128, NS]), ("c_tri8", [128, 1024]), ("c_ident", [128, 128]),
                        ("c_anti", [128, 128]), ("c_ind", [128, 8192]), ("c_amat", [128, NCB, NB]), ("c_cmpb", [128, NCB]),
                        ("c_winb", [128, NWIN]), ("c_fsel", [KSEG * 5, 128, NB]), ("c_hval", [128, KSEG]), ("c_pmk", [128, 512]),
                        ("c_ohs", [33, XS_LEN]), ("c_ohw", [33, XW_LEN]), ("c_ohc", [33, XC_LEN])):
        cn[name] = din(name, shape)

    o_y = dout("o_y", [NOWN, 128, D])
    o_kvc = dout("o_kvc", [NOWN, 128, 256]); o_kvs = dout("o_kvs", [NOWN, 128, 256]); o_kvw = dout("o_kvw", [NOWN, 128, 256])
    o_ret = dout("o_ret", [128, 512])
    o_conv = dout("o_conv", [2, D_FF])
    o_ys = dout("o_ys", [NSMP, D]); o_skv = dout("o_skv", [NSMP, 768]); o_swin = dout("o_swin", [NSMP, 512, 256])
    o_sret = dout("o_sret", [NSMP, 64, 512]); o_sconv = dout("o_sconv", [NSMP, 2, D_FF])
    dbg = {}
    if debug:
        dbg["kcmpT"] = dout("d_kcmpT", [128, NCB * 128])
        dbg["vcmp"] = dout("d_vcmp", [128, NCB * 2 * 65])
        dbg["omix"] = dout("d_omix", [KSEG * 5, 128, 1024])

    modv_d = nc.dram_tensor("modv_d", [33, 6 * D], F32)
    krv_d = nc.dram_tensor("krv_d", [KSEG * 5 + 1, 128, 1536], BF16)
    skv_d = nc.dram_tensor("skv_d", [4, 768], F32)
    bvs_d = nc.dram_tensor("bvs_d", [8, XS_LEN], F32)
    bvw_d = nc.dram_tensor("bvw_d", [8, XW_LEN], F32)
    bvc_d = nc.dram_tensor("bvc_d", [8, XC_LEN], F32)
    om_d = nc.dram_tensor("om_d", [KSEG * 5 + 1, 128, 1024], BF16)
    kw_d = nc.dram_tensor("kw_d", [NWIN + 5, 128, 128], BF16)
    vw_d = nc.dram_tensor("vw_d", [NWIN + 5, 128, 128], BF16)
    snap_d = nc.dram_tensor("snap_d", [KSEG, 128, 512], F32)
    qt_d = nc.dram_tensor("qt_d", [KSEG * 5 + 1, 128, 512], BF16)
    qrt_d = nc.dram_tensor("qrt_d", [KSEG * 5 + 1, 128, 512], BF16)
    gates_d = nc.dram_tensor("gates_d", [KSEG * 5 + 1, 128, 24], F32)
    sg_d = nc.dram_tensor("sg_d", [KSEG * 5 + 1, 128, 512], F32)

    with ExitStack() as top:
        S = Sched(nc, top)

        _uid = [0]

        def SB(stack, name, shape, dt):
            _uid[0] += 1
            return stack.enter_context(nc.sbuf_tensor(f"{name}_{_uid[0]}", list(shape), dt))

        def PS(stack, name, shape, dt):
            _uid[0] += 1
            return stack.enter_context(nc.psum_tensor(f"{name}_{_uid[0]}", list(shape), dt))

        p1 = top.enter_context(ExitStack())
        ident_f = SB(p1, "ident_f", [128, 128], F32)
        ident_b = SB(p1, "ident_b", [128, 128], BF16)
        anti_b = SB(p1, "anti_b", [128, 128], BF16)
        A1bc = SB(p1, "A1bc", [128, D], F32)
        SH1bc = SB(p1, "SH1bc", [128, D], F32)
        KsT = SB(p1, "KsT", [128, NSLT * 128], BF16)
        Vs = SB(p1, "Vs", [128, NSLT, 2, 65], BF16)
        kcmpT = SB(p1, "kcmpT", [128, NCBT * 128], BF16)
        vcmp = SB(p1, "vcmp", [128, NCBT, 2, 65], BF16)
        S.dma("sp", out=ident_f[:], in_=cn["c_ident"].ap()[:, :], writes=["ident_f"])
        S.dma("pool", out=ident_b[:], in_=cn["c_ident"].ap()[:, :], writes=["ident_b"])
        S.dma("pool", out=anti_b[:], in_=cn["c_anti"].ap()[:, :], writes=["anti_b"])
        S.op("pool", "memset", ap=Vs[:], constant=1.0, writes=["Vs"])
        S.op("pool", "memset", ap=vcmp[:], constant=1.0, writes=["vcmp"])

        with ExitStack() as ph:
            cs = SB(ph, "cs", [33, D], F32)
            csT = SB(ph, "csT", [128, KC, 33], F32)
            mod = SB(ph, "mod", [33, 6 * D], F32)
            bab = SB(ph, "bab", [33, 6 * D], F32)
            gb = SB(ph, "gb", [33, 2, D], F32)
            mv = SB(ph, "mv", [33, 6, D], F32)
            wa = [SB(ph, f"wa{i}", [128, KC, 512], F32) for i in range(2)]
            ps0 = PS(ph, "ps0", [128, 512], F32)
            ps1 = [PS(ph, f"ps1_{i}", [128, 512], F32) for i in range(2)]
            S.dma("sp", out=cs[:], in_=cin.ap()[:, :], writes=["cs"])
            S.dma("sp", out=bab[:], in_=b_ada.ap()[0:1, :].partition_broadcast(33), writes=["bab"])
            S.dma("sp", out=gb[:, 0, :], in_=g1.ap()[0:1, :].partition_broadcast(33), writes=["gb0"])
            S.dma("sp", out=gb[:, 1, :], in_=g2.ap()[0:1, :].partition_broadcast(33), writes=["gb1"])
            S.op("act", "activation", out=cs[:], in_=cs[:], func=AF.Silu, reads=["cs"], writes=["cs"])
            for kc in range(KC):
                S.op("pe", "transpose", out=ps0[:, kc * 33:(kc + 1) * 33], in_=cs[0:33, kc * 128:(kc + 1) * 128],
                     identity=ident_f[0:33, 0:33], reads=["cs", "ident_f"], writes=["ps0"])
            S.op("dve", "tensor_copy", out=csT[:].rearrange("p k m -> p (k m)"), in_=ps0[:, 0:KC * 33], reads=["ps0"], writes=["csT"])
            for cb in range(12):
                w = wa[cb % 2]
                S.dma("sp" if cb % 2 == 0 else "act", out=w[:],
                      in_=w_ada.ap()[:, cb * 512:(cb + 1) * 512].rearrange("(k p) c -> p k c", p=128), writes=[f"wa{cb % 2}"])
                pz = ps1[cb % 2]
                for kc in range(KC):
                    S.op("pe", "matmul", out=pz[0:33, :], lhsT=csT[:, kc, :], rhs=w[:, kc, :], start=(kc == 0), stop=(kc == KC - 1),
                         reads=["csT", f"wa{cb % 2}"], writes=[f"ps1_{cb % 2}"])
                S.op("dve", "tensor_tensor", out=mod[:, cb * 512:(cb + 1) * 512], in0=pz[0:33, :], in1=bab[:, cb * 512:(cb + 1) * 512],
                     op=ALU.add, reads=[f"ps1_{cb % 2}", "bab"], writes=["mod"])
            S.op("dve", "scalar_tensor_tensor", out=mv[:, 0, :], in0=mod[:, D:2 * D], scalar=1.0, in1=gb[:, 0, :], op0=ALU.add, op1=ALU.mult,
                 reads=["mod", "gb0"], writes=["mv"])
            S.op("dve", "scalar_tensor_tensor", out=mv[:, 3, :], in0=mod[:, 4 * D:5 * D], scalar=1.0, in1=gb[:, 1, :], op0=ALU.add, op1=ALU.mult,
                 reads=["mod", "gb1"], writes=["mv"])
            S.op("pool", "tensor_copy", out=mv[:, 1, :], in_=mod[:, 0:D], reads=["mod"], writes=["mv1"])
            S.op("pool", "tensor_copy", out=mv[:, 2, :], in_=mod[:, 2 * D:3 * D], reads=["mod"], writes=["mv2"])
            S.op("pool", "tensor_copy", out=mv[:, 4, :], in_=mod[:, 3 * D:4 * D], reads=["mod"], writes=["mv4"])
            S.op("pool", "tensor_copy", out=mv[:, 5, :], in_=mod[:, 5 * D:6 * D], reads=["mod"], writes=["mv5"])
            S.dma("sp", out=modv_d.ap()[:, :], in_=mv[:].rearrange("p a d -> p (a d)"), reads=["mv", "mv1", "mv2", "mv4", "mv5"], writes=["modv_d"])
            S.dma("sp", out=A1bc[:], in_=modv_d.ap()[32:33, 0:D].partition_broadcast(128), reads=["modv_d"], writes=["A1bc"])
            S.dma("sp", out=SH1bc[:], in_=modv_d.ap()[32:33, D:2 * D].partition_broadcast(128), reads=["modv_d"], writes=["SH1bc"])
            S.flush()

        if "1a" not in phases:
            return nc
        def phase_1a(smp):
            NSL = NS if smp is None else NPG
            with ExitStack() as ph:
                W1 = [SB(ph, "W1k", [128, 32, 128], BF16), SB(ph, "W1v", [128, 32, 128], BF16)]
                W2 = SB(ph, "W2", [128, 2, 2, 128], BF16)
                peT = SB(ph, "peT", [128, 2, 32], BF16)
                PRE = SB(ph, "PRE", [128, 4, max(NS, NPG) * 8 + 2], F32)
                GK = SB(ph, "GK", [128, 512], F32)
                DC = SB(ph, "DC", [128, 512], F32)
                VALID = SB(ph, "VALID", [128, NS], F32)
                Sst = SB(ph, "Sst", [128, 512], F32)
                psT = PS(ph, "psT", [128, 1024], BF16)
                psT2 = PS(ph, "psT2", [128, 1024], BF16)
                psZ = [PS(ph, f"psZ{i}", [128, 512], F32) for i in range(4)]
                psU = PS(ph, "psU", [128, 512], F32)
                psC = PS(ph, "psC", [128, 512], F32)

                ws = ExitStack()
                Wkv = SB(ws, "Wkv", [128, KC, W_KV_COLS], BF16)
                csb = [SB(ws, f"csb{i}", [128, 2, 32], F32) for i in range(2)]
                xb = [SB(ws, f"xb{i}", [128, D], F32) for i in range(2)]
                st4 = [SB(ws, f"st4_{i}", [128, 4], F32) for i in range(2)]
                t1 = SB(ws, "t1", [128, D], F32)
                hn = [SB(ws, f"hn{i}", [128, D], BF16) for i in range(2)]
                hnT = [SB(ws, f"hnT{i}", [128, KC, 128], BF16) for i in range(2)]
                NKB = 4
                kvb = [SB(ws, f"kvb{i}", [128, 768], BF16) for i in range(NKB)]
                kvf = [SB(ws, "kvf0", [128, 768], F32)] * 2
                BT = 16 if NSL % 16 == 0 else 4
                CT4 = [SB(ws, f"CT4_{i}", [128, BT, 2, 128], BF16) for i in range(2)]
                rt = [SB(ws, f"rt{i}", [128, 8, 32], F32) for i in range(4)]
                kro = SB(ws, "kro", [128, 8, 2, 32], F32)
                k2 = [SB(ws, f"k2_{i}", [128, 8, 2, 64], BF16) for i in range(2)]
                vb = [SB(ws, f"vb{i}", [128, 512], BF16) for i in range(2)]
                tmpS = SB(ws, "tmpS", [128, 512], F32)
                kwt = [SB(ws, f"kwt{i}", [128, 128], BF16) for i in range(2)]
                pti = SB(ws, "pti", [128, NPG], I32)
                IDX = SB(ws, "IDX", [128, NPG], I32)
                iota = SB(ws, "iota", [128, 1], F32)

                if smp is None:
                    for kc in range(KC):
                        S.dma("pool", out=Wkv[:, kc, :], in_=w_kv.ap()[kc * 128:(kc + 1) * 128, :], writes=["Wkv"])
                else:
                    S.dma("sp", out=pti[:], in_=ptab.ap()[smp:smp + 1, :].partition_broadcast(128), writes=["pti"])
                    S.dma("sp", out=iota[:], in_=cn["c_iota"].ap()[:, :], writes=["iota"])
                    S.op("dve", "tensor_scalar", out=IDX[:], in0=pti[:], scalar1=128.0, scalar2=iota[:, 0:1], op0=ALU.mult, op1=ALU.add,
                         reads=["pti", "iota"], writes=["IDX"])
                for i, w1 in enumerate((w1k, w1v)):
                    for l0 in range(0, 32, 8):
                        S.dma("pool", out=W1[i][:, l0:l0 + 8, :], in_=w1.ap()[:, l0 * 128:(l0 + 8) * 128].rearrange("p (l h) -> p l h", h=128),
                              writes=[f"W1_{i}"])
                S.op("pool", "memset", ap=W2[:], constant=0.0, writes=["W2"])
                for kv, w2 in enumerate((w2k, w2v)):
                    for g in range(2):
                        S.dma("pool", out=W2[:, kv, g, 64 * g:64 * g + 64], in_=w2.ap()[:, :], reads=["W2"], writes=[f"W2_{kv}{g}"])
                S.dma("pool", out=peT[:, 0, :], in_=pek.ap()[:, :], writes=["peT0"])
                S.dma("pool", out=peT[:, 1, :], in_=pev.ap()[:, :], writes=["peT1"])
                S.dma("sp", out=GK[:], in_=cn["c_gk"].ap()[:, :], writes=["GK"])
                S.dma("sp", out=DC[:], in_=cn["c_dc"].ap()[:, :], writes=["DC"])
                S.dma("sp", out=VALID[:], in_=cn["c_valid"].ap()[:, :], writes=["VALID"])
                S.op("dve", "memset", ap=Sst[:], constant=0.0, writes=["Sst"])
                S.op("dve", "memset", ap=PRE[:], constant=0.0, writes=["PRE"])

                own_of_slot = {}
                for k in range(KSEG):
                    own_of_slot[cfg.halo_slot(k)] = (k, 0)
                    for u in range(4):
                        own_of_slot[cfg.own_slot(k, u)] = (k, u + 1)

                for p in range(NSL):
                    b2 = p % 2
                    kb = p % NKB
                    if smp is None:
                        x_t = xb[b2]
                        S.dma("sp", out=x_t[:], in_=xp.ap()[p * 128:(p + 1) * 128, :], writes=[f"xb{b2}"])
                        stt_ = st4[b2]
                        S.op("act", "activation", out=t1[:], in_=x_t[:], func=AF.Square, accum_out=stt_[:, 0:1],
                             reads=[f"xb{b2}"], writes=["t1", f"st{b2}"])
                        S.op("dve", "tensor_scalar", out=stt_[:, 1:2], in0=stt_[:, 0:1], scalar1=1.0 / D, scalar2=RMS_EPS, op0=ALU.mult, op1=ALU.add,
                             reads=[f"st{b2}"], writes=[f"st{b2}"])
                        S.op("act", "activation", out=stt_[:, 2:3], in_=stt_[:, 1:2], func=AF.Sqrt, reads=[f"st{b2}"], writes=[f"st{b2}"])
                        S.op("dve", "reciprocal", out=stt_[:, 3:4], in_=stt_[:, 2:3], reads=[f"st{b2}"], writes=[f"st{b2}"])
                        S.op("dve", "scalar_tensor_tensor", out=t1[:], in0=x_t[:], scalar=stt_[:, 3:4], in1=A1bc[:], op0=ALU.mult, op1=ALU.mult,
                             reads=[f"xb{b2}", f"st{b2}", "A1bc"], writes=["t1"])
                        S.op("pool", "tensor_tensor", out=hn[b2][:], in0=t1[:], in1=SH1bc[:], op=ALU.add, reads=["t1", "SH1bc"], writes=[f"hn{b2}"])
                        for kc in range(KC):
                            S.op("pe", "transpose", out=psT[:, kc * 128:(kc + 1) * 128], in_=hn[b2][:, kc * 128:(kc + 1) * 128], identity=ident_b[:],
                                 reads=[f"hn{b2}", "ident_b"], writes=["psT"])
                        S.op("act", "copy", out=hnT[b2][:].rearrange("p k t -> p (k t)"), in_=psT[:, :], reads=["psT"], writes=[f"hnT{b2}"])
                        pieces = ((0, 0, 512), (1, 512, 256), (2, 768, 512), (3, 1280, 512))
                        for zi, c0, w in pieces:
                            for kc in range(KC):
                                S.op("pe", "matmul", out=psZ[zi][:, 0:w], lhsT=hnT[b2][:, kc, :], rhs=Wkv[:, kc, c0:c0 + w], start=(kc == 0), stop=(kc == KC - 1),
                                     reads=[f"hnT{b2}", "Wkv"], writes=[f"psZ{zi}"])
                        S.op("act", "copy", out=kvb[kb][:, 0:512], in_=psZ[0][:, 0:512], reads=["psZ0"], writes=[f"kvb{kb}a"])
                        S.op("dve", "tensor_copy", out=kvb[kb][:, 512:768], in_=psZ[1][:, 0:256], reads=["psZ1"], writes=[f"kvb{kb}b"])
                        if p in own_of_slot and own_of_slot[p][1] >= 1:
                            k, j = own_of_slot[p]
                            oi = 4 * k + (j - 1)
                            S.op("dve", "tensor_copy", out=kvf[b2][:, 0:512], in_=psZ[0][:, 0:512], reads=["psZ0"], writes=["kvf0a"])
                            S.op("act", "copy", out=kvf[b2][:, 512:768], in_=psZ[1][:, 0:256], reads=["psZ1"], writes=["kvf0b"])
                            if os.environ.get("K_NOOUT", "") != "1":
                                S.dma("sp", out=o_kvc.ap()[oi], in_=kvf[b2][:, 0:256], reads=["kvf0a"])
                                S.dma("sp", out=o_kvs.ap()[oi], in_=kvf[b2][:, 256:512], reads=["kvf0a"])
                                S.dma("sp", out=o_kvw.ap()[oi], in_=kvf[b2][:, 512:768], reads=["kvf0b"])

                    else:
                        S.dma("pool", method="indirect_dma_start", out=kvb[kb][:, 0:256], out_offset=None, in_=cache_c.ap()[:, :],
                              in_offset=bass.IndirectOffsetOnAxis(ap=IDX[:, p:p + 1], axis=0), reads=["IDX"], writes=[f"kvb{kb}a"])
                        S.dma("pool", method="indirect_dma_start", out=kvb[kb][:, 256:512], out_offset=None, in_=cache_s.ap()[:, :],
                              in_offset=bass.IndirectOffsetOnAxis(ap=IDX[:, p:p + 1], axis=0), reads=["IDX"], writes=[f"kvb{kb}s"])
                    S.op("pool", "tensor_copy", out=Vs[:, p, :, 0:64], in_=kvb[kb][:, 384:512].rearrange("p (g d) -> p g d", g=2),
                         reads=[f"kvb{kb}a", f"kvb{kb}s", "Vs"], writes=[f"Vs{p}"])
                    wi = cfg.win_index(p) if smp is None else None
                    S.op("pe", "transpose", out=psT2[:, 0:128], in_=kvb[kb][:, 0:128], identity=ident_b[:], reads=[f"kvb{kb}a", "ident_b"], writes=["psT2"])
                    S.op("pe", "transpose", out=psT2[:, 128:256], in_=kvb[kb][:, 128:256], identity=ident_b[:], reads=[f"kvb{kb}a"], writes=["psT2"])
                    S.op("pe", "transpose", out=psT2[:, 256:384], in_=kvb[kb][:, 256:384], identity=ident_b[:], reads=[f"kvb{kb}a", f"kvb{kb}s"], writes=["psT2"])
                    if wi is not None:
                        S.op("pe", "transpose", out=psT2[:, 384:512], in_=kvb[kb][:, 512:640], identity=ident_b[:], reads=[f"kvb{kb}b"], writes=["psT2"])
                    c4 = CT4[(p // BT) % 2]
                    S.op("act", "copy", out=c4[:, p % BT, :, :], in_=psT2[:, 0:256].rearrange("p (a x) -> p a x", a=2), reads=["psT2"],
                         writes=[f"CT4_{(p // BT) % 2}_{p % BT}"])
                    S.op("act", "copy", out=KsT[:, p * 128:(p + 1) * 128], in_=psT2[:, 256:384], reads=["psT2"], writes=[f"KsT{p}"])
                    if wi is not None:
                        S.op("act", "copy", out=kwt[b2][:], in_=psT2[:, 384:512], reads=["psT2"], writes=[f"kwt{b2}"])
                        S.dma("sp", out=kw_d.ap()[wi], in_=kwt[b2][:], reads=[f"kwt{b2}"])
                        S.dma("sp", out=vw_d.ap()[wi], in_=kvb[kb][:, 640:768], reads=[f"kvb{kb}b"])
                    if smp is None:
                        kq = psZ[2][:, :].rearrange("p (h two i) -> p h two i", h=8, two=2)
                        S.dma("act", out=csb[b2][:, 0, :], in_=cn["c_cos"].ap()[p * 128:(p + 1) * 128, :], writes=[f"cosr{b2}"])
                        S.dma("act", out=csb[b2][:, 1, :], in_=cn["c_sin"].ap()[p * 128:(p + 1) * 128, :], writes=[f"sinr{b2}"])
                        cs_ = csb[b2][:, 0, :].unsqueeze(1).to_broadcast([128, 8, 32])
                        sn_ = csb[b2][:, 1, :].unsqueeze(1).to_broadcast([128, 8, 32])
                        S.op("dve", "tensor_tensor", out=rt[0][:], in0=kq[:, :, 0, :], in1=cs_, op=ALU.mult, reads=["psZ2", f"cosr{b2}"], writes=["rt0"])
                        S.op("dve", "tensor_tensor", out=rt[1][:], in0=kq[:, :, 1, :], in1=sn_, op=ALU.mult, reads=["psZ2", f"sinr{b2}"], writes=["rt1"])
                        S.op("dve", "tensor_tensor", out=rt[2][:], in0=kq[:, :, 0, :], in1=sn_, op=ALU.mult, reads=["psZ2", f"sinr{b2}"], writes=["rt2"])
                        S.op("dve", "tensor_tensor", out=rt[3][:], in0=kq[:, :, 1, :], in1=cs_, op=ALU.mult, reads=["psZ2", f"cosr{b2}"], writes=["rt3"])
                        S.op("pool", "tensor_tensor", out=kro[:, :, 0, :], in0=rt[0][:], in1=rt[1][:], op=ALU.subtract, reads=["rt0", "rt1"], writes=["kro0"])
                        S.op("pool", "tensor_tensor", out=kro[:, :, 1, :], in0=rt[2][:], in1=rt[3][:], op=ALU.add, reads=["rt2", "rt3"], writes=["kro1"])
                        krov = kro[:].rearrange("p h two i -> p h (two i)")
                        gkv = GK[:].rearrange("p (h d) -> p h d", h=8)
                        S.op("pool", "tensor_tensor", out=k2[b2][:, :, 0, :], in0=krov, in1=gkv, op=ALU.mult, reads=["kro0", "kro1", "GK"], writes=[f"k2_{b2}a"])
                        S.op("dve", "tensor_tensor", out=k2[b2][:, :, 1, :], in0=krov, in1=gkv, op=ALU.mult, reads=["kro0", "kro1", "GK"], writes=[f"k2_{b2}b"])
                        S.op("act", "copy", out=vb[b2][:], in_=psZ[3][:, :], reads=["psZ3"], writes=[f"vb{b2}"])
                        if p in own_of_slot:
                            k, j = own_of_slot[p]
                            ti = 5 * k + j
                            S.dma("sp", out=krv_d.ap()[ti, :, 0:1024], in_=k2[b2][:].rearrange("p h a d -> p (h a d)"), reads=[f"k2_{b2}a", f"k2_{b2}b"], writes=[f"krv{ti}a"])
                            S.dma("sp", out=krv_d.ap()[ti, :, 1024:1536], in_=vb[b2][:], reads=[f"vb{b2}"], writes=[f"krv{ti}b"])
                            if j == 0:
                                S.dma("sp", out=snap_d.ap()[k], in_=Sst[:], reads=["Sst"])
                        for h in range(8):
                            S.op("pe", "matmul", out=psU[:, h * 64:(h + 1) * 64], lhsT=k2[b2][:, h, :, :].rearrange("p a d -> p (a d)"), rhs=vb[b2][:, h * 64:(h + 1) * 64],
                                 start=True, stop=True, reads=[f"k2_{b2}a", f"k2_{b2}b", f"vb{b2}"], writes=["psU"])
                        S.op("dve", "scalar_tensor_tensor", out=tmpS[:], in0=psU[:, :], scalar=VALID[:, p:p + 1], in1=Sst[:], op0=ALU.mult, op1=ALU.add,
                             reads=["psU", "VALID", "Sst"], writes=["tmpS"])
                        S.op("pool", "tensor_tensor", out=Sst[:], in0=tmpS[:], in1=DC[:], op=ALU.mult, reads=["tmpS", "DC"], writes=["Sst"])

                    if p % BT == BT - 1:
                        bi = p // BT
                        NBc = 8 * BT
                        c4k = [f"CT4_{bi % 2}_{s}" for s in range(BT)]
                        for g in range(2):
                            bank, bkey = (psC, "psC") if g == 0 else (psU, "psU")
                            for kv in range(2):
                                for hl in range(2):
                                    col0 = (kv * 2 + hl) * NBc
                                    for l in range(16):
                                        S.op("pe", "matmul", out=bank[:, col0:col0 + NBc].rearrange("p (s x) -> p s x", s=BT), lhsT=W1[kv][64 * g:64 * g + 64, hl * 16 + l, :],
                                             rhs=c4[64 * g:64 * g + 64, :, kv, l::16], start=(l == 0), stop=(l == 15),
                                             reads=c4k + [f"W1_{kv}"], writes=[bkey])
                        for g in range(2):
                            bank, bkey = (psC, "psC") if g == 0 else (psU, "psU")
                            pc = bank[:, 0:4 * NBc].rearrange("p (kv hl n) -> p kv hl n", kv=2, hl=2)
                            S.op("dve", "tensor_copy", out=PRE[:, g::2, NBc * bi + 1:NBc * bi + NBc + 1], in_=pc[:, :, 0, :], reads=[bkey, "PRE"], writes=["PRE"])
                            S.op("dve", "tensor_tensor", out=PRE[:, g::2, NBc * bi:NBc * bi + NBc], in0=PRE[:, g::2, NBc * bi:NBc * bi + NBc], in1=pc[:, :, 1, :], op=ALU.add,
                                 reads=[bkey, "PRE"], writes=["PRE"])
                if smp is None:
                    S.dma("sp", out=o_ret.ap()[:, :], in_=Sst[:], reads=["Sst"])
                else:
                    S.op("pool", "memset", ap=KsT[:, NPG * 128:(NPG + 1) * 128], constant=0.0, writes=[f"KsT{NPG}"])
                    S.op("pool", "memset", ap=Vs[:, NPG, :, 0:64], constant=0.0, reads=["Vs"], writes=[f"Vs{NPG}"])
                    S.dma("pool", out=KsT[:, NPG * 128:NPG * 128 + 1], in_=skv_d.ap()[smp:smp + 1, 256:384].rearrange("o c -> c o"),
                          reads=[f"KsT{NPG}"], writes=[f"KsT{NPG}x"], allow_slow_non_contiguous=True)
                    S.dma("pool", out=Vs[0:1, NPG, :, 0:64], in_=skv_d.ap()[smp:smp + 1, 384:512].rearrange("o (g d) -> o g d", g=2),
                          reads=[f"Vs{NPG}"], writes=[f"Vs{NPG}x"])
                    for c in range(5):
                        b2 = c % 2
                        wt = xb[b2]
                        if c < 4:
                            S.dma("sp", out=wt[:, 0:256], in_=swin.ap()[smp, 128 * c:128 * (c + 1), :], writes=[f"xb{b2}"])
                            if c == 0:
                                S.dma("sp", out=o_swin.ap()[smp, 0:127, :], in_=wt[1:128, 0:256], reads=[f"xb{b2}"])
                            else:
                                S.dma("sp", out=o_swin.ap()[smp, 128 * c - 1:128 * c + 127, :], in_=wt[:, 0:256], reads=[f"xb{b2}"])
                        else:
                            S.op("dve", "memset", ap=wt[:, 0:256], constant=0.0, writes=[f"xb{b2}"])
                            S.dma("sp", out=wt[0:1, 0:256], in_=skv_d.ap()[smp:smp + 1, 512:768], reads=[f"xb{b2}"], writes=[f"xb{b2}r"])
                            S.dma("sp", out=o_swin.ap()[smp, 511:512, :], in_=wt[0:1, 0:256], reads=[f"xb{b2}r"])
                        S.op("dve", "tensor_copy", out=kvb[b2][:, 512:768], in_=wt[:, 0:256], reads=[f"xb{b2}", f"xb{b2}r"], writes=[f"kvb{b2}b"])
                        S.op("pe", "transpose", out=psT2[:, 384:512], in_=kvb[b2][:, 512:640], identity=ident_b[:], reads=[f"kvb{b2}b", "ident_b"], writes=["psT2"])
                        S.op("act", "copy", out=kwt[b2][:], in_=psT2[:, 384:512], reads=["psT2"], writes=[f"kwt{b2}"])
                        S.dma("sp", out=kw_d.ap()[NWIN + c], in_=kwt[b2][:], reads=[f"kwt{b2}"])
                        S.dma("sp", out=vw_d.ap()[NWIN + c], in_=kvb[b2][:, 640:768], reads=[f"kvb{b2}b"])

                S.flush()
                ws.close()
                NBK = NSL * 8
                NCBx = NCB if smp is None else NCB_S
                pet = SB(ph, "pet", [128, 2], F32)
                u_ = SB(ph, "u_", [128, NBK], F32)
                w_ = SB(ph, "w_", [128, NBK], F32)
                hid = SB(ph, "hid", [128, 4, NCBx * 128], BF16)
                for kv in range(2):
                    for l in range(32):
                        S.op("pe", "matmul", out=psC[:, 300 + kv:301 + kv], lhsT=W1[kv][0:64, l, :], rhs=peT[0:64, kv, l:l + 1], start=(l == 0), stop=(l == 31),
                             reads=[f"W1_{kv}", f"peT{kv}"], writes=["psC"])
                S.op("dve", "tensor_copy", out=pet[:], in_=psC[:, 300:302], reads=["psC"], writes=["pet"])
                S.op("pool", "memset", ap=hid[:], constant=0.0, writes=["hid"])
                for q in range(4):
                    S.op("dve", "tensor_scalar", out=u_[:], in0=PRE[:, q, 1:NBK + 1], scalar1=pet[:, q // 2:q // 2 + 1], scalar2=None, op0=ALU.add,
                         reads=["PRE", "pet"], writes=["u_"])
                    S.op("pool", "tensor_tensor", out=w_[:], in0=u_[:], in1=u_[:], op=ALU.mult, reads=["u_"], writes=["w_"])
                    S.op("dve", "tensor_scalar", out=w_[:], in0=w_[:], scalar1=0.044715, scalar2=1.0, op0=ALU.mult, op1=ALU.add, reads=["w_"], writes=["w_"])
                    S.op("pool", "tensor_tensor", out=w_[:], in0=w_[:], in1=u_[:], op=ALU.mult, reads=["u_", "w_"], writes=["w_"])
                    S.op("act", "activation", out=w_[:], in_=w_[:], func=AF.Sigmoid, scale=1.5957691216057308, reads=["w_"], writes=["w_"])
                    S.op("dve", "tensor_tensor", out=hid[:, q, 0:NBK], in0=w_[:], in1=u_[:], op=ALU.mult, reads=["u_", "w_", "hid"], writes=["hid"])
                for cb in range(NCBx):
                    for g in range(2):
                        S.op("pe", "matmul", out=psZ[0][:, 0:128], lhsT=W2[:, 0, g, :], rhs=hid[:, 0 * 2 + g, cb * 128:(cb + 1) * 128], start=(g == 0), stop=(g == 1),
                             reads=["hid", "W2_00", "W2_01"], writes=["psZ0"])
                    S.op("act", "copy", out=kcmpT[:, cb * 128:(cb + 1) * 128], in_=psZ[0][:, 0:128], reads=["psZ0"], writes=["kcmpT"])
                    for g in range(2):
                        S.op("pe", "matmul", out=psZ[1][:, 64 * g:64 * g + 64], lhsT=hid[:, 2 + g, cb * 128:(cb + 1) * 128], rhs=W2[:, 1, g, 64 * g:64 * g + 64],
                             start=True, stop=True, reads=["hid", "W2_10", "W2_11"], writes=["psZ1"])
                    S.op("dve", "tensor_copy", out=vcmp[:, cb, :, 0:64], in_=psZ[1][:, 0:128].rearrange("p (g d) -> p g d", g=2), reads=["psZ1", "vcmp"],
                         writes=[f"vcmp{cb}"])
                if debug and smp is None:
                    dk = SB(ph, "dk", [128, NCB * 128], F32)
                    dv = SB(ph, "dv", [128, NCB * 130], F32)
                    S.op("dve", "tensor_copy", out=dk[:], in_=kcmpT[:, 0:NCB * 128], reads=["kcmpT"], writes=["dk"])
                    S.op("dve", "tensor_copy", out=dv[:], in_=vcmp[:, 0:NCB].rearrange("p c g d -> p (c g d)"), reads=[f"vcmp{cb}" for cb in range(NCB)], writes=["dv"])
                    S.dma("sp", out=dbg["kcmpT"].ap()[:, :], in_=dk[:], reads=["dk"])
                    S.dma("sp", out=dbg["vcmp"].ap()[:, :], in_=dv[:], reads=["dv"])
                S.flush()


        phase_1a(None)
        if "1b" not in phases:
            return nc

        with ExitStack() as ph:
            Wq = SB(ph, "Wq", [128, KC, W_Q_COLS], BF16)
            GQf = SB(ph, "GQf", [128, 512], F32)
            QTz = [SB(ph, f"QTq{g}", [128, 4, 128], BF16) for g in range(2)]
            xq = SB(ph, "xq", [128, D], F32)
            t1 = SB(ph, "t1b", [128, D], F32)
            st4 = SB(ph, "st4b", [128, 4], F32)
            hnq = SB(ph, "hnq", [128, D], BF16)
            hnqT = SB(ph, "hnqT", [128, KC, 128], BF16)
            gates = SB(ph, "gatesq", [128, 24], F32)
            csq = SB(ph, "csq", [128, 2, 32], F32)
            rt = [SB(ph, f"rtb{i}", [128, 8, 32], F32) for i in range(4)]
            qro = SB(ph, "qro", [128, 8, 2, 32], F32)
            qt_ = SB(ph, "qt_", [128, 512], BF16)
            qT = SB(ph, "qTq", [128, 4, 128], BF16)
            sg = SB(ph, "sgq", [128, 512], F32)
            Wkv = SB(ph, "Wkvq", [128, KC, W_KV_COLS], BF16)
            A1s = SB(ph, "A1s", [NSMP, D], F32)
            SH1s = SB(ph, "SH1s", [NSMP, D], F32)
            GQ0 = SB(ph, "GQ0", [NSMP, 512], F32)
            GK0 = SB(ph, "GK0", [NSMP, 512], F32)
            kvf_s = SB(ph, "kvf_s", [NSMP, 768], F32)
            k2s = SB(ph, "k2s", [NSMP, 8, 2, 64], BF16)
            vs_ = SB(ph, "vs_", [NSMP, 512], BF16)
            psT = PS(ph, "psTq", [128, 1024], BF16)
            psQ = PS(ph, "psQq", [128, 512], F32)
            psZa = PS(ph, "psZaq", [128, 512], F32)
            psZb = PS(ph, "psZbq", [128, 512], F32)
            psX = PS(ph, "psXq", [128, 512], F32)
            for kc in range(KC):
                S.dma("pool", out=Wq[:, kc, :], in_=w_q.ap()[kc * 128:(kc + 1) * 128, :], writes=["Wq"])
            S.dma("sp", out=GQf[:], in_=cn["c_gq"].ap()[:, :], writes=["GQf"])
            GQh = GQf
            items = [(32 * k + 27 + j, 5 * k + j, 128, None) for k in range(KSEG) for j in range(5)]
            if "s" in phases:
                items.append((None, TS, NSMP, "s"))
                for kc in range(KC):
                    S.dma("pool", out=Wkv[:, kc, :], in_=w_kv.ap()[kc * 128:(kc + 1) * 128, :], writes=["Wkvq"])
                S.dma("sp", out=A1s[:], in_=modv_d.ap()[0:NSMP, 0:D], writes=["A1s"])
                S.dma("sp", out=SH1s[:], in_=modv_d.ap()[0:NSMP, D:2 * D], writes=["SH1s"])
                S.dma("sp", out=GQ0[:], in_=cn["c_gq"].ap()[0:1, :].partition_broadcast(NSMP), writes=["GQ0"])
                S.dma("sp", out=GK0[:], in_=cn["c_gk"].ap()[0:1, :].partition_broadcast(NSMP), writes=["GK0"])
            for (p, ti, T, kind) in items:
                if True:
                    tq0, j = 0, 1
                    N = 4 * T
                    if kind is None:
                        x_src = xp.ap()[p * 128:p * 128 + T, :]
                        cos_src = cn["c_cos"].ap()[p * 128:p * 128 + T, :]
                        sin_src = cn["c_sin"].ap()[p * 128:p * 128 + T, :]
                        A_t, SH_t, GQ_t = A1bc, SH1bc, GQf
                    else:
                        x_src = xs.ap()[:, :]
                        cos_src = cn["c_cs_s"].ap()[0:1, :].partition_broadcast(T)
                        sin_src = cn["c_cs_s"].ap()[1:2, :].partition_broadcast(T)
                        A_t, SH_t, GQ_t = A1s, SH1s, GQ0
                    S.dma("sp", out=xq[0:T, :], in_=x_src, writes=["xq"])
                    S.dma("act", out=csq[0:T, 0, :], in_=cos_src, writes=["csq0"])
                    S.dma("act", out=csq[0:T, 1, :], in_=sin_src, writes=["csq1"])
                    S.op("act", "activation", out=t1[0:T, :], in_=xq[0:T, :], func=AF.Square, accum_out=st4[0:T, 0:1], reads=["xq"], writes=["t1", "st4"])
                    S.op("dve", "tensor_scalar", out=st4[0:T, 1:2], in0=st4[0:T, 0:1], scalar1=1.0 / D, scalar2=RMS_EPS, op0=ALU.mult, op1=ALU.add,
                         reads=["st4"], writes=["st4"])
                    S.op("act", "activation", out=st4[0:T, 2:3], in_=st4[0:T, 1:2], func=AF.Sqrt, reads=["st4"], writes=["st4"])
                    S.op("dve", "reciprocal", out=st4[0:T, 3:4], in_=st4[0:T, 2:3], reads=["st4"], writes=["st4"])
                    S.op("dve", "scalar_tensor_tensor", out=t1[0:T, :], in0=xq[0:T, :], scalar=st4[0:T, 3:4], in1=A_t[0:T, :], op0=ALU.mult, op1=ALU.mult,
                         reads=["xq", "st4", "A1bc", "A1s"], writes=["t1"])
                    S.op("pool", "tensor_tensor", out=hnq[0:T, :], in0=t1[0:T, :], in1=SH_t[0:T, :], op=ALU.add, reads=["t1", "SH1bc", "SH1s"], writes=["hnq"])
                    for kc in range(KC):
                        S.op("pe", "transpose", out=psT[:, kc * 128:kc * 128 + T], in_=hnq[0:T, kc * 128:(kc + 1) * 128], identity=ident_b[0:T, 0:T],
                             reads=["hnq", "ident_b"], writes=["psT"])
                    S.op("act", "copy", out=hnqT[:, :, 0:T], in_=psT[:, :].rearrange("p (k t) -> p k t", k=KC)[:, :, 0:T], reads=["psT"], writes=["hnqT"])
                    for c in range(4):
                        for kc in range(KC):
                            S.op("pe", "matmul", out=psQ[:, c * T:(c + 1) * T], lhsT=Wq[:, kc, c * 128:(c + 1) * 128], rhs=hnqT[:, kc, 0:T],
                                 start=(kc == 0), stop=(kc == KC - 1), reads=["Wq", "hnqT"], writes=["psQ"])
                    for g in range(2):
                        S.op("act", "activation", out=QTz[g][64 * g:64 * g + 64, :, 0:T], in_=psQ[64 * g:64 * g + 64, 0:N].rearrange("p (c t) -> p c t", c=4),
                             func=AF.Copy, scale=0.125, reads=["psQ", f"QTz{g}"], writes=[f"QTz{g}"])
                    for (bank, bkey, c0, w) in ((psX, "psX", 512, 24), (psZa, "psZa", 536, 512), (psZb, "psZb", 1048, 512)):
                        for kc in range(KC):
                            S.op("pe", "matmul", out=bank[0:T, 0:w], lhsT=hnqT[:, kc, 0:T], rhs=Wq[:, kc, c0:c0 + w], start=(kc == 0), stop=(kc == KC - 1),
                                 reads=["Wq", "hnqT"], writes=[bkey])
                    S.op("act", "activation", out=gates[0:T, :], in_=psX[0:T, 0:24], func=AF.Sigmoid, reads=["psX"], writes=["gates"])
                    S.op("act", "activation", out=sg[0:T, :], in_=psZb[0:T, :], func=AF.Silu, reads=["psZb"], writes=["sg"])
                    qv = psZa[0:T, :].rearrange("p (h two i) -> p h two i", h=8, two=2)
                    cs_ = csq[0:T, 0, :].unsqueeze(1).to_broadcast([T, 8, 32])
                    sn_ = csq[0:T, 1, :].unsqueeze(1).to_broadcast([T, 8, 32])
                    S.op("dve", "tensor_tensor", out=rt[0][0:T], in0=qv[:, :, 0, :], in1=cs_, op=ALU.mult, reads=["psZa", "csq0"], writes=["rt0"])
                    S.op("dve", "tensor_tensor", out=rt[1][0:T], in0=qv[:, :, 1, :], in1=sn_, op=ALU.mult, reads=["psZa", "csq1"], writes=["rt1"])
                    S.op("dve", "tensor_tensor", out=rt[2][0:T], in0=qv[:, :, 0, :], in1=sn_, op=ALU.mult, reads=["psZa", "csq1"], writes=["rt2"])
                    S.op("dve", "tensor_tensor", out=rt[3][0:T], in0=qv[:, :, 1, :], in1=cs_, op=ALU.mult, reads=["psZa", "csq0"], writes=["rt3"])
                    S.op("pool", "tensor_tensor", out=qro[0:T, :, 0, :], in0=rt[0][0:T], in1=rt[1][0:T], op=ALU.subtract, reads=["rt0", "rt1"], writes=["qro0"])
                    S.op("pool", "tensor_tensor", out=qro[0:T, :, 1, :], in0=rt[2][0:T], in1=rt[3][0:T], op=ALU.add, reads=["rt2", "rt3"], writes=["qro1"])
                    S.op("pool", "tensor_tensor", out=qt_[0:T, :], in0=qro[0:T].rearrange("p h a i -> p (h a i)"), in1=GQ_t[0:T, :], op=ALU.mult,
                         reads=["qro0", "qro1", "GQf", "GQ0"], writes=["qt_"])
                    for c in range(4):
                        S.op("pe", "transpose", out=psT[:, c * 128:c * 128 + T], in_=qt_[0:T, c * 128:(c + 1) * 128], identity=ident_b[0:T, 0:T],
                             reads=["qt_", "ident_b"], writes=["psT"])
                    S.op("act", "copy", out=qT[:, :, 0:T], in_=psT[:, 0:512].rearrange("p (c t) -> p c t", c=4)[:, :, 0:T], reads=["psT"], writes=["qT"])
                    S.dma("sp", out=qt_d.ap()[ti, 0:64, :].rearrange("p (c t) -> p c t", c=4), in_=QTz[0][0:64, :, :], reads=["QTz0"], writes=[f"qt_d{ti}"])
                    S.dma("sp", out=qt_d.ap()[ti, 64:128, :].rearrange("p (c t) -> p c t", c=4), in_=QTz[1][64:128, :, :], reads=["QTz1"], writes=[f"qt_d{ti}"])
                    S.dma("sp", out=qrt_d.ap()[ti].rearrange("p (c t) -> p c t", c=4), in_=qT[:, :, :], reads=["qT"], writes=[f"qrt_d{ti}"])
                    S.dma("sp", out=gates_d.ap()[ti], in_=gates[:, :], reads=["gates"], writes=[f"gates_d{ti}"])
                    S.dma("sp", out=sg_d.ap()[ti], in_=sg[:, :], reads=["sg"], writes=[f"sg_d{ti}"])
                    if kind == "s":
                        for (bank, bkey, c0, w) in ((psZa, "psZa", 0, 512), (psZb, "psZb", 512, 256)):
                            for kc in range(KC):
                                S.op("pe", "matmul", out=bank[0:T, 0:w], lhsT=hnqT[:, kc, 0:T], rhs=Wkv[:, kc, c0:c0 + w], start=(kc == 0), stop=(kc == KC - 1),
                                     reads=["Wkvq", "hnqT"], writes=[bkey])
                        S.op("dve", "tensor_copy", out=kvf_s[0:T, 0:512], in_=psZa[0:T, 0:512], reads=["psZa"], writes=["kvf_sa"])
                        S.op("dve", "tensor_copy", out=kvf_s[0:T, 512:768], in_=psZb[0:T, 0:256], reads=["psZb"], writes=["kvf_sb"])
                        S.dma("sp", out=o_skv.ap()[:, :], in_=kvf_s[:, :], reads=["kvf_sa", "kvf_sb"])
                        S.dma("sp", out=skv_d.ap()[:, :], in_=kvf_s[:, :], reads=["kvf_sa", "kvf_sb"], writes=["skv_d"])
                        for (bank, bkey, c0, w) in ((psZa, "psZa", 768, 512), (psZb, "psZb", 1280, 512)):
                            for kc in range(KC):
                                S.op("pe", "matmul", out=bank[0:T, 0:w], lhsT=hnqT[:, kc, 0:T], rhs=Wkv[:, kc, c0:c0 + w], start=(kc == 0), stop=(kc == KC - 1),
                                     reads=["Wkvq", "hnqT"], writes=[bkey])
                        S.op("act", "copy", out=vs_[0:T, :], in_=psZb[0:T, :], reads=["psZb"], writes=["vs_"])
                        kq = psZa[0:T, :].rearrange("p (h two i) -> p h two i", h=8, two=2)
                        S.op("dve", "tensor_tensor", out=rt[0][0:T], in0=kq[:, :, 0, :], in1=cs_, op=ALU.mult, reads=["psZa", "csq0"], writes=["rt0"])
                        S.op("dve", "tensor_tensor", out=rt[1][0:T], in0=kq[:, :, 1, :], in1=sn_, op=ALU.mult, reads=["psZa", "csq1"], writes=["rt1"])
                        S.op("dve", "tensor_tensor", out=rt[2][0:T], in0=kq[:, :, 0, :], in1=sn_, op=ALU.mult, reads=["psZa", "csq1"], writes=["rt2"])
                        S.op("dve", "tensor_tensor", out=rt[3][0:T], in0=kq[:, :, 1, :], in1=cs_, op=ALU.mult, reads=["psZa", "csq0"], writes=["rt3"])
                        S.op("pool", "tensor_tensor", out=qro[0:T, :, 0, :], in0=rt[0][0:T], in1=rt[1][0:T], op=ALU.subtract, reads=["rt0", "rt1"], writes=["qro0"])
                        S.op("pool", "tensor_tensor", out=qro[0:T, :, 1, :], in0=rt[2][0:T], in1=rt[3][0:T], op=ALU.add, reads=["rt2", "rt3"], writes=["qro1"])
                        gkv = GK0[0:T, :].rearrange("p (h d) -> p h d", h=8)
                        krov = qro[0:T].rearrange("p h two i -> p h (two i)")
                        S.op("pool", "tensor_tensor", out=k2s[0:T, :, 0, :], in0=krov, in1=gkv, op=ALU.mult, reads=["qro0", "qro1", "GK0"], writes=["k2sa"])
                        S.op("dve", "tensor_tensor", out=k2s[0:T, :, 1, :], in0=krov, in1=gkv, op=ALU.mult, reads=["qro0", "qro1", "GK0"], writes=["k2sb"])
                        S.dma("sp", out=krv_d.ap()[TS, 0:T, 0:1024], in_=k2s[0:T].rearrange("p h a d -> p (h a d)"), reads=["k2sa", "k2sb"], writes=["krvs"])
                        S.dma("sp", out=krv_d.ap()[TS, 0:T, 1024:1536], in_=vs_[0:T, :], reads=["vs_"], writes=["krvs2"])
            S.flush()
        def phase_1b(smp):
            NBv = NB if smp is None else NB_S
            NPv = NBv // 128
            NCBv = NCB if smp is None else NCB_S
            nmc_s = (8 * NPG - 2) // 128 + 1
            with ExitStack() as ph:
                IND = SB(ph, "IND", [128, 8192], BF16)
                GS = SB(ph, "GS", [128, 8, 8, 128], BF16)
                GW = SB(ph, "GW", [128, 5, 8, 128], BF16)
                if smp is None:
                    rc_list = sorted({(32 * k + 27 + j) - 16 * m for k in range(KSEG) for j in range(5)
                                      for m in range(((8 * (32 * k + 27 + j) + 7) + 127) // 128) if (32 * k + 27 + j) - 16 * m < 23})
                else:
                    rc_list = sorted({NPG - 16 * m for m in range(nmc_s) if NPG - 16 * m < 23})
                rc_idx = {r: i for i, r in enumerate(rc_list)}
                GC = SB(ph, "GC", [128, max(len(rc_list), 1), 8, 128], BF16)
                CHT = SB(ph, "CHT", [128, 8, 128], BF16)
                CH8 = SB(ph, "CH8", [128, 8], F32)
                AM = SB(ph, "AM", [128, NCBv, NBv], BF16)
                TRI8 = SB(ph, "TRI8", [128, 4, 128], BF16)
                GNG = SB(ph, "GNG", [128, 512], F32)
                PMK = SB(ph, "PMK", [128, 512], F32)
                CMPB = SB(ph, "CMPB", [128, NCB], F32)
                WINB = SB(ph, "WINB", [128, NWIN + 5], F32)
                VALID = SB(ph, "VALIDb", [128, max(NS, NPG + 1)], F32)
                DC = SB(ph, "DCb", [128, 512], F32)
                Sst = SB(ph, "Sstb", [128, 512], F32)
                Sblk = SB(ph, "Sblk", [128, 512], BF16)
                QTz = [SB(ph, f"QTz{g}", [128, 4, 128], BF16) for g in range(2)]
                MST = SB(ph, "MST", [128, NPv, 4, 128], BF16)
                PTc = [SB(ph, f"PTc{i}", [128, 4, 128], BF16) for i in range(NCBv)]
                PT = [SB(ph, f"PT{i}", [128, 4, 128], BF16) for i in range(4)]
                KW = SB(ph, "KW", [128, 5, 128], BF16)
                VW = SB(ph, "VW", [128, 5, 2, 65], BF16)
                gates = SB(ph, "gates", [128, 24], F32)
                qT = SB(ph, "qT", [128, 4, 128], BF16)
                sg = SB(ph, "sg", [128, 512], F32)
                k2 = SB(ph, "k2b", [128, 8, 2, 64], BF16)
                vbk = SB(ph, "vbk", [128, 512], BF16)
                kT = SB(ph, "kT", [128, 8, 128], BF16)
                attm = SB(ph, "attm", [128, 8, 128], BF16)
                sq_ = SB(ph, "sq_", [128, 512], F32)
                gst = SB(ph, "gst", [128, 8, 6], F32)
                on_ = SB(ph, "on_", [128, 512], F32)
                oret = SB(ph, "oret", [128, 512], BF16)
                tmpS = SB(ph, "tmpSb", [128, 512], F32)
                OT = SB(ph, "OT", [65, 512], F32)
                Ob = [SB(ph, f"Ob{i}", [128, 4, 65], F32) for i in range(3)]
                coef = SB(ph, "coef", [128, 3, 4], F32)
                rz = SB(ph, "rz", [128, 3, 4], F32)
                impa = SB(ph, "impa", [128, NBv], F32)
                FS = SB(ph, "FS", [128, NBv], F32)
                wk = SB(ph, "wk", [128, NBv], F32)
                m8 = SB(ph, "m8", [128, 16], F32)
                Mneg = SB(ph, "Mneg", [128, NBv], BF16)
                onsa = SB(ph, "onsa", [128, 8, 64], F32)
                onsb = SB(ph, "onsb", [128, 512], BF16)
                omT = SB(ph, "omT", [128, 8, 128], BF16)
                tabx = SB(ph, "tabx", [33, 8], F32)
                ohb = sq_
                bvr = on_
                psT = PS(ph, "psTb", [128, 1024], BF16)
                psQ = PS(ph, "psQ", [128, 512], F32)
                psZa = PS(ph, "psZa", [128, 512], F32)
                psZb = PS(ph, "psZb", [128, 512], F32)
                psS = [PS(ph, f"psS{i}", [128, 512], F32) for i in range(2)]
                psO = PS(ph, "psO", [128, 512], F32)
                psX = PS(ph, "psX", [128, 512], F32)

                for i in range(4):
                    S.dma("pool", out=IND[:, i * 2048:(i + 1) * 2048], in_=cn["c_ind"].ap()[:, i * 2048:(i + 1) * 2048], writes=["IND"])
                for cb in range(NCBv):
                    S.dma("pool", out=AM[:, cb, :], in_=cn["c_amat" if smp is None else "c_amat_s"].ap()[:, cb, :], writes=["AM"])
                S.dma("pool", out=TRI8[:].rearrange("p h n -> p (h n)"), in_=cn["c_tri8"].ap()[:, 0:512], writes=["TRI8"])
                S.dma("sp", out=GNG[:], in_=gng.ap()[0:1, :].partition_broadcast(128), writes=["GNG"])
                S.dma("sp", out=PMK[:], in_=cn["c_pmk"].ap()[:, :], writes=["PMK"])
                if smp is None:
                    S.dma("sp", out=CMPB[:], in_=cn["c_cmpb"].ap()[:, :], writes=["CMPB"])
                    S.dma("sp", out=WINB[:, 0:NWIN], in_=cn["c_winb"].ap()[:, :], writes=["WINB"])
                    S.dma("sp", out=VALID[:, 0:NS], in_=cn["c_valid"].ap()[:, :], writes=["VALID"])
                    S.dma("sp", out=DC[:], in_=cn["c_dc"].ap()[:, :], writes=["DC"])
                else:
                    S.op("dve", "memset", ap=CMPB[:], constant=0.0, writes=["CMPB"])
                    S.op("dve", "memset", ap=WINB[:], constant=0.0, writes=["WINB"])
                    S.op("dve", "memset", ap=VALID[:], constant=1.0, writes=["VALID"])
                    S.dma("sp", out=DC[:], in_=cn["c_dc1"].ap()[:, :], writes=["DC"])
                S.dma("sp", out=CH8[:], in_=table.ap()[31:32, :].partition_broadcast(128), writes=["CH8"])
                S.op("dve", "tensor_copy", out=CHT[:], in_=CH8[:].unsqueeze(2).to_broadcast([128, 8, 128]), reads=["CH8"], writes=["CHT"])
                for g in range(2):
                    S.op("pool", "memset", ap=QTz[g][:], constant=0.0, writes=[f"QTz{g}"])
                S.op("pool", "memset", ap=VW[:], constant=1.0, writes=["VW"])
                S.op("dve", "memset", ap=tabx[:], constant=-BIG, writes=["tabx"])
                S.dma("sp", out=tabx[0:32, :], in_=table.ap()[:, :], reads=["tabx"], writes=["tabx"])
                for (ohd, bvd, xlen) in ((cn["c_ohs"], bvs_d, XS_LEN), (cn["c_ohw"], bvw_d, XW_LEN), (cn["c_ohc"], bvc_d, XC_LEN)):
                    for x0 in range(0, xlen, 512):
                        w = min(512, xlen - x0)
                        S.dma("sp", out=ohb[0:33, 0:w], in_=ohd.ap()[:, x0:x0 + w], writes=["ohb"])
                        S.op("pe", "matmul", out=psQ[0:8, 0:w], lhsT=tabx[:, :], rhs=ohb[0:33, 0:w], start=True, stop=True, reads=["tabx", "ohb"], writes=["psQ"])
                        S.op("dve", "tensor_copy", out=bvr[0:8, 0:w], in_=psQ[0:8, 0:w], reads=["psQ"], writes=["bvr"])
                        S.dma("sp", out=bvd.ap()[:, x0:x0 + w], in_=bvr[0:8, 0:w], reads=["bvr"], writes=["bv_d"])
                for h in range(8):
                    S.dma("pool", out=GS[:, :, h, :], in_=bass.AP(bvs_d, h * XS_LEN, [[1, 128], [128, 8], [1, 128]]), reads=["bv_d"], writes=["GS"])
                    S.dma("pool", out=GW[:, :, h, :], in_=bass.AP(bvw_d, h * XW_LEN, [[1, 128], [128, 5], [1, 128]]), reads=["bv_d"], writes=["GW"])
                    for r_, i_ in rc_idx.items():
                        S.dma("pool", out=GC[:, i_, h, :], in_=bass.AP(bvc_d, h * XC_LEN + 128 * r_, [[16, 128], [1, 128]]), reads=["bv_d"], writes=["GC"])

                if smp is None:
                    tiles = [(k, j, 32 * k + 27 + j, 5 * k + j) + ((2, 126, 126) if j == 0 else (128, 0, 0)) for k in range(KSEG) for j in range(5)]
                else:
                    tiles = [(0, 1, NPG, TS, 1, 0, smp)]
                for (k, j, p, ti, T, tq0, tsr) in tiles:
                    if smp is None and j == 0:
                        S.dma("sp", out=Sst[:], in_=snap_d.ap()[k], writes=["Sst"])
                    if smp is not None:
                        for hf in range(2):
                            S.dma("sp", out=Sst[64 * hf:64 * hf + 64, :].rearrange("p (h e) -> p h e", h=8), in_=sret.ap()[smp].rearrange("h d e -> d h e"), writes=[f"Sst_{hf}"])
                    if True:
                        N = 4 * T
                        if smp is None:
                            S.dma("sp", out=FS[0:T, :], in_=cn["c_fsel"].ap()[ti, tq0:tq0 + T, :], writes=["FS"])
                            S.dma("sp", out=k2[:].rearrange("p h a d -> p (h a d)"), in_=krv_d.ap()[ti, :, 0:1024], writes=["k2"])
                            S.dma("sp", out=vbk[:], in_=krv_d.ap()[ti, :, 1024:1536], writes=["vbk"])
                            wi0 = cfg.win_index(p - 4)
                        else:
                            S.dma("sp", out=FS[0:T, :], in_=cn["c_fsel_s"].ap()[0:1, :], writes=["FS"])
                            S.op("pool", "memset", ap=k2[:], constant=0.0, writes=["k2"])
                            S.op("pool", "memset", ap=vbk[:], constant=0.0, writes=["vbk"])
                            S.dma("sp", out=k2[0:1].rearrange("p h a d -> p (h a d)"), in_=krv_d.ap()[TS, smp:smp + 1, 0:1024], reads=["k2"], writes=["k2r"])
                            S.dma("sp", out=vbk[0:1, :], in_=krv_d.ap()[TS, smp:smp + 1, 1024:1536], reads=["vbk"], writes=["vbkr"])
                            wi0 = NWIN
                        S.dma("sp", out=KW[:], in_=kw_d.ap()[wi0:wi0 + 5].rearrange("w p c -> p w c"), writes=["KW"])
                        for g_ in range(2):
                            S.dma("sp", out=VW[:, :, g_, 0:64], in_=vw_d.ap()[wi0:wi0 + 5, :, 64 * g_:64 * g_ + 64].rearrange("w p d -> p w d"), reads=["VW"], writes=[f"VWd{g_}"])
                        for g_ in range(2):
                            S.dma("sp", out=QTz[g_][64 * g_:64 * g_ + 64, :, 0:T], in_=qt_d.ap()[ti, 64 * g_:64 * g_ + 64, :].rearrange("p (c t) -> p c t", c=4)[:, :, tsr:tsr + T],
                                  reads=[f"QTz{g_}"], writes=[f"QTz{g_}"], allow_slow_non_contiguous=True)
                        S.dma("sp", out=qT[:, :, 0:T], in_=qrt_d.ap()[ti].rearrange("p (c t) -> p c t", c=4)[:, :, tsr:tsr + T], writes=["qT"], allow_slow_non_contiguous=True)
                        S.dma("sp", out=gates[0:T, :], in_=gates_d.ap()[ti, tsr:tsr + T, :], writes=["gates"])
                        S.dma("sp", out=sg[0:T, :], in_=sg_d.ap()[ti, tsr:tsr + T, :], writes=["sg"])
                        for h in range(8):
                            S.op("pe", "transpose", out=psT[:, h * 128:(h + 1) * 128], in_=k2[:, h, :, :].rearrange("p a d -> p (a d)"), identity=ident_b[:],
                                 reads=["k2", "k2r", "ident_b"], writes=["psT"])
                        S.op("act", "copy", out=kT[:].rearrange("p h m -> p (h m)"), in_=psT[:, :], reads=["psT"], writes=["kT"])
                        for h in range(8):
                            b = h % 2
                            bank, bkey = (psZa, "psZa") if b == 0 else (psZb, "psZb")
                            S.op("pe", "matmul", out=bank[:, (h // 2) * T:(h // 2 + 1) * T], lhsT=kT[64 * b:64 * b + 64, h, :], rhs=qT[64 * b:64 * b + 64, h // 2, 0:T],
                                 start=True, stop=True, reads=["kT", "qT"], writes=[bkey])
                        for b in range(2):
                            bank, bkey = (psZa, "psZa") if b == 0 else (psZb, "psZb")
                            S.op("dve", "tensor_tensor", out=attm[:, b::2, 0:T], in0=bank[:, 0:N].rearrange("p (c t) -> p c t", c=4), in1=TRI8[:, 0:4, tq0:tq0 + T],
                                 op=ALU.mult, reads=[bkey, "TRI8"], writes=[f"attm{b}"])
                        S.op("pool", "tensor_tensor", out=Sblk[:], in0=Sst[:], in1=PMK[:], op=ALU.mult, reads=["Sst", "Sst_0", "Sst_1", "PMK"], writes=["Sblk"])
                        for h in range(8):
                            S.op("pe", "matmul", out=psX[0:T, h * 64:(h + 1) * 64], lhsT=attm[:, h, 0:T], rhs=vbk[:, h * 64:(h + 1) * 64], start=True, stop=False,
                                 reads=["attm0", "attm1", "vbk", "vbkr"], writes=["psX"])
                            S.op("pe", "matmul", out=psX[0:T, h * 64:(h + 1) * 64], lhsT=qT[:, h // 2, 0:T], rhs=Sblk[:, h * 64:(h + 1) * 64], start=False, stop=True,
                                 reads=["qT", "Sblk"], writes=["psX"])
                        S.op("dve", "tensor_reduce", out=gst[0:T, :, 0], in_=psX[0:T, :].rearrange("p (h e) -> p h e", h=8), axis=AX.X, op=ALU.add, reads=["psX"], writes=["gst0"])
                        S.op("act", "activation", out=sq_[0:T, :], in_=psX[0:T, :], func=AF.Square, reads=["psX"], writes=["sq_"])
                        S.op("dve", "tensor_reduce", out=gst[0:T, :, 1], in_=sq_[0:T, :].rearrange("p (h e) -> p h e", h=8), axis=AX.X, op=ALU.add, reads=["sq_"], writes=["gst1"])
                        S.op("dve", "tensor_scalar", out=gst[0:T, :, 2], in0=gst[0:T, :, 0], scalar1=1.0 / 64, scalar2=None, op0=ALU.mult, reads=["gst0"], writes=["gst2"])
                        S.op("dve", "tensor_tensor", out=gst[0:T, :, 3], in0=gst[0:T, :, 2], in1=gst[0:T, :, 2], op=ALU.mult, reads=["gst2"], writes=["gst3"])
                        S.op("dve", "scalar_tensor_tensor", out=gst[0:T, :, 4], in0=gst[0:T, :, 1], scalar=1.0 / 64, in1=gst[0:T, :, 3], op0=ALU.mult, op1=ALU.subtract,
                             reads=["gst1", "gst3"], writes=["gst4"])
                        S.op("dve", "tensor_scalar", out=gst[0:T, :, 4], in0=gst[0:T, :, 4], scalar1=GN_EPS, scalar2=None, op0=ALU.add, reads=["gst4"], writes=["gst4"])
                        S.op("act", "activation", out=gst[0:T, :, 5], in_=gst[0:T, :, 4], func=AF.Sqrt, reads=["gst4"], writes=["gst5"])
                        S.op("dve", "reciprocal", out=gst[0:T, :, 5], in_=gst[0:T, :, 5], reads=["gst5"], writes=["gst5"])
                        for h in range(8):
                            S.op("dve", "tensor_scalar", out=on_[0:T, h * 64:(h + 1) * 64], in0=psX[0:T, h * 64:(h + 1) * 64], scalar1=gst[0:T, h, 2:3], scalar2=gst[0:T, h, 5:6],
                                 op0=ALU.subtract, op1=ALU.mult, reads=["psX", "gst2", "gst5"], writes=["on_"])
                        S.op("pool", "tensor_tensor", out=on_[0:T, :], in0=on_[0:T, :], in1=GNG[0:T, :], op=ALU.mult, reads=["on_", "GNG"], writes=["on_"])
                        S.op("pool", "tensor_tensor", out=oret[0:T, :], in0=on_[0:T, :], in1=sg[0:T, :], op=ALU.mult, reads=["on_", "sg"], writes=["oret"])
                        if j < 4:
                            for h in range(8):
                                S.op("pe", "matmul", out=psQ[:, h * 64:(h + 1) * 64], lhsT=k2[:, h, :, :].rearrange("p a d -> p (a d)"), rhs=vbk[:, h * 64:(h + 1) * 64],
                                     start=True, stop=True, reads=["k2", "k2r", "vbk", "vbkr"], writes=["psQ"])
                            S.op("dve", "scalar_tensor_tensor", out=tmpS[:], in0=psQ[:, :], scalar=VALID[:, p:p + 1], in1=Sst[:], op0=ALU.mult, op1=ALU.add,
                                 reads=["psQ", "VALID", "Sst", "Sst_0", "Sst_1"], writes=["tmpS"])
                            S.op("pool", "tensor_tensor", out=Sst[:], in0=tmpS[:], in1=DC[:], op=ALU.mult, reads=["tmpS", "DC", "Sblk"], writes=["Sst"])
                            if smp is not None:
                                S.dma("sp", out=o_sret.ap()[smp], in_=Sst[0:64, :], reads=["Sst"])
                        nmc = ((8 * p + 7) + 127) // 128 if smp is None else nmc_s
                        for g in range(2):
                            def evac_O(bi_, gate_b):
                                S.op("act", "copy", out=OT[0:65, 0:N], in_=psO[0:65, 0:N], reads=["psO"], writes=["OT"])
                                for r in range(4):
                                    S.op("pe", "transpose", out=psX[0:T, r * 65:(r + 1) * 65], in_=OT[0:65, r * T:(r + 1) * T], identity=ident_f[0:65, 0:65],
                                         reads=["OT", "ident_f"], writes=["psX"])
                                S.op("dve", "tensor_copy", out=Ob[bi_][0:T].rearrange("p r d -> p (r d)"), in_=psX[0:T, 0:260], reads=["psX"], writes=[f"Ob{bi_}"])
                                S.op("dve", "tensor_scalar", out=rz[0:T, bi_, :], in0=Ob[bi_][0:T, :, 64], scalar1=1e-30, scalar2=None, op0=ALU.add, reads=[f"Ob{bi_}"], writes=[f"rz{bi_}"])
                                S.op("dve", "reciprocal", out=rz[0:T, bi_, :], in_=rz[0:T, bi_, :], reads=[f"rz{bi_}"], writes=[f"rz{bi_}"])
                                S.op("dve", "tensor_tensor", out=coef[0:T, bi_, :], in0=rz[0:T, bi_, :], in1=gates[0:T, gate_b * 8 + 4 * g:gate_b * 8 + 4 * g + 4], op=ALU.mult,
                                     reads=[f"rz{bi_}", "gates"], writes=[f"coef{bi_}"])
                            for m in range(nmc):
                                rr = p - 16 * m
                                pz, pk = psS[m % 2], f"psS{m % 2}"
                                S.op("pe", "matmul", out=pz[:, 0:N], lhsT=kcmpT[:, m * 128:(m + 1) * 128], rhs=QTz[g][:, :, 0:T], start=True, stop=False,
                                     reads=["kcmpT", f"QTz{g}"], writes=[pk])
                                brhs = GC[:, rc_idx[rr], 4 * g:4 * g + 4, tq0:tq0 + T] if rr < 23 else CHT[:, 4 * g:4 * g + 4, tq0:tq0 + T]
                                S.op("pe", "matmul", out=pz[:, 0:N], lhsT=anti_b[:], rhs=brhs, start=False, stop=True, reads=["GC", "CHT", "anti_b"], writes=[pk])
                                S.op("act", "activation", out=PTc[m][:, :, 0:T], in_=pz[:, 0:N].rearrange("p (c t) -> p c t", c=4), func=AF.Exp, bias=CMPB[:, m:m + 1], scale=1.0,
                                     reads=[pk, "CMPB"], writes=[f"PTc{m}"])
                                S.op("pe", "matmul", out=psO[0:65, 0:N], lhsT=vcmp[:, m, g, :], rhs=PTc[m][:, :, 0:T], start=(m == 0), stop=(m == nmc - 1),
                                     reads=["vcmp", f"PTc{m}"], writes=["psO"])
                            evac_O(0, 0)
                            for r in range(4):
                                for m in range(nmc):
                                    S.op("pe", "matmul", out=psQ[0:T, 0:NBv], lhsT=PTc[m][:, r, 0:T], rhs=AM[:, m, :], start=(m == 0), stop=(m == nmc - 1),
                                         reads=[f"PTc{m}", "AM"], writes=["psQ"])
                                if r == 0:
                                    S.op("dve", "scalar_tensor_tensor", out=impa[0:T, :], in0=psQ[0:T, 0:NBv], scalar=rz[0:T, 0, r:r + 1], in1=FS[0:T, :], op0=ALU.mult, op1=ALU.add,
                                         reads=["psQ", "rz0", "FS"], writes=["impa"])
                                else:
                                    S.op("dve", "scalar_tensor_tensor", out=impa[0:T, :], in0=psQ[0:T, 0:NBv], scalar=rz[0:T, 0, r:r + 1], in1=impa[0:T, :], op0=ALU.mult, op1=ALU.add,
                                         reads=["psQ", "rz0", "impa"], writes=["impa"])
                            S.op("dve", "max", out=m8[0:T, 0:8], in_=impa[0:T, :], reads=["impa"], writes=["m8"])
                            S.op("dve", "match_replace", out=wk[0:T, :], in_to_replace=m8[0:T, 0:8], in_values=impa[0:T, :], imm_value=-1e30, reads=["impa", "m8"], writes=["wk"])
                            S.op("dve", "max", out=m8[0:T, 8:16], in_=wk[0:T, :], reads=["wk"], writes=["m8"])
                            S.op("dve", "tensor_scalar", out=m8[0:T, 15:16], in0=m8[0:T, 15:16], scalar1=-1e29, scalar2=None, op0=ALU.max, reads=["m8"], writes=["m8"])
                            S.op("dve", "tensor_scalar", out=Mneg[0:T, :], in0=impa[0:T, :], scalar1=m8[0:T, 15:16], scalar2=-BIG, op0=ALU.is_lt, op1=ALU.mult,
                                 reads=["impa", "m8"], writes=["Mneg"])
                            for pc_ in range(NPv):
                                S.op("pe", "transpose", out=psT[:, pc_ * 128:pc_ * 128 + T], in_=Mneg[0:T, pc_ * 128:(pc_ + 1) * 128], identity=ident_b[0:T, 0:T],
                                     reads=["Mneg", "ident_b"], writes=["psT"])
                            for pc_ in range(NPv):
                                for r in range(4):
                                    S.op("act", "activation", out=MST[:, pc_, r, 0:T], in_=psT[:, pc_ * 128:pc_ * 128 + T], func=AF.Identity, bias=CH8[:, 4 * g + r:4 * g + r + 1], scale=1.0,
                                         reads=["psT", "CH8"], writes=["MST"])
                            for jj in range(p + 1):
                                rr = p - jj
                                pz, pk = psS[jj % 2], f"psS{jj % 2}"
                                S.op("pe", "matmul", out=pz[:, 0:N], lhsT=IND[:, 128 * (jj % 64):128 * (jj % 64) + 128], rhs=MST[:, jj // 64, :, 0:T], start=True, stop=False,
                                     reads=["IND", "MST"], writes=[pk])
                                S.op("pe", "matmul", out=pz[:, 0:N], lhsT=KsT[:, jj * 128:(jj + 1) * 128], rhs=QTz[g][:, :, 0:T], start=False, stop=(rr > 7),
                                     reads=["KsT", f"QTz{g}"], writes=[pk])
                                if rr <= 7:
                                    S.op("pe", "matmul", out=pz[:, 0:N], lhsT=anti_b[:], rhs=GS[:, rr, 4 * g:4 * g + 4, tq0:tq0 + T], start=False, stop=True,
                                         reads=["GS", "anti_b"], writes=[pk])
                                pt, ptk = PT[jj % 4], f"PT{jj % 4}"
                                S.op("act", "activation", out=pt[:, :, 0:T], in_=pz[:, 0:N].rearrange("p (c t) -> p c t", c=4), func=AF.Exp, reads=[pk], writes=[ptk])
                                S.op("pe", "matmul", out=psO[0:65, 0:N], lhsT=Vs[:, jj, g, :], rhs=pt[:, :, 0:T], start=(jj == 0), stop=(jj == p),
                                     reads=["Vs", ptk], writes=["psO"])
                            evac_O(1, 1)
                            for rr in range(5):
                                wslot = 4 - rr
                                pz, pk = psS[rr % 2], f"psS{rr % 2}"
                                S.op("pe", "matmul", out=pz[:, 0:N], lhsT=KW[:, wslot, :], rhs=QTz[g][:, :, 0:T], start=True, stop=False, reads=["KW", f"QTz{g}"], writes=[pk])
                                S.op("pe", "matmul", out=pz[:, 0:N], lhsT=anti_b[:], rhs=GW[:, rr, 4 * g:4 * g + 4, tq0:tq0 + T], start=False, stop=True,
                                     reads=["GW", "anti_b"], writes=[pk])
                                pt, ptk = PT[rr % 4], f"PT{rr % 4}"
                                S.op("act", "activation", out=pt[:, :, 0:T], in_=pz[:, 0:N].rearrange("p (c t) -> p c t", c=4), func=AF.Exp, bias=WINB[:, wi0 + wslot:wi0 + wslot + 1], scale=1.0,
                                     reads=[pk, "WINB"], writes=[ptk])
                                S.op("pe", "matmul", out=psO[0:65, 0:N], lhsT=VW[:, wslot, g, :], rhs=pt[:, :, 0:T], start=(rr == 0), stop=(rr == 4),
                                     reads=["VW", "VWd0", "VWd1", ptk], writes=["psO"])
                            evac_O(2, 2)
                            for r in range(4):
                                hh = 4 * g + r
                                S.op("dve", "tensor_scalar", out=onsa[0:T, hh, :], in0=Ob[0][0:T, r, 0:64], scalar1=coef[0:T, 0, r:r + 1], scalar2=None, op0=ALU.mult,
                                     reads=["Ob0", "coef0"], writes=[f"onsa{hh}"])
                                for bi_ in (1, 2):
                                    S.op("dve", "scalar_tensor_tensor", out=onsa[0:T, hh, :], in0=Ob[bi_][0:T, r, 0:64], scalar=coef[0:T, bi_, r:r + 1], in1=onsa[0:T, hh, :],
                                         op0=ALU.mult, op1=ALU.add, reads=[f"Ob{bi_}", f"coef{bi_}", f"onsa{hh}"], writes=[f"onsa{hh}"])
                        S.op("pool", "tensor_copy", out=onsb[0:T, :], in_=onsa[0:T].rearrange("p h d -> p (h d)"), reads=[f"onsa{h}" for h in range(8)], writes=["onsb"])
                        for c in range(4):
                            S.op("pe", "transpose", out=psT[:, c * 128:c * 128 + T], in_=onsb[0:T, c * 128:(c + 1) * 128], identity=ident_b[0:T, 0:T],
                                 reads=["onsb", "ident_b"], writes=["psT"])
                            S.op("pe", "transpose", out=psT[:, (4 + c) * 128:(4 + c) * 128 + T], in_=oret[0:T, c * 128:(c + 1) * 128], identity=ident_b[0:T, 0:T],
                                 reads=["oret"], writes=["psT"])
                        S.op("act", "copy", out=omT[:, :, 0:T], in_=psT[:, :].rearrange("p (k t) -> p k t", k=8)[:, :, 0:T], reads=["psT"], writes=["omT"])
                        S.dma("sp", out=om_d.ap()[ti].rearrange("p (k t) -> p k t", k=8)[:, :, (tsr if smp is not None else 0):(tsr if smp is not None else 0) + T], in_=omT[:, :, 0:T], reads=["omT"], writes=[f"om_d{ti}"], allow_slow_non_contiguous=True)
                        if debug and smp is None:
                            S.dma("sp", out=dbg["omix"].ap()[ti, 0:T, 0:512], in_=onsa[0:T].rearrange("p h d -> p (h d)"), reads=[f"onsa{h}" for h in range(8)])
                            S.dma("pool", out=dbg["omix"].ap()[ti, 0:T, 512:1024], in_=oret[0:T, :], reads=["oret"])
                S.flush()


        phase_1b(None)
        if "s" in phases:
            for smp_ in range(NSMP):
                phase_1a(smp_)
                phase_1b(smp_)
        if "2" not in phases:
            return nc
        p1.close()
        with ExitStack() as ph:
            idb2 = SB(ph, "idb2", [128, 128], BF16)
            Wo = SB(ph, "Wo", [128, KC, D], BF16)
            Wd = SB(ph, "Wd", [128, NFC, D], BF16)
            wu = [SB(ph, f"wu{i}", [128, KC, 2, 128], BF16) for i in range(2)]
            VEC = SB(ph, "VEC", [128, 5, D], F32)
            CW = SB(ph, "CW", [128, NFC, 4], F32)
            HVAL = SB(ph, "HVAL", [128, KSEG], F32)
            x1s = SB(ph, "x1s", [128, 5, D], F32)
            hn2T = SB(ph, "hn2T", [128, KC, 514], BF16)
            hT = SB(ph, "hT", [128, NFC, 512], BF16)
            omT = SB(ph, "omT2", [128, KC, 514], BF16)
            agx = SB(ph, "agx", [128, 514], F32)
            cv = SB(ph, "cv", [128, 512], F32)
            sl = SB(ph, "sl", [128, 512], F32)
            cvo = SB(ph, "cvo", [128, NFC, 2], F32)
            xt = [SB(ph, f"xt2_{i}", [128, D], F32) for i in range(2)]
            t1 = SB(ph, "t1c", [128, D], F32)
            t2 = SB(ph, "t2c", [128, D], F32)
            hn2 = SB(ph, "hn2", [128, D], BF16)
            st4 = SB(ph, "st4c", [128, 4], F32)
            yb = [SB(ph, f"yb{i}", [128, D], F32) for i in range(2)]
            MVs = SB(ph, "MVs", [NSMP, 4, D], F32)
            bld = SB(ph, "bld", [8, D_FF], F32)
            Bst = SB(ph, "Bst", [128, NFC, 8], F32)
            ags = SB(ph, "ags", [128, NFC, NSMP], F32)
            idf2 = SB(ph, "idf2", [128, 128], F32)
            psA = PS(ph, "psA2", [128, 512], F32)
            psB = PS(ph, "psB2", [128, 512], F32)
            psG = PS(ph, "psG2", [128, 512], F32)
            psV = PS(ph, "psV2", [128, 512], F32)
            psH = PS(ph, "psH2", [128, 512], F32)
            psT = PS(ph, "psT2c", [128, 1024], BF16)
            S.dma("pool", out=idb2[:], in_=cn["c_ident"].ap()[:, :], writes=["idb2"])
            for kc in range(KC):
                S.dma("pool", out=Wo[:, kc, :], in_=w_out.ap()[kc * 128:(kc + 1) * 128, :], writes=["Wo"])
            for f in range(NFC):
                S.dma("pool", out=Wd[:, f, :], in_=w_down.ap()[f * 128:(f + 1) * 128, :], writes=["Wd"])
            for i, col in enumerate((2, 3, 4, 5)):
                S.dma("sp", out=VEC[:, i, :], in_=modv_d.ap()[32:33, col * D:(col + 1) * D].partition_broadcast(128), writes=[f"VEC{i}"])
            S.dma("sp", out=VEC[:, 4, :], in_=gf.ap()[0:1, :].partition_broadcast(128), writes=["VEC4"])
            for jw in range(3):
                S.dma("sp", out=CW[:, :, jw], in_=conv_w.ap()[jw:jw + 1, :].rearrange("o (f p) -> p (o f)", p=128), writes=[f"CW{jw}"], allow_slow_non_contiguous=True)
            S.dma("sp", out=CW[:, :, 3], in_=conv_b.ap()[0:1, :].rearrange("o (f p) -> p (o f)", p=128), writes=["CW3"], allow_slow_non_contiguous=True)
            S.dma("sp", out=HVAL[:], in_=cn["c_hval"].ap()[:, :], writes=["HVAL"])
            cwk = ["CW0", "CW1", "CW2", "CW3"]
            GT1, A2, SH2, GT2, GFb = [VEC[:, i, :] for i in range(5)]

            def rms(src, T, key):
                S.op("act", "activation", out=t2[0:T, :], in_=src, func=AF.Square, accum_out=st4[0:T, 0:1], reads=[key], writes=["t2", "st4"])
                S.op("dve", "tensor_scalar", out=st4[0:T, 1:2], in0=st4[0:T, 0:1], scalar1=1.0 / D, scalar2=RMS_EPS, op0=ALU.mult, op1=ALU.add, reads=["st4"], writes=["st4"])
                S.op("act", "activation", out=st4[0:T, 2:3], in_=st4[0:T, 1:2], func=AF.Sqrt, reads=["st4"], writes=["st4"])
                S.op("dve", "reciprocal", out=st4[0:T, 3:4], in_=st4[0:T, 2:3], reads=["st4"], writes=["st4"])

            for k in range(KSEG):
                S.dma("sp", out=omT[:, :, 0:2], in_=om_d.ap()[5 * k].rearrange("p (k t) -> p k t", k=KC)[:, :, 0:2], writes=["omT"])
                for u in range(4):
                    S.dma("sp", out=omT[:, :, 2 + 128 * u:2 + 128 * (u + 1)], in_=om_d.ap()[5 * k + 1 + u].rearrange("p (k t) -> p k t", k=KC), writes=[f"omT{u}"])
                for j in range(5):
                    p = 32 * k + 27 + j
                    T, tq0 = (2, 126) if j == 0 else (128, 0)
                    c0 = 0 if j == 0 else 2 + 128 * (j - 1)
                    xx = xt[j % 2]
                    S.dma("sp", out=xx[0:T, :], in_=xp.ap()[p * 128 + tq0:p * 128 + tq0 + T, :], writes=[f"xt{j % 2}"])
                    for half, bank, bkey in ((0, psA, "psA"), (1, psB, "psB")):
                        for kc in range(KC):
                            S.op("pe", "matmul", out=bank[0:T, :], lhsT=omT[:, kc, c0:c0 + T], rhs=Wo[:, kc, half * 512:(half + 1) * 512], start=(kc == 0), stop=(kc == KC - 1),
                                 reads=["omT", f"omT{max(j - 1, 0)}", "Wo"], writes=[bkey])
                        S.op("dve", "tensor_tensor", out=t1[0:T, half * 512:(half + 1) * 512], in0=bank[0:T, :], in1=GT1[0:T, half * 512:(half + 1) * 512], op=ALU.mult,
                             reads=[bkey, "VEC0"], writes=[f"t1h{half}"])
                    S.op("pool", "tensor_tensor", out=x1s[0:T, j, :], in0=t1[0:T, :], in1=xx[0:T, :], op=ALU.add, reads=["t1h0", "t1h1", f"xt{j % 2}"], writes=[f"x1s{j}"])
                    rms(x1s[0:T, j, :], T, f"x1s{j}")
                    S.op("dve", "scalar_tensor_tensor", out=t1[0:T, :], in0=x1s[0:T, j, :], scalar=st4[0:T, 3:4], in1=A2[0:T, :], op0=ALU.mult, op1=ALU.mult,
                         reads=[f"x1s{j}", "st4", "VEC1", "t1h0", "t1h1"], writes=["t1h0", "t1h1"])
                    S.op("pool", "tensor_tensor", out=hn2[0:T, :], in0=t1[0:T, :], in1=SH2[0:T, :], op=ALU.add, reads=["t1h0", "t1h1", "VEC2"], writes=["hn2"])
                    for kc in range(KC):
                        S.op("pe", "transpose", out=psT[:, kc * 128:kc * 128 + T], in_=hn2[0:T, kc * 128:(kc + 1) * 128], identity=idb2[0:T, 0:T],
                             reads=["hn2", "idb2"], writes=["psT"])
                    S.op("act", "copy", out=hn2T[:, :, c0:c0 + T], in_=psT[:, :].rearrange("p (k t) -> p k t", k=KC)[:, :, 0:T], reads=["psT"], writes=[f"hn2T{j}"])
                hk = [f"hn2T{j}" for j in range(5)]
                for f in range(NFC):
                    w = wu[f % 2]
                    wk_ = f"wu{f % 2}"
                    S.dma("pool", out=w[:, :, 0, :], in_=w_up.ap()[:, f * 128:(f + 1) * 128].rearrange("(k p) c -> p k c", p=128), writes=[wk_ + "g"])
                    S.dma("pool", out=w[:, :, 1, :], in_=w_up.ap()[:, D_FF + f * 128:D_FF + (f + 1) * 128].rearrange("(k p) c -> p k c", p=128), writes=[wk_ + "v"])
                    for kc in range(KC):
                        S.op("pe", "matmul", out=psG[:, :], lhsT=w[:, kc, 0, :], rhs=hn2T[:, kc, 2:514], start=(kc == 0), stop=(kc == KC - 1), reads=hk + [wk_ + "g"], writes=["psG"])
                    for kc in range(KC):
                        S.op("pe", "matmul", out=psH[:, 0:2], lhsT=w[:, kc, 0, :], rhs=hn2T[:, kc, 0:2], start=(kc == 0), stop=(kc == KC - 1), reads=hk + [wk_ + "g"], writes=["psH"])
                    for kc in range(KC):
                        S.op("pe", "matmul", out=psV[:, :], lhsT=w[:, kc, 1, :], rhs=hn2T[:, kc, 2:514], start=(kc == 0), stop=(kc == KC - 1), reads=hk + [wk_ + "v"], writes=["psV"])
                    S.op("dve", "tensor_scalar", out=agx[:, 0:2], in0=psH[:, 0:2], scalar1=HVAL[:, k:k + 1], scalar2=None, op0=ALU.mult, reads=["psH", "HVAL"], writes=["agxh"])
                    S.op("act", "copy", out=agx[:, 2:514], in_=psG[:, :], reads=["psG"], writes=["agx"])
                    S.op("dve", "tensor_scalar", out=cv[:], in0=agx[:, 0:512], scalar1=CW[:, f, 0:1], scalar2=CW[:, f, 3:4], op0=ALU.mult, op1=ALU.add,
                         reads=["agx", "agxh"] + cwk, writes=["cv"])
                    S.op("dve", "scalar_tensor_tensor", out=cv[:], in0=agx[:, 1:513], scalar=CW[:, f, 1:2], in1=cv[:], op0=ALU.mult, op1=ALU.add, reads=["agx", "agxh", "cv"] + cwk, writes=["cv"])
                    S.op("dve", "scalar_tensor_tensor", out=cv[:], in0=agx[:, 2:514], scalar=CW[:, f, 2:3], in1=cv[:], op0=ALU.mult, op1=ALU.add, reads=["agx", "cv"] + cwk, writes=["cv"])
                    S.op("act", "activation", out=sl[:], in_=cv[:], func=AF.Silu, reads=["cv"], writes=["sl"])
                    S.op("dve", "tensor_tensor", out=hT[:, f, :], in0=sl[:], in1=psV[:, :], op=ALU.mult, reads=["sl", "psV"], writes=[f"hT{f}"])
                    if k == KSEG - 1:
                        S.op("pool", "tensor_copy", out=cvo[:, f, :], in_=agx[:, 512:514], reads=["agx"], writes=["cvo"])
                htk = [f"hT{f}" for f in range(NFC)]
                for u in range(4):
                    j = u + 1
                    for half, bank, bkey in ((0, psA, "psA"), (1, psB, "psB")):
                        for f in range(NFC):
                            S.op("pe", "matmul", out=bank[:, :], lhsT=hT[:, f, u * 128:(u + 1) * 128], rhs=Wd[:, f, half * 512:(half + 1) * 512], start=(f == 0), stop=(f == NFC - 1),
                                 reads=htk + ["Wd"], writes=[bkey])
                        S.op("dve", "tensor_tensor", out=t1[:, half * 512:(half + 1) * 512], in0=bank[:, :], in1=GT2[:, half * 512:(half + 1) * 512], op=ALU.mult,
                             reads=[bkey, "VEC3"], writes=[f"t1h{half}"])
                    S.op("pool", "tensor_tensor", out=t1[:], in0=t1[:], in1=x1s[:, j, :], op=ALU.add, reads=["t1h0", "t1h1", f"x1s{j}"], writes=["t1h0", "t1h1"])
                    rms(t1[:], 128, "t1h0")
                    y_ = yb[u % 2]
                    S.op("dve", "scalar_tensor_tensor", out=y_[:], in0=t1[:], scalar=st4[:, 3:4], in1=GFb, op0=ALU.mult, op1=ALU.mult,
                         reads=["t1h0", "t1h1", "st4", "VEC4"], writes=[f"yb{u % 2}"])
                    S.dma("sp", out=o_y.ap()[4 * k + u], in_=y_[:], reads=[f"yb{u % 2}"])

            if "s" in phases:
                T = NSMP
                S.dma("sp", out=idf2[:], in_=cn["c_ident"].ap()[:, :], writes=["idf2"])
                S.dma("sp", out=omT[:, :, 0:T], in_=om_d.ap()[TS].rearrange("p (k t) -> p k t", k=KC)[:, :, 0:T], writes=["omT"])
                for i, col in enumerate((2, 3, 4, 5)):
                    S.dma("sp", out=MVs[0:T, i, :], in_=modv_d.ap()[0:T, col * D:(col + 1) * D], writes=[f"MVs{i}"])
                xx = xt[0]
                S.dma("sp", out=xx[0:T, :], in_=xs.ap()[:, :], writes=["xt0"])
                S.dma("sp", out=bld[:], in_=sconv.ap()[:, :], writes=["bld"])
                for f in range(NFC):
                    S.op("pe", "transpose", out=psG[:, f * 8:(f + 1) * 8], in_=bld[0:8, f * 128:(f + 1) * 128], identity=idf2[0:8, 0:8],
                         reads=["bld", "idf2"], writes=["psG"])
                S.op("dve", "tensor_copy", out=Bst[:].rearrange("p f e -> p (f e)"), in_=psG[:, 0:NFC * 8], reads=["psG"], writes=["Bst"])
                for half, bank, bkey in ((0, psA, "psA"), (1, psB, "psB")):
                    for kc in range(KC):
                        S.op("pe", "matmul", out=bank[0:T, :], lhsT=omT[:, kc, 0:T], rhs=Wo[:, kc, half * 512:(half + 1) * 512], start=(kc == 0), stop=(kc == KC - 1),
                             reads=["omT", "Wo"], writes=[bkey])
                    S.op("dve", "tensor_tensor", out=t1[0:T, half * 512:(half + 1) * 512], in0=bank[0:T, :], in1=MVs[0:T, 0, half * 512:(half + 1) * 512], op=ALU.mult,
                         reads=[bkey, "MVs0"], writes=[f"t1h{half}"])
                S.op("pool", "tensor_tensor", out=x1s[0:T, 0, :], in0=t1[0:T, :], in1=xx[0:T, :], op=ALU.add, reads=["t1h0", "t1h1", "xt0"], writes=["x1s0"])
                rms(x1s[0:T, 0, :], T, "x1s0")
                S.op("dve", "scalar_tensor_tensor", out=t1[0:T, :], in0=x1s[0:T, 0, :], scalar=st4[0:T, 3:4], in1=MVs[0:T, 1, :], op0=ALU.mult, op1=ALU.mult,
                     reads=["x1s0", "st4", "MVs1", "t1h0", "t1h1"], writes=["t1h0", "t1h1"])
                S.op("pool", "tensor_tensor", out=hn2[0:T, :], in0=t1[0:T, :], in1=MVs[0:T, 2, :], op=ALU.add, reads=["t1h0", "t1h1", "MVs2"], writes=["hn2"])
                for kc in range(KC):
                    S.op("pe", "transpose", out=psT[:, kc * 128:kc * 128 + T], in_=hn2[0:T, kc * 128:(kc + 1) * 128], identity=idb2[0:T, 0:T],
                         reads=["hn2", "idb2"], writes=["psT"])
                S.op("act", "copy", out=hn2T[:, :, 0:T], in_=psT[:, :].rearrange("p (k t) -> p k t", k=KC)[:, :, 0:T], reads=["psT"], writes=["hn2Ts"])
                for f in range(NFC):
                    w = wu[f % 2]
                    wk_ = f"wu{f % 2}"
                    S.dma("pool", out=w[:, :, 0, :], in_=w_up.ap()[:, f * 128:(f + 1) * 128].rearrange("(k p) c -> p k c", p=128), writes=[wk_ + "g"])
                    S.dma("pool", out=w[:, :, 1, :], in_=w_up.ap()[:, D_FF + f * 128:D_FF + (f + 1) * 128].rearrange("(k p) c -> p k c", p=128), writes=[wk_ + "v"])
                    for kc in range(KC):
                        S.op("pe", "matmul", out=psG[:, 0:T], lhsT=w[:, kc, 0, :], rhs=hn2T[:, kc, 0:T], start=(kc == 0), stop=(kc == KC - 1), reads=["hn2Ts", wk_ + "g"], writes=["psG"])
                    for kc in range(KC):
                        S.op("pe", "matmul", out=psV[:, 0:T], lhsT=w[:, kc, 1, :], rhs=hn2T[:, kc, 0:T], start=(kc == 0), stop=(kc == KC - 1), reads=["hn2Ts", wk_ + "v"], writes=["psV"])
                    S.op("act", "copy", out=ags[:, f, :], in_=psG[:, 0:T], reads=["psG"], writes=[f"ags{f}"])
                    S.op("dve", "tensor_scalar", out=cv[:, 0:T], in0=Bst[:, f, 0::2], scalar1=CW[:, f, 0:1], scalar2=CW[:, f, 3:4], op0=ALU.mult, op1=ALU.add,
                         reads=["Bst"] + cwk, writes=["cv"])
                    S.op("dve", "scalar_tensor_tensor", out=cv[:, 0:T], in0=Bst[:, f, 1::2], scalar=CW[:, f, 1:2], in1=cv[:, 0:T], op0=ALU.mult, op1=ALU.add, reads=["Bst", "cv"] + cwk, writes=["cv"])
                    S.op("dve", "scalar_tensor_tensor", out=cv[:, 0:T], in0=ags[:, f, :], scalar=CW[:, f, 2:3], in1=cv[:, 0:T], op0=ALU.mult, op1=ALU.add, reads=[f"ags{f}", "cv"] + cwk, writes=["cv"])
                    S.op("act", "activation", out=sl[:, 0:T], in_=cv[:, 0:T], func=AF.Silu, reads=["cv"], writes=["sl"])
                    S.op("dve", "tensor_tensor", out=hT[:, f, 0:T], in0=sl[:, 0:T], in1=psV[:, 0:T], op=ALU.mult, reads=["sl", "psV"], writes=[f"hTs{f}"])
                htk = [f"hTs{f}" for f in range(NFC)]
                for half, bank, bkey in ((0, psA, "psA"), (1, psB, "psB")):
                    for f in range(NFC):
                        S.op("pe", "matmul", out=bank[0:T, :], lhsT=hT[:, f, 0:T], rhs=Wd[:, f, half * 512:(half + 1) * 512], start=(f == 0), stop=(f == NFC - 1),
                             reads=htk + ["Wd"], writes=[bkey])
                    S.op("dve", "tensor_tensor", out=t1[0:T, half * 512:(half + 1) * 512], in0=bank[0:T, :], in1=MVs[0:T, 3, half * 512:(half + 1) * 512], op=ALU.mult,
                         reads=[bkey, "MVs3"], writes=[f"t1h{half}"])
                S.op("pool", "tensor_tensor", out=t1[0:T, :], in0=t1[0:T, :], in1=x1s[0:T, 0, :], op=ALU.add, reads=["t1h0", "t1h1", "x1s0"], writes=["t1h0", "t1h1"])
                rms(t1[0:T, :], T, "t1h0")
                S.op("dve", "scalar_tensor_tensor", out=yb[0][0:T, :], in0=t1[0:T, :], scalar=st4[0:T, 3:4], in1=GFb[0:T, :], op0=ALU.mult, op1=ALU.mult,
                     reads=["t1h0", "t1h1", "st4", "VEC4"], writes=["yb0"])
                S.dma("sp", out=o_ys.ap()[:, :], in_=yb[0][0:T, :], reads=["yb0"])
                for sq in range(NSMP):
                    S.dma("sp", out=o_sconv.ap()[sq, 0:1, :], in_=bld[2 * sq + 1:2 * sq + 2, :], reads=["bld"])
                    S.dma("sp", out=o_sconv.ap()[sq, 1:2, :].rearrange("o (f p) -> p (o f)", p=128), in_=ags[:, :, sq], reads=[f"ags{f}" for f in range(NFC)],
                          allow_slow_non_contiguous=True)
            for jj in range(2):
                S.dma("sp", out=o_conv.ap()[jj:jj + 1, :].rearrange("o (f p) -> p (o f)", p=128), in_=cvo[:, :, jj], reads=["cvo"], allow_slow_non_contiguous=True)
            S.flush()
    return nc


def prep_shared(inputs):
    w_in = np.asarray(inputs["w_in"][0], np.float32)
    sh = {}
    sh["w_kv"] = np.ascontiguousarray(np.concatenate([w_in[:, 512:1280], w_in[:, 1816:2840]], axis=1))
    qperm = []
    for c in range(4):
        qperm += list(range(64 * c, 64 * c + 64)) + list(range(64 * (4 + c), 64 * (4 + c) + 64))
    sh["w_q"] = np.ascontiguousarray(np.concatenate([w_in[:, qperm], w_in[:, 1280:1304], w_in[:, 1304:1816], w_in[:, 2840:3352]], axis=1))
    sh["w_ada"] = np.ascontiguousarray(inputs["w_ada"][0], np.float32)
    sh["b_ada"] = np.ascontiguousarray(inputs["b_ada"][0:1], np.float32)
    sh["g1"] = np.ascontiguousarray(inputs["norm_mix_g"][0:1], np.float32)
    sh["g2"] = np.ascontiguousarray(inputs["norm_ffn_g"][0:1], np.float32)
    sh["gf"] = np.ascontiguousarray(np.asarray(inputs["final_g"], np.float32)[None, :])
    sh["table"] = np.ascontiguousarray(inputs["rel_bias_table"], np.float32)
    for nm, key in (("w1k", "w_cmp1_k"), ("w1v", "w_cmp1_v")):
        w = np.asarray(inputs[key][0], np.float32).transpose(1, 0, 2).reshape(64, 32 * 128)
        sh[nm] = np.ascontiguousarray(np.concatenate([w, w], axis=0))
    sh["w2k"] = np.ascontiguousarray(inputs["w_cmp2_k"][0], np.float32)
    sh["w2v"] = np.ascontiguousarray(inputs["w_cmp2_v"][0], np.float32)
    for nm, key in (("pek", "cmp_pe_k"), ("pev", "cmp_pe_v")):
        pe = np.asarray(inputs[key][0], np.float32).T
        sh[nm] = np.ascontiguousarray(np.concatenate([pe, pe], axis=0))
    sh["gng"] = np.ascontiguousarray(inputs["ret_gn_g"][0:1], np.float32)
    sh["w_out"] = np.ascontiguousarray(inputs["w_out"][0], np.float32)
    sh["w_up"] = np.ascontiguousarray(inputs["w_up"][0], np.float32)
    sh["w_down"] = np.ascontiguousarray(inputs["w_down"][0], np.float32)
    sh["conv_w"] = np.ascontiguousarray(inputs["conv_w"][0], np.float32)
    sh["conv_b"] = np.ascontiguousarray(inputs["conv_b"][0:1], np.float32)
    cc = np.asarray(inputs["cache_cmp_kv"], np.float32)
    sh["_cache_c"] = np.ascontiguousarray(cc[0]).reshape(cc.shape[1] * 128, 256)
    cs_ = np.asarray(inputs["cache_slc_kv"], np.float32)
    sh["_cache_s"] = np.ascontiguousarray(cs_[0]).reshape(cs_.shape[1] * 128, 256)
    return sh


def prep_core(cfg, inputs, shared, c):
    m = dict(shared)
    x = np.asarray(inputs["x_prompt"][0], np.float32)
    shf = cfg.shift(c) * 128
    xs = np.zeros((cfg.NSLOT * 128, D), np.float32)
    n = cfg.NSLOT * 128 - shf
    xs[shf:] = x[:n]
    m["xp"] = xs
    cin = np.zeros((33, D), np.float32)
    cs = np.asarray(inputs["c_sample"], np.float32)
    nb = cs.shape[0] // NCORE
    cin[0:nb] = cs[c * nb:(c + 1) * nb]
    cin[32] = np.asarray(inputs["c_prompt"], np.float32)[0]
    m["cin"] = cin
    m.update(make_consts(cfg, c))
    sl_ = slice(c * nb, (c + 1) * nb)
    m["xs"] = np.ascontiguousarray(np.asarray(inputs["x_sample"], np.float32)[sl_, 0, :])
    m["pt"] = np.ascontiguousarray(np.asarray(inputs["page_table"], np.int32)[sl_])
    m["cache_c"] = shared["_cache_c"]
    m["cache_s"] = shared["_cache_s"]
    m["swin"] = np.ascontiguousarray(np.asarray(inputs["state_win_kv"], np.float32)[0, sl_].reshape(nb, 512, 256))
    m["sret"] = np.ascontiguousarray(np.asarray(inputs["state_ret"], np.float32)[0, sl_])
    m["sconv"] = np.ascontiguousarray(np.asarray(inputs["state_conv"], np.float32)[0, sl_].reshape(nb * 2, D_FF))
    m.pop("_cache_c"); m.pop("_cache_s")
    return m


_PROG_CACHE = {}


def kernel(**inputs):
    x_prompt = np.asarray(inputs["x_prompt"])
    S_len = x_prompt.shape[1]
    NT = S_len // 128
    NPG = np.asarray(inputs["page_table"]).shape[1]
    NPOOL = np.asarray(inputs["cache_cmp_kv"]).shape[1]
    cfg = Cfg(NT, NPG=NPG, NPOOL=NPOOL)
    key = (NT, NPG, NPOOL)
    if key not in _PROG_CACHE:
        _PROG_CACHE[key] = build_program(cfg, phases=("0", "1a", "1b", "2", "s"))
    nc = _PROG_CACHE[key]
    shared = prep_shared(inputs)
    maps = [prep_core(cfg, inputs, shared, c) for c in range(NCORE)]
    res = run_bass_kernel_spmd(nc, maps, core_ids=list(range(NCORE)))
    rs = res.results

    def assemble(name, width):
        out = np.zeros((NT * 128, width), np.float32)
        for c in range(NCORE):
            for k in range(cfg.KSEG):
                for u in range(4):
                    t = 32 * k + 4 * c + u
                    out[t * 128:(t + 1) * 128] = rs[c][name][4 * k + u]
        return out

    y_prompt = assemble("o_y", D).reshape(1, S_len, D)
    kvc = assemble("o_kvc", 256).reshape(1, 1, S_len, 2, 2, 64)
    kvs = assemble("o_kvs", 256).reshape(1, 1, S_len, 2, 2, 64)
    kvw = assemble("o_kvw", 256)[-512:].reshape(1, 1, 512, 2, 2, 64)
    ret_p = rs[7]["o_ret"][:64].reshape(64, 8, 64).transpose(1, 0, 2).reshape(1, 1, 8, 64, 64).copy()
    conv_p = rs[7]["o_conv"].reshape(1, 1, 2, D_FF).copy()
    NBS = np.asarray(inputs["x_sample"]).shape[0]
    y_sample = np.concatenate([rs[c]["o_ys"] for c in range(NCORE)], axis=0).reshape(NBS, 1, D)
    skv = np.concatenate([rs[c]["o_skv"] for c in range(NCORE)], axis=0)
    cmp_s = skv[:, 0:256].reshape(1, NBS, 1, 2, 2, 64).copy()
    slc_s = skv[:, 256:512].reshape(1, NBS, 1, 2, 2, 64).copy()
    win_s = np.concatenate([rs[c]["o_swin"] for c in range(NCORE)], axis=0).reshape(1, NBS, 512, 2, 2, 64)
    ret_s = np.concatenate([rs[c]["o_sret"] for c in range(NCORE)], axis=0).reshape(NBS, 64, 8, 64).transpose(0, 2, 1, 3).reshape(1, NBS, 8, 64, 64).copy()
    conv_s = np.concatenate([rs[c]["o_sconv"] for c in range(NCORE)], axis=0).reshape(1, NBS, 2, D_FF)
    return (y_prompt, y_sample, kvc, cmp_s, kvs, slc_s, kvw, win_s, ret_p, ret_s, conv_p, conv_s)
```
